# Optimizing a Trainium2 kernel written in Bass

```python
import math
import jax, jax.numpy as jnp
from jax import lax
import numpy as np

D_MODEL = 2048
BATCH = 4
SEQ = 4096
DEPTH = 2

GRID_W = 64
CTX_LEN = 256
HEAD_DIM = 128
MIX_WIDTH = D_MODEL
N_HEADS_TOTAL = MIX_WIDTH // HEAD_DIM
SWA_HEADS = N_HEADS_TOTAL // 4
SWA_KV_HEADS = SWA_HEADS // 2
SWA_WINDOW = 128
SWA_BLOCK = 128
DN_HEADS = N_HEADS_TOTAL // 2
DN_CONV = 5
DN_CHUNK = 64
NA_HEADS = N_HEADS_TOTAL - SWA_HEADS - DN_HEADS
NA_KH = 8
NA_KW = 16
FFN_HIDDEN = ((8 * D_MODEL // 3 + 255) // 256) * 256
ROPE_THETA = 10000.0
EPS = 1e-6

SWA_Q = SWA_HEADS * HEAD_DIM
SWA_KV = SWA_KV_HEADS * HEAD_DIM
DN_DIM = DN_HEADS * HEAD_DIM
NA_DIM = NA_HEADS * HEAD_DIM
IN_SPLITS = (SWA_Q, SWA_KV, SWA_KV, 3 * DN_DIM, DN_DIM, 4 * DN_HEADS, NA_DIM, NA_DIM, NA_DIM)
N_IN = sum(IN_SPLITS)

kernel_name = 'hybrid_swa_gdn_natten_prefix_dit'


def rms_norm(x, g):
    xf = x.astype(jnp.float32)
    y = xf * lax.rsqrt(jnp.mean(xf * xf, axis=-1, keepdims=True) + EPS)
    return (y * g.astype(jnp.float32)).astype(x.dtype)


def l2_normalize(t):
    return t * lax.rsqrt(jnp.sum(t * t, axis=-1, keepdims=True) + EPS)


def modulate(h, shift, scale):
    return h * (1 + scale) + shift


def heads(t, n):
    return t.reshape(t.shape[:-1] + (n, HEAD_DIM))


def split_cols(p):
    outs, start = [], 0
    for w in IN_SPLITS:
        outs.append(p[..., start:start + w])
        start += w
    return outs


def swiglu(h, w_gate, w_up, w_down):
    return (jax.nn.silu(h @ w_gate) * (h @ w_up)) @ w_down


def axial_rope_tables(L, dtype):
    t = jnp.arange(L, dtype=jnp.int32)
    row = (t // GRID_W).astype(jnp.float32)
    col = (t % GRID_W).astype(jnp.float32)
    n_freq = HEAD_DIM // 4
    inv = ROPE_THETA ** (-jnp.arange(n_freq, dtype=jnp.float32) / n_freq)
    ang_r = row[:, None] * inv[None, :]
    ang_c = col[:, None] * inv[None, :]
    ang = jnp.concatenate([ang_r, ang_r, ang_c, ang_c], axis=-1)
    return jnp.cos(ang).astype(dtype), jnp.sin(ang).astype(dtype)


def apply_axial_rope(x, cos, sin):
    xa = x.reshape(x.shape[:-1] + (2, 2, HEAD_DIM // 4))
    rot = jnp.stack([-xa[..., 1, :], xa[..., 0, :]], axis=-2).reshape(x.shape)
    return x * cos[None, :, None, :] + rot * sin[None, :, None, :]


def context_attention(qc, kc, vc, sink):
    Bn, Lc, G, R, hd = qc.shape
    s = jnp.einsum('bqgrd,bkgd->bgrqk', qc, kc).astype(jnp.float32) * hd ** -0.5
    if sink is not None:
        s_sink = jnp.broadcast_to(sink.astype(jnp.float32)[None, :, :, None, None], (Bn, G, R, Lc, 1))
        s = jnp.concatenate([s, s_sink], axis=-1)
    p = jax.nn.softmax(s, axis=-1)[..., :Lc].astype(qc.dtype)
    o = jnp.einsum('bgrqk,bkgd->bqgrd', p, vc)
    return o.reshape(Bn, Lc, G * R * hd)


def swa_attention(q, k, v, kc, vc, sink):
    Bn, L, G, R, hd = q.shape
    T = SWA_BLOCK
    nb = L // T
    Lc = kc.shape[1]
    qb = q.reshape(Bn, nb, T, G, R, hd)

    def band(t):
        tp = jnp.pad(t, ((0, 0), (T, T), (0, 0), (0, 0))).reshape(Bn, nb + 2, T, G, hd)
        return jnp.concatenate([tp[:, :-2], tp[:, 1:-1], tp[:, 2:]], axis=2)

    kb, vb = band(k), band(v)
    qpos = jnp.arange(nb)[:, None] * T + jnp.arange(T)[None, :]
    kpos = jnp.arange(nb)[:, None] * T - T + jnp.arange(3 * T)[None, :]
    rel = kpos[:, None, :] - qpos[:, :, None]
    valid = (jnp.abs(rel) <= SWA_WINDOW) & (kpos[:, None, :] >= 0) & (kpos[:, None, :] < L)
    scale = hd ** -0.5
    s_loc = jnp.einsum('bnqgrd,bnkgd->bgrnqk', qb, kb).astype(jnp.float32) * scale
    s_loc = jnp.where(valid[None, None, None], s_loc, -jnp.inf)
    s_ctx = jnp.einsum('bnqgrd,bcgd->bgrnqc', qb, kc).astype(jnp.float32) * scale
    s_sink = jnp.broadcast_to(sink.astype(jnp.float32)[None, :, :, None, None, None], (Bn, G, R, nb, T, 1))
    p = jax.nn.softmax(jnp.concatenate([s_loc, s_ctx, s_sink], axis=-1), axis=-1).astype(q.dtype)
    o = (jnp.einsum('bgrnqk,bnkgd->bnqgrd', p[..., :3 * T], vb)
         + jnp.einsum('bgrnqc,bcgd->bnqgrd', p[..., 3 * T:3 * T + Lc], vc))
    return o.reshape(Bn, L, G * R * hd)


def neighborhood_attention(q, k, v, kc, vc, rpb):
    Bn, L, H, hd = q.shape
    W = GRID_W
    R = L // W
    KH = min(NA_KH, R)
    KW = NA_KW
    scale = hd ** -0.5
    qg = q.reshape(Bn, R, W, H, hd)
    kg = k.reshape(Bn, R, W, H, hd)
    vg = v.reshape(Bn, R, W, H, hd)
    rows = jnp.arange(R)
    row_start = jnp.clip(rows - KH // 2, 0, R - KH)
    row_idx = row_start[:, None] + jnp.arange(KH)[None, :]
    k_rows = kg[:, row_idx]
    v_rows = vg[:, row_idx]
    cols = jnp.arange(W)
    col_start = jnp.clip(cols - KW // 2, 0, W - KW)
    col_in = (cols[None, :] >= col_start[:, None]) & (cols[None, :] < col_start[:, None] + KW)
    dr = row_idx - rows[:, None] + (NA_KH - 1)
    dc = jnp.clip(cols[None, :] - cols[:, None], -(KW - 1), KW - 1) + (KW - 1)
    bias = rpb[:, dr[:, None, :, None], dc[None, :, None, :]].astype(jnp.float32)
    s_nb = jnp.einsum('brqhd,brjkhd->bhrqjk', qg, k_rows).astype(jnp.float32) * scale + bias[None]
    s_nb = jnp.where(col_in[:, None, :], s_nb, -jnp.inf)
    s_ctx = jnp.einsum('brqhd,bchd->bhrqc', qg, kc).astype(jnp.float32) * scale
    n_nb = KH * W
    s = jnp.concatenate([s_nb.reshape(Bn, H, R, W, n_nb), s_ctx], axis=-1)
    p = jax.nn.softmax(s, axis=-1).astype(q.dtype)
    p_nb = p[..., :n_nb].reshape(Bn, H, R, W, KH, W)
    o = (jnp.einsum('bhrqjk,brjkhd->brqhd', p_nb, v_rows)
         + jnp.einsum('bhrqc,bchd->brqhd', p[..., n_nb:], vc))
    return o.reshape(Bn, L, H * hd)


def short_conv_silu(x, w):
    C = x.shape[-1]
    pad = (DN_CONV - 1) // 2
    y = lax.conv_general_dilated(x, w[:, None, :].astype(x.dtype), window_strides=(1,),
                                 padding=[(pad, pad)], dimension_numbers=('NWC', 'WIO', 'NWC'),
                                 feature_group_count=C)
    return jax.nn.silu(y)


def gdn_inputs(qkv, ab, conv_w, A_log, dt_bias):
    y = short_conv_silu(qkv, conv_w).astype(jnp.float32)
    q, k, v = (heads(t, DN_HEADS) for t in jnp.split(y, 3, axis=-1))
    q = l2_normalize(q) * HEAD_DIM ** -0.5
    k = l2_normalize(k)
    ab = ab.astype(jnp.float32).reshape(ab.shape[:2] + (2, 2, DN_HEADS))
    log_a = -jnp.exp(A_log.astype(jnp.float32)) * jax.nn.softplus(ab[:, :, 0] + dt_bias.astype(jnp.float32))
    beta = jax.nn.sigmoid(ab[:, :, 1])
    return q, k, v, log_a, beta


def gdn_chunked(q, k, v, log_a, beta, state0):
    Bn, L, H, dk = q.shape
    dv = v.shape[-1]
    C = DN_CHUNK
    n = L // C

    def chunks(t):
        t = t.reshape((Bn, n, C, H) + t.shape[3:])
        return jnp.moveaxis(t, 3, 1)

    q, k, v, beta = chunks(q), chunks(k), chunks(v), chunks(beta)
    g = jnp.cumsum(chunks(log_a), axis=-1)
    idx = jnp.arange(C)
    incl = idx[:, None] >= idx[None, :]
    strict = idx[:, None] > idx[None, :]
    decay = jnp.exp(jnp.where(incl, g[..., :, None] - g[..., None, :], -jnp.inf))
    kb = k * beta[..., None]
    lower = jnp.where(strict, jnp.einsum('bhnid,bhnjd->bhnij', kb, k) * decay, 0.0)
    eye = jnp.eye(C, dtype=q.dtype)
    rhs = jnp.concatenate([v * beta[..., None], kb * jnp.exp(g)[..., None]], axis=-1)
    sol = lax.linalg.triangular_solve(lower + eye, rhs, left_side=True, lower=True, unit_diagonal=True)
    u, w = sol[..., :dv], sol[..., dv:]
    intra = jnp.einsum('bhnid,bhnjd->bhnij', q, k) * decay
    q_dec = q * jnp.exp(g)[..., None]
    g_last = g[..., -1]
    k_dec = k * jnp.exp(g_last[..., None] - g)[..., None]

    def step(S, xs):
        u_c, w_c, q_c, a_c, k_c, gl_c = xs
        v_new = u_c - jnp.einsum('bhcd,bhde->bhce', w_c, S)
        o_c = jnp.einsum('bhcd,bhde->bhce', q_c, S) + jnp.einsum('bhij,bhje->bhie', a_c, v_new)
        S = S * jnp.exp(gl_c)[..., None, None] + jnp.einsum('bhcd,bhce->bhde', k_c, v_new)
        return S, o_c

    xs = tuple(jnp.moveaxis(t, 2, 0) for t in (u, w, q_dec, intra, k_dec, g_last))
    state, o = lax.scan(step, state0, xs)
    o = jnp.transpose(o, (1, 0, 3, 2, 4)).reshape(Bn, L, H, dv)
    return o, state


def gated_rms_norm(o, z, g):
    y = rms_norm(o, g) * jax.nn.silu(z.reshape(o.shape).astype(jnp.float32))
    return y.reshape(o.shape[:2] + (-1,))


def gdn_mixer(qkv, ab, z, qkv_c, ab_c, z_c, conv_w, A_log, dt_bias, out_g, ctx_out):
    q, k, v, la, be = gdn_inputs(qkv, ab, conv_w, A_log, dt_bias)
    qc, kc, vc, lac, bec = gdn_inputs(qkv_c, ab_c, conv_w, A_log, dt_bias)
    s0 = jnp.zeros((q.shape[0], DN_HEADS, HEAD_DIM, HEAD_DIM), jnp.float32)
    flip = lambda t: t[:, ::-1]
    oc_f, s_f = gdn_chunked(qc, kc, vc, lac[:, :, 0], bec[:, :, 0], s0)
    oc_b, s_b = gdn_chunked(flip(qc), flip(kc), flip(vc), flip(lac[:, :, 1]), flip(bec[:, :, 1]), s0)
    o_f, _ = gdn_chunked(q, k, v, la[:, :, 0], be[:, :, 0], s_f)
    o_b, _ = gdn_chunked(flip(q), flip(k), flip(v), flip(la[:, :, 1]), flip(be[:, :, 1]), s_b)
    out = gated_rms_norm(o_f + flip(o_b), z, out_g).astype(qkv.dtype)
    out_c = gated_rms_norm(oc_f + flip(oc_b), z_c, out_g).astype(qkv.dtype) if ctx_out else None
    return out, out_c


def hybrid_layer(x, xc, c_silu, c_ctx_silu, cos, sin, w_ada, b_ada, norm1_g, norm2_g, w_in,
                 swa_q_g, swa_k_g, swa_sink, dn_conv_w, dn_A_log, dn_dt_bias, dn_out_g,
                 na_q_g, na_k_g, na_rpb, w_out, w_gate, w_up, w_down, ctx_out):
    G, R = SWA_KV_HEADS, SWA_HEADS // SWA_KV_HEADS
    mod = (c_silu @ w_ada + b_ada)[:, None, :]
    mod_c = c_ctx_silu @ w_ada + b_ada
    sh1, sc1, g1, sh2, sc2, g2 = jnp.split(mod, 6, axis=-1)
    csh1, csc1, cg1, csh2, csc2, cg2 = jnp.split(mod_c, 6, axis=-1)

    h = modulate(rms_norm(x, norm1_g), sh1, sc1)
    hc = modulate(rms_norm(xc, norm1_g), csh1, csc1)
    aq, ak, av, d_qkv, d_z, d_ab, nq, nk, nv = split_cols(h @ w_in)
    aqc, akc, avc, d_qkvc, d_zc, d_abc, nqc, nkc, nvc = split_cols(hc @ w_in)

    q_a = apply_axial_rope(rms_norm(heads(aq, SWA_HEADS), swa_q_g), cos, sin)
    k_a = apply_axial_rope(rms_norm(heads(ak, G), swa_k_g), cos, sin)
    v_a = heads(av, G)
    kc_a = rms_norm(heads(akc, G), swa_k_g)
    vc_a = heads(avc, G)
    sink = swa_sink.reshape(G, R)
    o_a = swa_attention(q_a.reshape(q_a.shape[:2] + (G, R, HEAD_DIM)), k_a, v_a, kc_a, vc_a, sink)

    o_b, o_bc = gdn_mixer(d_qkv, d_ab, d_z, d_qkvc, d_abc, d_zc, dn_conv_w, dn_A_log, dn_dt_bias, dn_out_g, ctx_out)

    q_n = rms_norm(heads(nq, NA_HEADS), na_q_g)
    k_n = rms_norm(heads(nk, NA_HEADS), na_k_g)
    v_n = heads(nv, NA_HEADS)
    kc_n = rms_norm(heads(nkc, NA_HEADS), na_k_g)
    vc_n = heads(nvc, NA_HEADS)
    o_c = neighborhood_attention(q_n, k_n, v_n, kc_n, vc_n, na_rpb)

    x = x + g1 * (jnp.concatenate([o_a, o_b, o_c], axis=-1) @ w_out)
    x = x + g2 * swiglu(modulate(rms_norm(x, norm2_g), sh2, sc2), w_gate, w_up, w_down)

    if ctx_out:
        qc_a = rms_norm(heads(aqc, SWA_HEADS), swa_q_g)
        o_ac = context_attention(qc_a.reshape(qc_a.shape[:2] + (G, R, HEAD_DIM)), kc_a, vc_a, sink)
        qc_n = rms_norm(heads(nqc, NA_HEADS), na_q_g)
        o_cc = context_attention(qc_n[..., None, :], kc_n, vc_n, None)
        xc = xc + cg1 * (jnp.concatenate([o_ac, o_bc, o_cc], axis=-1) @ w_out)
        xc = xc + cg2 * swiglu(modulate(rms_norm(xc, norm2_g), csh2, csc2), w_gate, w_up, w_down)
    return x, xc


def setup_inputs(seed: int = 0) -> dict:
    key = jax.random.key(seed)
    ks = jax.random.split(key, 24)
    f32 = jnp.float32

    def nrm(k, shape, scale):
        return jax.random.normal(k, shape, f32) * scale

    dt = jnp.exp(jax.random.uniform(ks[14], (DEPTH, 2, DN_HEADS), f32, math.log(1e-3), math.log(1e-1)))
    return {
        'x': nrm(ks[0], (BATCH, SEQ, D_MODEL), 1.0),
        'c': nrm(ks[1], (BATCH, D_MODEL), 1.0),
        'ctx': nrm(ks[2], (BATCH, CTX_LEN, D_MODEL), 1.0),
        'c_ctx': nrm(ks[3], (D_MODEL,), 1.0),
        'w_ada': nrm(ks[4], (DEPTH, D_MODEL, 6 * D_MODEL), 0.5 * D_MODEL ** -0.5),
        'b_ada': nrm(ks[5], (DEPTH, 6 * D_MODEL), 0.01),
        'norm1_g': 1.0 + nrm(ks[6], (DEPTH, D_MODEL), 0.05),
        'norm2_g': 1.0 + nrm(ks[7], (DEPTH, D_MODEL), 0.05),
        'w_in': nrm(ks[8], (DEPTH, D_MODEL, N_IN), D_MODEL ** -0.5),
        'swa_q_g': 1.0 + nrm(ks[9], (DEPTH, HEAD_DIM), 0.05),
        'swa_k_g': 1.0 + nrm(ks[10], (DEPTH, HEAD_DIM), 0.05),
        'swa_sink': nrm(ks[11], (DEPTH, SWA_HEADS), 0.5),
        'dn_conv_w': nrm(ks[12], (DEPTH, DN_CONV, 3 * DN_DIM), DN_CONV ** -0.5),
        'dn_A_log': jnp.log(jax.random.uniform(ks[13], (DEPTH, 2, DN_HEADS), f32, 1.0, 16.0)),
        'dn_dt_bias': dt + jnp.log(-jnp.expm1(-dt)),
        'dn_out_g': 1.0 + nrm(ks[15], (DEPTH, HEAD_DIM), 0.05),
        'na_q_g': 1.0 + nrm(ks[16], (DEPTH, HEAD_DIM), 0.05),
        'na_k_g': 1.0 + nrm(ks[17], (DEPTH, HEAD_DIM), 0.05),
        'na_rpb': nrm(ks[18], (DEPTH, NA_HEADS, 2 * NA_KH - 1, 2 * NA_KW - 1), 0.1),
        'w_out': nrm(ks[19], (DEPTH, MIX_WIDTH, D_MODEL), MIX_WIDTH ** -0.5),
        'w_gate': nrm(ks[20], (DEPTH, D_MODEL, FFN_HIDDEN), D_MODEL ** -0.5),
        'w_up': nrm(ks[21], (DEPTH, D_MODEL, FFN_HIDDEN), D_MODEL ** -0.5),
        'w_down': nrm(ks[22], (DEPTH, FFN_HIDDEN, D_MODEL), FFN_HIDDEN ** -0.5),
    }


def reference(x, c, ctx, c_ctx, w_ada, b_ada, norm1_g, norm2_g, w_in, swa_q_g, swa_k_g, swa_sink,
              dn_conv_w, dn_A_log, dn_dt_bias, dn_out_g, na_q_g, na_k_g, na_rpb, w_out,
              w_gate, w_up, w_down):
    L = x.shape[1]
    cos, sin = axial_rope_tables(L, x.dtype)
    c_silu = jax.nn.silu(c)
    c_ctx_silu = jax.nn.silu(c_ctx)
    xc = ctx
    for l in range(DEPTH):
        x, xc = hybrid_layer(x, xc, c_silu, c_ctx_silu, cos, sin, w_ada[l], b_ada[l], norm1_g[l], norm2_g[l],
                             w_in[l], swa_q_g[l], swa_k_g[l], swa_sink[l], dn_conv_w[l], dn_A_log[l],
                             dn_dt_bias[l], dn_out_g[l], na_q_g[l], na_k_g[l], na_rpb[l], w_out[l],
                             w_gate[l], w_up[l], w_down[l], ctx_out=(l < DEPTH - 1))
    return x
```

```python
import numpy as np
import concourse.bass as bass
import concourse.mybir as mybir
from concourse.bass_utils import run_bass_kernel_spmd
from contextlib import ExitStack

F32 = mybir.dt.float32
BF16 = mybir.dt.bfloat16
AF = mybir.ActivationFunctionType
ALU = mybir.AluOpType
AX = mybir.AxisListType

D = 2048
B = 4
L = 4096
LC = 256
HD = 128
NIN = 6688
FFN = 5632
EPS = 1e-6
NCORES = 8
SAME_ENGINE_SYNC = True


class Prog:
    def __init__(self):
        self.nc = bass.Bass("TRN2", target_bir_lowering=False)
        nc = self.nc
        self.E = {'pe': nc.tensor, 'dve': nc.vector, 'act': nc.scalar, 'pool': nc.gpsimd, 'sp': nc.sync}
        self.sem = {e: nc.alloc_semaphore('sem_' + e) for e in self.E}
        self.cnt = {e: 0 for e in self.E}
        self.NS = 8
        self.ring = {q: [nc.alloc_semaphore('ring_%s_%d' % (q, i)) for i in range(self.NS)] for q in ('sp', 'pool')}
        self.ndma = {q: 0 for q in self.ring}
        self.seen = {e: {} for e in self.E}
        self.lastw = {}
        self.readers = {}
        self.semobj = {}
        for e in self.E:
            self.semobj[('eng', e)] = self.sem[e]
        for q in self.ring:
            for i in range(self.NS):
                self.semobj[('ring', q, i)] = self.ring[q][i]
        self.n_inst = 0
        self.marks = []

    @staticmethod
    def key(k):
        if isinstance(k, (str, tuple)):
            return k
        if hasattr(k, 'tensor'):
            return k.tensor.name
        return k.name

    def _deps(self, e, reads, writes, inorder=False):
        need = {}

        def add(tok):
            sk, val, peng = tok
            if peng == e and (e == 'pe' or inorder or not SAME_ENGINE_SYNC):
                return
            if self.seen[e].get(sk, 0) >= val:
                return
            if need.get(sk, 0) < val:
                need[sk] = val

        for k in reads:
            t = self.lastw.get(k)
            if t is not None:
                add(t)
        for k in writes:
            t = self.lastw.get(k)
            if t is not None:
                add(t)
            for sk, (val, peng) in self.readers.get(k, {}).items():
                add((sk, val, peng))
        return need

    def _record(self, tok, reads, writes):
        sk, val, peng = tok
        for k in writes:
            self.lastw[k] = tok
            self.readers[k] = {}
        for k in reads:
            r = self.readers.setdefault(k, {})
            r[sk] = (val, peng)

    def op(self, e, fn, reads=(), writes=(), inorder=False):
        reads = [self.key(k) for k in reads]
        writes = [self.key(k) for k in writes]
        eng = self.E[e]
        need = self._deps(e, reads, writes, inorder)
        items = list(need.items())
        for sk, val in items[:-1]:
            eng.wait_ge(self.semobj[sk], val)
            self.seen[e][sk] = val
        ins = fn(eng)
        if items:
            sk, val = items[-1]
            ins.wait_op(self.semobj[sk], val, 'sem-ge')
            self.seen[e][sk] = val
        self.cnt[e] += 1
        ins.then_inc(self.sem[e], 1)
        self._record((('eng', e), self.cnt[e], e), reads, writes)
        self.n_inst += 1
        return ins

    def dma(self, q, out, in_, reads=(), writes=(), **kw):
        reads = [self.key(k) for k in reads]
        writes = [self.key(k) for k in writes]
        eng = self.E[q]
        k = self.ndma[q]
        r = k % self.NS
        base = 16 * (k // self.NS)
        need = self._deps(q, reads, writes)
        sk_ring = ('ring', q, r)
        if k >= self.NS and self.seen[q].get(sk_ring, 0) < base:
            need[sk_ring] = max(need.get(sk_ring, 0), base)
        for sk, val in need.items():
            eng.wait_ge(self.semobj[sk], val)
            self.seen[q][sk] = val
        ins = eng.dma_start(out=out, in_=in_, **kw)
        ins.then_inc(self.ring[q][r], 16)
        self.ndma[q] += 1
        self._record((sk_ring, base + 16, 'dma'), reads, writes)
        self.n_inst += 1
        return ins

    def finish(self):
        sp = self.E['sp']
        for q in self.ring:
            k = self.ndma[q]
            for r in range(self.NS):
                n = (k - r + self.NS - 1) // self.NS if k > r else 0
                if n > 0:
                    sp.wait_ge(self.ring[q][r], 16 * n)
        for e in self.E:
            if self.cnt[e] > 0:
                sp.wait_ge(self.sem[e], self.cnt[e])

    def mm(self, out, lhsT, rhs, start, stop, reads, writes):
        return self.op('pe', lambda g: g.matmul(out, lhsT, rhs, start=start, stop=stop), reads, writes)

    def tr(self, out, in_, ident, reads, writes):
        return self.op('pe', lambda g: g.transpose(out, in_, ident), reads, writes)

    def act(self, out, in_, func, reads, writes, bias=None, scale=None, accum_out=None, e='act'):
        kw = {}
        if bias is not None:
            kw['bias'] = bias
        if scale is not None:
            kw['scale'] = scale
        if accum_out is not None:
            kw['accum_out'] = accum_out
        return self.op('act', lambda g: g.activation(out, in_, func, **kw), reads, writes)

    def tt(self, e, out, in0, in1, op, reads, writes, inorder=False):
        return self.op(e, lambda g: g.tensor_tensor(out, in0, in1, op), reads, writes, inorder)

    def ts(self, e, out, in0, s1, s2, op0, op1, reads, writes):
        if op1 is None:
            return self.op(e, lambda g: g.tensor_scalar(out, in0, s1, None, op0), reads, writes)
        return self.op(e, lambda g: g.tensor_scalar(out, in0, s1, s2, op0, op1), reads, writes)

    def stt(self, out, in0, scalar, in1, op0, op1, reads, writes):
        return self.op('dve', lambda g: g.scalar_tensor_tensor(out, in0, scalar, in1, op0, op1), reads, writes)

    def copy(self, e, out, in_, reads, writes):
        if e == 'act':
            return self.op('act', lambda g: g.copy(out, in_), reads, writes)
        return self.op(e, lambda g: g.tensor_copy(out, in_), reads, writes)

    def memset(self, e, ap, val, writes):
        return self.op(e, lambda g: g.memset(ap, val), (), writes)


def run_prog(P, in_maps):
    P.finish()
    res = run_bass_kernel_spmd(P.nc, in_maps, core_ids=list(range(NCORES)))
    return res.results


NC0 = 6 * D // NCORES


def build_k0():
    P = Prog()
    nc = P.nc
    cT = nc.dram_tensor("cT", [D, 5], F32, kind="ExternalInput").ap()
    wa = nc.dram_tensor("wa", [2, D, NC0], F32, kind="ExternalInput").ap()
    ba = nc.dram_tensor("ba", [2, NC0], F32, kind="ExternalInput").ap()
    mod = nc.dram_tensor("mod", [2, 5, NC0], F32, kind="ExternalOutput").ap()
    cs = nc.alloc_sbuf_tensor("cs", [128, 16, 5], F32)
    w = [nc.alloc_sbuf_tensor("w%d" % i, [128, 8, NC0], F32) for i in range(2)]
    bb = nc.alloc_sbuf_tensor("bb", [5, 2, NC0], F32)
    res = nc.alloc_sbuf_tensor("res", [5, 2, NC0], F32)
    ps = [nc.alloc_psum_tensor("ps%d" % i, [128, 512], F32) for i in range(3)]
    with nc.allow_non_contiguous_dma(reason="tiny transposed load"):
        P.dma('sp', cs[:], cT.rearrange("(k p) r -> p k r", p=128), writes=['cs'])
    for l in range(2):
        P.dma('sp', bb[:, l, :], ba[l:l + 1, :].partition_broadcast(5), writes=['bb%d' % l])
    P.act(cs[:], cs[:], AF.Silu, reads=['cs'], writes=['cs'])
    for l in range(2):
        for half in range(2):
            wt = w[half]
            wk = 'w%d' % half
            P.dma('sp', wt[:], wa[l, half * 1024:(half + 1) * 1024, :].rearrange("(k p) n -> p k n", p=128),
                  writes=[wk])
            for n in range(3):
                for k in range(8):
                    kk = half * 8 + k
                    P.mm(ps[n][0:5, :], cs[:, kk, :], wt[:, k, n * 512:(n + 1) * 512],
                         start=(kk == 0), stop=(kk == 15), reads=['cs', wk], writes=['ps%d' % n])
        for n in range(3):
            P.tt('dve', res[:, l, n * 512:(n + 1) * 512], ps[n][0:5, :], bb[:, l, n * 512:(n + 1) * 512], ALU.add,
                 reads=['ps%d' % n, 'bb%d' % l], writes=['res%d' % l])
        P.dma('pool', mod[l], res[:, l, :], reads=['res%d' % l])
    return P


def run_k0(c, c_ctx, w_ada, b_ada):
    P = build_k0()
    cT = np.ascontiguousarray(np.concatenate([c, c_ctx[None, :]], axis=0).T)
    in_maps = []
    for i in range(NCORES):
        sl = slice(i * NC0, (i + 1) * NC0)
        in_maps.append({"cT": cT, "wa": np.ascontiguousarray(w_ada[:, :, sl]),
                        "ba": np.ascontiguousarray(b_ada[:, sl])})
    res = run_prog(P, in_maps)
    return np.concatenate([r["mod"] for r in res], axis=2)


RT = 2176
NT = RT // 128
SLABS = [(0, 512, 'F'), (512, 256, 'F'), (768, 256, 'T'),
         (1024, 512, 'F'), (1536, 512, 'F'), (2048, 512, 'F'), (2560, 512, 'F'), (3072, 512, 'F'), (3584, 512, 'F'),
         (4096, 512, 'T'), (4608, 512, 'T'), (5120, 32, 'T'),
         (5152, 512, 'F'), (5664, 512, 'F'), (6176, 512, 'T')]
FM_COLS = [c for (s, n, k) in SLABS if k == 'F' for c in range(s, s + n)]
TM_COLS = [c for (s, n, k) in SLABS if k == 'T' for c in range(s, s + n)]
NFM = len(FM_COLS)
NTM = len(TM_COLS)
TOKG = [(0, 512), (512, 512), (1024, 512), (1536, 512), (2048, 128)]


def emit_norm_mod_T(P, xt, xk, A_t, B_t, Ak, tmp, hb, hT, t, ident, pst, pstk, junk_scale):
    nc = P.nc
    ss = P.ss
    P.act(tmp[:], xt[:], AF.Square, reads=[xk], writes=['tmp', 'ss'], accum_out=ss[:, 0:1])
    P.act(ss[:, 1:2], ss[:, 0:1], AF.Sqrt, reads=['ss'], writes=['ss'], bias=P.eps_t[:, 0:1], scale=1.0 / D)
    P.op('dve', lambda g: g.reciprocal(ss[:, 2:3], ss[:, 1:2]), reads=['ss'], writes=['ss'])
    P.stt(tmp[:], xt[:], ss[:, 2:3], A_t[:], ALU.mult, ALU.mult, reads=[xk, 'ss', Ak], writes=['tmp'])
    P.tt('pool', hb[:], tmp[:], B_t[:], ALU.add, reads=['tmp', Ak], writes=['hb'])
    for half in range(2):
        pt = pst[half]
        pk = pstk[half]
        for j in range(8):
            k = half * 8 + j
            P.tr(pt[:, j * 128:(j + 1) * 128], hb[:, k * 128:(k + 1) * 128], ident[:], reads=['hb', 'ident'], writes=[pk])
        P.copy('act' if half == 0 else 'dve', hT[:, half * 8:(half + 1) * 8, t * 128:(t + 1) * 128],
               pt[:].rearrange("p (j n) -> p j n", j=8), reads=[pk], writes=['hT'])


def build_k1():
    P = Prog()
    nc = P.nc
    xr = nc.dram_tensor("xr", [RT, D], F32, kind="ExternalInput").ap()
    modv = nc.dram_tensor("modv", [4, D], F32, kind="ExternalInput").ap()
    g1 = nc.dram_tensor("g1", [1, D], F32, kind="ExternalInput").ap()
    w_in = nc.dram_tensor("w_in", [D, NIN], F32, kind="ExternalInput").ap()
    identd = nc.dram_tensor("identd", [128, 128], F32, kind="ExternalInput").ap()
    pT = nc.dram_tensor("pT", [NFM, RT], F32, kind="ExternalOutput").ap()
    pM = nc.dram_tensor("pM", [RT, NTM], F32, kind="ExternalOutput").ap()

    hT = nc.alloc_sbuf_tensor("hT", [128, 16, RT], BF16)
    A_lat = nc.alloc_sbuf_tensor("A_lat", [128, D], F32)
    B_lat = nc.alloc_sbuf_tensor("B_lat", [128, D], F32)
    A_ctx = nc.alloc_sbuf_tensor("A_ctx", [128, D], F32)
    B_ctx = nc.alloc_sbuf_tensor("B_ctx", [128, D], F32)
    xts = [nc.alloc_sbuf_tensor("xt%d" % i, [128, D], F32) for i in range(2)]
    tmp = nc.alloc_sbuf_tensor("tmp", [128, D], F32)
    hb = nc.alloc_sbuf_tensor("hb", [128, D], BF16)
    identf = nc.alloc_sbuf_tensor("identf", [128, 128], F32)
    ident = nc.alloc_sbuf_tensor("ident", [128, 128], BF16)
    P.ss = nc.alloc_sbuf_tensor("ss", [128, 4], F32)
    P.eps_t = nc.alloc_sbuf_tensor("eps_t", [128, 1], F32)
    slab = [nc.alloc_sbuf_tensor("slab%d" % i, [128, 16, 512], BF16) for i in range(2)]
    stf = [nc.alloc_sbuf_tensor("stf%d" % i, [128, RT], F32) for i in range(2)]
    stt_ = [nc.alloc_sbuf_tensor("stt%d" % i, [128, 512], F32) for i in range(2)]
    pst = [nc.alloc_psum_tensor("pst%d" % i, [128, 1024], BF16) for i in range(2)]
    psm = [nc.alloc_psum_tensor("psm%d" % i, [128, 512], F32) for i in range(4)]

    P.memset('dve', P.eps_t[:], EPS, writes=['eps_t'])
    P.dma('sp', identf[:], identd, writes=['identf'])
    P.copy('dve', ident[:], identf[:], reads=['identf'], writes=['ident'])
    P.dma('sp', tmp[:], g1[0:1, :].partition_broadcast(128), writes=['tmp'])
    for (A_t, B_t, r0, nm) in ((A_lat, B_lat, 0, 'A_lat'), (A_ctx, B_ctx, 2, 'A_ctx')):
        P.dma('sp', B_t[:], modv[r0:r0 + 1, :].partition_broadcast(128), writes=[nm])
        P.dma('sp', A_t[:], modv[r0 + 1:r0 + 2, :].partition_broadcast(128), writes=[nm])
        P.stt(A_t[:], A_t[:], 1.0, tmp[:], ALU.add, ALU.mult, reads=[nm, 'tmp'], writes=[nm])

    for t in range(NT):
        xt = xts[t % 2]
        xk = 'xt%d' % (t % 2)
        P.dma('sp', xt[:], xr[t * 128:(t + 1) * 128, :], writes=[xk])
        if t < 16:
            emit_norm_mod_T(P, xt, xk, A_lat, B_lat, 'A_lat', tmp, hb, hT, t, ident, pst, ['pst0', 'pst1'], None)
        else:
            emit_norm_mod_T(P, xt, xk, A_ctx, B_ctx, 'A_ctx', tmp, hb, hT, t, ident, pst, ['pst0', 'pst1'], None)

    w_v = w_in.rearrange("(k p) n -> p k n", p=128)
    fm_off = 0
    tm_off = 0
    ev = 0
    nstf = 0
    nstt = 0
    for si, (c0, ncol, kind) in enumerate(SLABS):
        sl = slab[si % 2]
        sk = 'slab%d' % (si % 2)
        for kh in range(2):
            P.dma('pool', sl[:, kh * 8:(kh + 1) * 8, 0:ncol], w_v[:, kh * 8:(kh + 1) * 8, c0:c0 + ncol], writes=[sk])
        if kind == 'F':
            for ct in range(ncol // 128):
                st = stf[nstf % 2]
                stk = 'stf%d' % (nstf % 2)
                nstf += 1
                for (t0, tn) in TOKG:
                    ps = psm[ev % 4]
                    pk = 'psm%d' % (ev % 4)
                    for k in range(16):
                        P.mm(ps[:, 0:tn], sl[:, k, ct * 128:(ct + 1) * 128], hT[:, k, t0:t0 + tn],
                             start=(k == 0), stop=(k == 15), reads=[sk, 'hT'], writes=[pk])
                    P.copy('act' if ev % 2 == 0 else 'dve', st[:, t0:t0 + tn], ps[:, 0:tn], reads=[pk], writes=[stk])
                    ev += 1
                P.dma('sp', pT[fm_off:fm_off + 128, :], st[:], reads=[stk])
                fm_off += 128
        else:
            for t in range(NT):
                st = stt_[nstt % 2]
                stk = 'stt%d' % (nstt % 2)
                nstt += 1
                ps = psm[ev % 4]
                pk = 'psm%d' % (ev % 4)
                for k in range(16):
                    P.mm(ps[:, 0:ncol], hT[:, k, t * 128:(t + 1) * 128], sl[:, k, 0:ncol],
                         start=(k == 0), stop=(k == 15), reads=[sk, 'hT'], writes=[pk])
                P.copy('act' if ev % 2 == 0 else 'dve', st[:, 0:ncol], ps[:, 0:ncol], reads=[pk], writes=[stk])
                ev += 1
                P.dma('sp', pM[t * 128:(t + 1) * 128, tm_off:tm_off + ncol], st[:, 0:ncol], reads=[stk])
            tm_off += ncol
    return P


_PROG_CACHE = {}


def get_prog(name, builder):
    return builder()


def core_rows(x_all, xc_all, i):
    b, hf = i // 2, i % 2
    return np.concatenate([x_all[b, hf * 2048:(hf + 1) * 2048], xc_all[b, hf * 128:(hf + 1) * 128]], axis=0)


def run_k1(x_all, xc_all, mod_l, g1_l, w_in_l):
    P = build_k1()
    ident = np.eye(128, dtype=np.float32)
    in_maps = []
    for i in range(NCORES):
        b = i // 2
        modv = np.stack([mod_l[b, 0:D], mod_l[b, D:2 * D], mod_l[4, 0:D], mod_l[4, D:2 * D]], axis=0)
        in_maps.append({"xr": np.ascontiguousarray(core_rows(x_all, xc_all, i)), "modv": np.ascontiguousarray(modv),
                        "g1": np.ascontiguousarray(g1_l[None, :]), "w_in": w_in_l, "identd": ident})
    res = run_prog(P, in_maps)
    proj = np.empty((B, L, NIN), np.float32)
    projc = np.empty((B, LC, NIN), np.float32)
    for i in range(NCORES):
        b, hf = i // 2, i % 2
        full = np.empty((RT, NIN), np.float32)
        full[:, FM_COLS] = res[i]["pT"].T
        full[:, TM_COLS] = res[i]["pM"]
        proj[b, hf * 2048:(hf + 1) * 2048] = full[:2048]
        projc[b, hf * 128:(hf + 1) * 128] = full[2048:]
    return proj, projc


def barrier(P):
    for e in P.E:
        eng = P.E[e]
        for e2 in P.E:
            if e2 != e and P.cnt[e2] > 0 and P.seen[e].get(('eng', e2), 0) < P.cnt[e2]:
                eng.wait_ge(P.sem[e2], P.cnt[e2])
                P.seen[e][('eng', e2)] = P.cnt[e2]
        for q in P.ring:
            k = P.ndma[q]
            for r in range(P.NS):
                n = (k - r + P.NS - 1) // P.NS if k > r else 0
                if n > 0 and P.seen[e].get(('ring', q, r), 0) < 16 * n:
                    eng.wait_ge(P.ring[q][r], 16 * n)
                    P.seen[e][('ring', q, r)] = 16 * n


    if P.cnt['pe'] > 30000:
        try:
            new = P.nc.alloc_semaphore('sem_pe_%d' % len(P.marks) + '_%d' % P.n_inst)
        except Exception:
            new = None
        if new is not None:
            sk = ('eng', 'pe')
            P.sem['pe'] = new
            P.semobj[sk] = new
            P.cnt['pe'] = 0
            for e in P.E:
                P.seen[e].pop(sk, None)
            for k in [k for k, t in P.lastw.items() if t[0] == sk]:
                del P.lastw[k]
            for k in P.readers:
                P.readers[k].pop(sk, None)


Prog.barrier = barrier

SGS = [[(0, 512), (512, 512)], [(1024, 512), (1536, 512), (2048, 128)]]
NH = FFN // 128


def build_k3():
    P = Prog()
    nc = P.nc
    xT = nc.dram_tensor("xT", [D, RT], F32, kind="ExternalInput").ap()
    oT = nc.dram_tensor("oT", [D, RT], F32, kind="ExternalInput").ap()
    vec = nc.dram_tensor("vec", [9, D], F32, kind="ExternalInput").ap()
    w_out = nc.dram_tensor("w_out", [D, D], F32, kind="ExternalInput").ap()
    w_gate = nc.dram_tensor("w_gate", [D, FFN], F32, kind="ExternalInput").ap()
    w_up = nc.dram_tensor("w_up", [D, FFN], F32, kind="ExternalInput").ap()
    w_down = nc.dram_tensor("w_down", [FFN, D], F32, kind="ExternalInput").ap()
    xoT = nc.dram_tensor("xoT", [D, RT], F32, kind="ExternalOutput").ap()
    xmT = nc.dram_tensor("xmT", [D, RT], F32, kind="Internal").ap()

    V = nc.alloc_sbuf_tensor("V", [128, 9, 16], F32)
    A2 = nc.alloc_sbuf_tensor("A2", [128, 2, 16], F32)
    ones = nc.alloc_sbuf_tensor("ones", [128, 128], F32)
    eps_t = nc.alloc_sbuf_tensor("eps_t", [128, 1], F32)
    ps = [nc.alloc_psum_tensor("ps%d" % i, [128, 512], F32) for i in range(8)]
    with nc.allow_non_contiguous_dma(reason="small per-feature vectors, feature-major"):
        P.dma('sp', V[:], vec.rearrange("r (c p) -> p r c", p=128), writes=['V'])
    P.memset('dve', ones[:], 1.0, writes=['ones'])
    P.memset('dve', eps_t[:], EPS, writes=['eps_t'])
    P.stt(A2[:, 0, :], V[:, 2, :], 1.0, V[:, 8, :], ALU.add, ALU.mult, reads=['V'], writes=['A2'])
    P.stt(A2[:, 1, :], V[:, 6, :], 1.0, V[:, 8, :], ALU.add, ALU.mult, reads=['V'], writes=['A2'])

    with nc.sbuf_tensor("oTb", [128, 16, RT], BF16) as oTb, \
            nc.sbuf_tensor("wo0", [128, 16, 128], BF16) as wo0, nc.sbuf_tensor("wo1", [128, 16, 128], BF16) as wo1, \
            nc.sbuf_tensor("xa0", [128, RT], F32) as xa0, nc.sbuf_tensor("xa1", [128, RT], F32) as xa1, \
            nc.sbuf_tensor("xb0", [128, RT], F32) as xb0, nc.sbuf_tensor("xb1", [128, RT], F32) as xb1:
        wo = [wo0, wo1]
        xa = [xa0, xa1]
        xb = [xb0, xb1]
        oT_v = oT.rearrange("(k p) r -> p k r", p=128)
        for kq in range(4):
            P.dma('pool', oTb[:, kq * 4:(kq + 1) * 4, :], oT_v[:, kq * 4:(kq + 1) * 4, :], writes=['oTb'])
        wo_v = w_out.rearrange("(k p) n -> p k n", p=128)
        ev = 0
        for c in range(16):
            w = wo[c % 2]
            wk = 'wo%d' % (c % 2)
            P.dma('pool', w[:], wo_v[:, :, c * 128:(c + 1) * 128], writes=[wk])
            xin = xa[c % 2]
            xik = 'xa%d' % (c % 2)
            xout = xb[c % 2]
            xok = 'xb%d' % (c % 2)
            P.dma('sp', xin[:], xT[c * 128:(c + 1) * 128, :], writes=[xik])
            for (t0, tn) in TOKG:
                pt = ps[ev % 2]
                pk = 'ps%d' % (ev % 2)
                ev += 1
                for k in range(16):
                    P.mm(pt[:, 0:tn], w[:, k, :], oTb[:, k, t0:t0 + tn], start=(k == 0), stop=(k == 15),
                         reads=[wk, 'oTb'], writes=[pk])
                gcol = V[:, 0, c:c + 1] if t0 < 2048 else V[:, 4, c:c + 1]
                P.stt(xout[:, t0:t0 + tn], pt[:, 0:tn], gcol, xin[:, t0:t0 + tn], ALU.mult, ALU.add,
                      reads=[pk, 'V', xik], writes=[xok])
            P.dma('sp', xmT[c * 128:(c + 1) * 128, :], xout[:], reads=[xok], writes=['xmT%d' % c])
    P.barrier()

    SGMAX = 1152
    with nc.sbuf_tensor("h2T", [128, 16, SGMAX], BF16) as h2T, nc.sbuf_tensor("aT", [128, NH, SGMAX], BF16) as aT, \
            nc.sbuf_tensor("rstd", [128, SGMAX], F32) as rstd, \
            nc.sbuf_tensor("xc0", [128, SGMAX], F32) as xc0, nc.sbuf_tensor("xc1", [128, SGMAX], F32) as xc1, \
            nc.sbuf_tensor("t0", [128, SGMAX], F32) as tb0, nc.sbuf_tensor("t1", [128, SGMAX], F32) as tb1, \
            nc.sbuf_tensor("wg0", [128, 16, 128], BF16) as wg0, nc.sbuf_tensor("wg1", [128, 16, 128], BF16) as wg1, \
            nc.sbuf_tensor("wu0", [128, 16, 128], BF16) as wu0, nc.sbuf_tensor("wu1", [128, 16, 128], BF16) as wu1, \
            nc.sbuf_tensor("wd0", [128, NH, 128], BF16) as wd0, nc.sbuf_tensor("wd1", [128, NH, 128], BF16) as wd1, \
            nc.sbuf_tensor("sl0", [128, 512], F32) as sl0, nc.sbuf_tensor("sl1", [128, 512], F32) as sl1:
        xc = [xc0, xc1]
        tb = [tb0, tb1]
        wg = [wg0, wg1]
        wu = [wu0, wu1]
        wd = [wd0, wd1]
        sl = [sl0, sl1]
        wg_v = w_gate.rearrange("(k p) n -> p k n", p=128)
        wu_v = w_up.rearrange("(k p) n -> p k n", p=128)
        wd_v = w_down.rearrange("(j p) n -> p j n", p=128)
        nld = 0
        for sg in SGS:
            s0 = sg[0][0]
            sn = sum(tn for (_, tn) in sg)
            nlat = sum(tn for (t0, tn) in sg if t0 < 2048)
            for c in range(16):
                xt = xc[nld % 2]
                xk = 'xc%d' % (nld % 2)
                tt_ = tb[nld % 2]
                tk = 't%d' % (nld % 2)
                nld += 1
                P.dma('sp', xt[:, 0:sn], xmT[c * 128:(c + 1) * 128, s0:s0 + sn], reads=['xmT%d' % c], writes=[xk])
                P.act(tt_[:, 0:sn], xt[:, 0:sn], AF.Square, reads=[xk], writes=[tk])
                for gi, (t0, tn) in enumerate(sg):
                    P.mm(ps[2 + gi][:, 0:tn], ones[:], tt_[:, t0 - s0:t0 - s0 + tn], start=(c == 0), stop=(c == 15),
                         reads=['ones', tk], writes=['ps%d' % (2 + gi)])
            for gi, (t0, tn) in enumerate(sg):
                P.act(rstd[:, t0 - s0:t0 - s0 + tn], ps[2 + gi][:, 0:tn], AF.Ln, reads=['ps%d' % (2 + gi), 'eps_t'],
                      writes=['rstd'], bias=eps_t[:, 0:1], scale=1.0 / D)
            P.act(rstd[:, 0:sn], rstd[:, 0:sn], AF.Exp, reads=['rstd'], writes=['rstd'], scale=-0.5)
            for c in range(16):
                xt = xc[nld % 2]
                xk = 'xc%d' % (nld % 2)
                tt_ = tb[nld % 2]
                tk = 't%d' % (nld % 2)
                nld += 1
                P.dma('sp', xt[:, 0:sn], xmT[c * 128:(c + 1) * 128, s0:s0 + sn], reads=['xmT%d' % c], writes=[xk])
                P.tt('dve', tt_[:, 0:sn], xt[:, 0:sn], rstd[:, 0:sn], ALU.mult, reads=[xk, 'rstd'], writes=[tk])
                P.ts('pool', h2T[:, c, 0:nlat], tt_[:, 0:nlat], A2[:, 0, c:c + 1], V[:, 1, c:c + 1], ALU.mult, ALU.add,
                     reads=[tk, 'A2', 'V'], writes=['h2T'])
                if nlat < sn:
                    P.ts('pool', h2T[:, c, nlat:sn], tt_[:, nlat:sn], A2[:, 1, c:c + 1], V[:, 5, c:c + 1], ALU.mult, ALU.add,
                         reads=[tk, 'A2', 'V'], writes=['h2T'])
            ev = 0
            for j in range(NH):
                g_ = wg[j % 2]
                gk = 'wg%d' % (j % 2)
                u_ = wu[j % 2]
                uk = 'wu%d' % (j % 2)
                P.dma('pool', g_[:], wg_v[:, :, j * 128:(j + 1) * 128], writes=[gk])
                P.dma('pool', u_[:], wu_v[:, :, j * 128:(j + 1) * 128], writes=[uk])
                for (t0, tn) in sg:
                    pg = ps[ev % 2]
                    pgk = 'ps%d' % (ev % 2)
                    pu = ps[5 + ev % 2]
                    puk = 'ps%d' % (5 + ev % 2)
                    s_ = sl[ev % 2]
                    sk = 'sl%d' % (ev % 2)
                    ev += 1
                    for k in range(16):
                        P.mm(pg[:, 0:tn], g_[:, k, :], h2T[:, k, t0 - s0:t0 - s0 + tn], start=(k == 0), stop=(k == 15),
                             reads=[gk, 'h2T'], writes=[pgk])
                    for k in range(16):
                        P.mm(pu[:, 0:tn], u_[:, k, :], h2T[:, k, t0 - s0:t0 - s0 + tn], start=(k == 0), stop=(k == 15),
                             reads=[uk, 'h2T'], writes=[puk])
                    P.act(s_[:, 0:tn], pg[:, 0:tn], AF.Silu, reads=[pgk], writes=[sk])
                    P.tt('dve', aT[:, j, t0 - s0:t0 - s0 + tn], pu[:, 0:tn], s_[:, 0:tn], ALU.mult, reads=[puk, sk], writes=['aT'])
            ev = 0
            for c in range(16):
                w = wd[c % 2]
                wk = 'wd%d' % (c % 2)
                for jh in range(4):
                    P.dma('pool', w[:, jh * 11:(jh + 1) * 11, :], wd_v[:, jh * 11:(jh + 1) * 11, c * 128:(c + 1) * 128], writes=[wk])
                xt = xc[nld % 2]
                xk = 'xc%d' % (nld % 2)
                tt_ = tb[nld % 2]
                tk = 't%d' % (nld % 2)
                nld += 1
                P.dma('sp', xt[:, 0:sn], xmT[c * 128:(c + 1) * 128, s0:s0 + sn], reads=['xmT%d' % c], writes=[xk])
                for (t0, tn) in sg:
                    pt = ps[3 + ev % 2]
                    pk = 'ps%d' % (3 + ev % 2)
                    ev += 1
                    for j in range(NH):
                        P.mm(pt[:, 0:tn], w[:, j, :], aT[:, j, t0 - s0:t0 - s0 + tn], start=(j == 0), stop=(j == NH - 1),
                             reads=[wk, 'aT'], writes=[pk])
                    gcol = V[:, 3, c:c + 1] if t0 < 2048 else V[:, 7, c:c + 1]
                    P.stt(tt_[:, t0 - s0:t0 - s0 + tn], pt[:, 0:tn], gcol, xt[:, t0 - s0:t0 - s0 + tn], ALU.mult, ALU.add,
                          reads=[pk, 'V', xk], writes=[tk])
                P.dma('sp', xoT[c * 128:(c + 1) * 128, s0:s0 + sn], tt_[:, 0:sn], reads=[tk])
    return P


def run_k3(x_all, xc_all, o_all, oc_all, mod_l, g2n_l, w_out_l, w_gate_l, w_up_l, w_down_l):
    P = build_k3()
    in_maps = []
    for i in range(NCORES):
        b = i // 2
        m, mc = mod_l[b], mod_l[4]
        vec = np.stack([m[2 * D:3 * D], m[3 * D:4 * D], m[4 * D:5 * D], m[5 * D:6 * D],
                        mc[2 * D:3 * D], mc[3 * D:4 * D], mc[4 * D:5 * D], mc[5 * D:6 * D], g2n_l], axis=0)
        in_maps.append({"xT": np.ascontiguousarray(core_rows(x_all, xc_all, i).T),
                        "oT": np.ascontiguousarray(core_rows(o_all, oc_all, i).T),
                        "vec": np.ascontiguousarray(vec), "w_out": w_out_l, "w_gate": w_gate_l,
                        "w_up": w_up_l, "w_down": w_down_l})
    res = run_prog(P, in_maps)
    xo = np.empty((B, L, D), np.float32)
    xco = np.empty((B, LC, D), np.float32)
    for i in range(NCORES):
        b, hf = i // 2, i % 2
        r = res[i]["xoT"].T
        xo[b, hf * 2048:(hf + 1) * 2048] = r[:2048]
        xco[b, hf * 128:(hf + 1) * 128] = r[2048:]
    return xo, xco


TT = LC + L
NTT = TT // 128
NEG = -30000.0
C_ID, C_ONES, C_RM, C_TRI, C_U, C_SEL, C_MP, C_MN, C_XI, C_END = 0, 128, 256, 384, 640, 896, 1152, 1408, 1664, 1920
RP_OUTG, RP_ALOG, RP_DTB, RP_SINK, RP_END = 0, 128, 136, 144, 146
BLK = [(0, 256)] + [(256 + 512 * i, 512) for i in range(8)]


def k2_consts():
    c = np.zeros((128, C_END), np.float32)
    idx = np.arange(128)
    same = (idx[:, None] // 64) == (idx[None, :] // 64)
    c[:, C_ID:C_ID + 128] = np.eye(128)
    c[:, C_ONES:C_ONES + 128] = 1.0
    rm = np.zeros((128, 128), np.float32)
    for m in range(128):
        blk = m // 32
        if blk % 2 == 0:
            rm[m + 32, m] = -1.0
        else:
            rm[m - 32, m] = 1.0
    c[:, C_RM:C_RM + 128] = rm
    c[:, C_TRI:C_TRI + 128] = (idx[:, None] <= idx[None, :]) & same
    c[:, C_TRI + 128:C_TRI + 256] = (idx[:, None] >= idx[None, :]) & same
    c[:, C_U:C_U + 128] = (idx[:, None] > idx[None, :]) & same
    c[:, C_U + 128:C_U + 256] = (idx[:, None] < idx[None, :]) & same
    c[:, C_SEL:C_SEL + 128] = (idx[:, None] < 64)
    c[:, C_SEL + 128:C_SEL + 256] = (idx[:, None] >= 64)
    mp = np.where(idx[:, None] >= idx[None, :], 0.0, NEG)
    mn = np.where(idx[:, None] <= idx[None, :], 0.0, NEG)
    c[:, C_MP:C_MP + 256] = np.concatenate([mp, mp], axis=1)
    c[:, C_MN:C_MN + 256] = np.concatenate([mn, mn], axis=1)
    for p in range(128):
        cg = (p % 64) % 16
        for mm in range(4):
            m = 4 * cg + mm
            i = m if p < 64 else 63 - m
            c[p, C_XI + mm * 64 + i] = 1.0
    return c


def na_tables(rpb_h):
    W, R = 64, 64
    out = np.full((128, 21, 128), NEG, np.float32)
    kp = np.arange(128)
    qp = np.arange(128)

    def table(kt, pr):
        rk = 2 * kt + kp // 64
        ck = kp % 64
        rq = 2 * pr + qp // 64
        cq = qp % 64
        rs = np.clip(rq - 4, 0, R - 8)
        cs = np.clip(cq - 8, 0, W - 16)
        valid = (rk[:, None] >= rs[None, :]) & (rk[:, None] < rs[None, :] + 8) & \
                (ck[:, None] >= cs[None, :]) & (ck[:, None] < cs[None, :] + 16)
        dr = np.clip(rk[:, None] - rq[None, :] + 7, 0, 14)
        dc = np.clip(ck[:, None] - cq[None, :], -15, 15) + 15
        return np.where(valid, rpb_h[dr, dc], np.float32(NEG)).astype(np.float32)

    for i in range(5):
        out[:, i, :] = table(i, 2)
    for a, pr in enumerate((0, 1)):
        for i in range(4):
            out[:, 5 + a * 4 + i, :] = table(i, pr)
    for a, pr in enumerate((30, 31)):
        for i in range(4):
            out[:, 13 + a * 4 + i, :] = table(28 + i, pr)
    return out


def na_chunks(pr):
    if pr <= 1:
        return [(i, 5 + pr * 4 + i) for i in range(4)]
    if pr >= 30:
        return [(28 + i, 13 + (pr - 30) * 4 + i) for i in range(4)]
    return [(pr - 2 + i, i) for i in range(5)]


def build_k2(do_attn=True, do_gdn=True, gdn_heads=(0, 1, 2, 3), dbg=None):
    P = Prog()
    nc = P.nc
    fm = nc.dram_tensor("fm", [19 * 128, TT], F32, kind="ExternalInput").ap()
    tm = nc.dram_tensor("tm", [TT, 912], F32, kind="ExternalInput").ap()
    gains = nc.dram_tensor("gains", [128, 4], F32, kind="ExternalInput").ap()
    convw = nc.dram_tensor("convw", [128, 12, 5], F32, kind="ExternalInput").ap()
    rowp = nc.dram_tensor("rowp", [1, RP_END], F32, kind="ExternalInput").ap()
    cosT = nc.dram_tensor("cosT", [128, L], F32, kind="ExternalInput").ap()
    sinT = nc.dram_tensor("sinT", [128, L], F32, kind="ExternalInput").ap()
    constd = nc.dram_tensor("consts", [128, C_END], F32, kind="ExternalInput").ap()
    nabd = nc.dram_tensor("nab", [128, 2, 21, 128], F32, kind="ExternalInput").ap()
    o_tm = nc.dram_tensor("o_tm", [TT, 1024], F32, kind="ExternalOutput").ap()

    C = nc.alloc_sbuf_tensor("C", [128, C_XI], F32)
    identb = nc.alloc_sbuf_tensor("identb", [128, 128], BF16)
    rp = nc.alloc_sbuf_tensor("rp", [128, RP_END], F32)
    gn = nc.alloc_sbuf_tensor("gn", [128, 8], F32)
    eps_t = nc.alloc_sbuf_tensor("eps_t", [128, 2], F32)
    ps = [nc.alloc_psum_tensor("ps%d" % i, [128, 512], F32) for i in range(8)]
    P.dma('sp', C[:], constd[:, 0:C_XI], writes=['C'])
    P.dma('sp', rp[:], rowp.partition_broadcast(128), writes=['rp'])
    P.dma('sp', gn[:, 0:4], gains, writes=['gn'])
    P.memset('dve', eps_t[:, 0:1], EPS, writes=['eps_t'])
    P.memset('dve', eps_t[:, 1:2], float(np.log(HD ** -0.5)), writes=['eps_t'])
    P.copy('dve', identb[:], C[:, C_ID:C_ID + 128], reads=['C'], writes=['identb'])
    P.ts('dve', gn[:, 4:5], gn[:, 0:1], HD ** -0.5, None, ALU.mult, None, reads=['gn'], writes=['gn'])
    P.ts('dve', gn[:, 5:6], gn[:, 2:3], HD ** -0.5, None, ALU.mult, None, reads=['gn'], writes=['gn'])
    ident = C[:, C_ID:C_ID + 128]
    ones = C[:, C_ONES:C_ONES + 128]

    io = K2IO(fm=fm, tm=tm, o_tm=o_tm)
    if do_attn:
        emit_attention(P, io, cosT, sinT, nabd, C, rp, gn, eps_t, ps, ident, ones, identb)
        P.barrier()
    if do_gdn:
        emit_gdn(P, io, convw, C, rp, eps_t, ps, ident, ones, gdn_heads, dbg, xinit=constd[:, C_XI:C_END])
    return P


class K2IO:
    def __init__(self, fm=None, tm=None, o_tm=None, PT=None, PM=None, O=None, g=0, sfx=''):
        self.fm, self.tm, self.o_tm, self.PT, self.PM, self.O, self.g, self.sfx = fm, tm, o_tm, PT, PM, O, g, sfx

    def fm_rows(self, tile):
        if self.fm is not None:
            return self.fm[tile * 128:(tile + 1) * 128, :]
        g = self.g
        if tile < 2:
            r = 256 * g + 128 * tile
        elif tile == 2:
            r = 512 + 128 * g
        elif tile < 5:
            r = 3840 + 256 * g + 128 * (tile - 3)
        elif tile < 7:
            r = 4352 + 256 * g + 128 * (tile - 5)
        elif tile < 11:
            r = 768 + 512 * g + 128 * (tile - 7)
        elif tile < 15:
            r = 1792 + 512 * g + 128 * (tile - 11)
        else:
            r = 2816 + 512 * g + 128 * (tile - 15)
        return self.PT[r:r + 128, :]

    def tm_cols(self, what, i=0):
        g = self.g
        if self.tm is not None:
            v = self.tm.rearrange("(t p) c -> p t c", p=128)
            if what == 'sv':
                return v[:, :, 0:128]
            if what == 'nv':
                return v[:, :, 128 + i * 128:256 + i * 128]
            if what == 'z':
                return v[:, :, 384 + i * 128:512 + i * 128]
            return v[:, :, 896 + 4 * i:900 + 4 * i]
        v = self.PM.rearrange("(t p) c -> p t c", p=128)
        if what == 'sv':
            return v[:, :, 128 * g:128 * g + 128]
        if what == 'nv':
            c = 1312 + 256 * g + 128 * i
            return v[:, :, c:c + 128]
        if what == 'z':
            c = 256 + 512 * g + 128 * i
            return v[:, :, c:c + 128]
        c = 1280 + 4 * g + i
        return v[:, :, c:c + 32:8]

    def out_rows(self, t, what):
        if self.o_tm is not None:
            c = 0 if what == 'swa' else 768
            return self.o_tm[t * 128:(t + 1) * 128, c:c + 256]
        c = 256 * self.g if what == 'swa' else 1536 + 256 * self.g
        return self.O[t * 128:(t + 1) * 128, c:c + 256]

    def out_gdn(self, hh):
        if self.o_tm is not None:
            return self.o_tm.rearrange("(t p) c -> p t c", p=128)[:, :, 256 + hh * 128:384 + hh * 128]
        c = 512 + 512 * self.g + 128 * hh
        return self.O.rearrange("(t p) c -> p t c", p=128)[:, :, c:c + 128]


def emit_attention(P, io, cosT, sinT, nabd, C, rp, gn, eps_t, ps, ident, ones, identb):
    nc = P.nc
    with ExitStack() as stack:
        def T(name, shape, dt):
            return stack.enter_context(nc.sbuf_tensor(name + io.sfx, shape, dt))
        qa = T("qa", [128, NTT, 2, 128], BF16)
        ka = T("ka", [128, NTT, 128], BF16)
        qn = T("qn", [128, 2, NTT, 128], BF16)
        kn = T("kn", [128, 2, NTT, 128], BF16)
        va = T("va", [128, NTT, 132], BF16)
        vn = T("vn", [128, NTT, 2, 132], BF16)
        nab = T("nab_sb", [128, 2, 21, 128], BF16)
        mkb = T("mkb", [128, 512], BF16)
        bulk = [T("bulk%d" % i, [128, TT], F32) for i in range(2)]
        sq = T("sq", [128, TT], F32)
        rs = T("rs", [128, TT], F32)
        w0b = [T("w0b%d" % i, [128, 512], F32) for i in range(2)]
        w3b = [T("w3b%d" % i, [128, 512], F32) for i in range(2)]
        csb = [T("csb%d" % i, [128, 512], F32) for i in range(2)]
        snb = [T("snb%d" % i, [128, 512], F32) for i in range(2)]
        pt0 = T("pt0", [128, 256], BF16)
        pt1 = T("pt1", [128, 256], BF16)
        pt2 = T("pt2", [128, 256], BF16)
        ost0 = T("ost0", [128, 256], F32)
        ost1 = T("ost1", [128, 256], F32)
        den = T("den", [128, 8], F32)
        es = T("es", [128, 2], F32)
        pts = [pt0, pt1, pt2]
        osts = [ost0, ost1]
        P.dma("pool", nab[:], nabd, writes=["nab"])
        P.copy('dve', mkb[:], C[:, C_MP:C_MP + 512], reads=['C'], writes=['mkb'])
        P.act(es[:], rp[:, RP_SINK:RP_SINK + 2], AF.Exp, reads=['rp'], writes=['es'])
        P.dma('pool', va[:, :, 0:128], io.tm_cols('sv'), reads=['PM'], writes=['va'])
        for h in range(2):
            P.dma('pool', vn[:, :, h, 0:128], io.tm_cols('nv', h), reads=['PM'], writes=['vn'])
        P.memset('dve', va[:, :, 128:129], 1.0, writes=['va'])
        P.memset('dve', vn[:, :, :, 128:129], 1.0, writes=['vn'])

        nb = 0

        def qknorm(row_tile, dst_fn, gcol, rope, dkey):
            nonlocal nb
            raw = bulk[nb % 2]
            rk = 'bulk%d' % (nb % 2)
            nb += 1
            P.dma('sp', raw[:], io.fm_rows(row_tile), reads=['PT'], writes=[rk])
            P.act(sq[:], raw[:], AF.Square, reads=[rk], writes=['sq'])
            for bi, (t0, tn) in enumerate(BLK):
                pb = 6 + bi % 2
                P.mm(ps[pb][:, 0:tn], ones, sq[:, t0:t0 + tn], True, True, reads=['C', 'sq'], writes=['ps%d' % pb])
                P.act(rs[:, t0:t0 + tn], ps[pb][:, 0:tn], AF.Ln, reads=['ps%d' % pb, 'eps_t'], writes=['rs'],
                      bias=eps_t[:, 0:1], scale=1.0 / HD)
            P.act(rs[:], rs[:], AF.Exp, reads=['rs'], writes=['rs'], scale=-0.5)
            v3 = lambda ap: ap.rearrange("p (a b) -> p a b", b=128)
            if not rope:
                P.stt(dst_fn(0, NTT), v3(raw[:]), gcol, v3(rs[:]), ALU.mult, ALU.mult, reads=[rk, 'gn', 'rs'], writes=[dkey])
                return
            P.stt(dst_fn(0, 2), v3(raw[:, 0:LC]), gcol, v3(rs[:, 0:LC]), ALU.mult, ALU.mult, reads=[rk, 'gn', 'rs'], writes=[dkey])
            P.stt(sq[:, LC:TT], raw[:, LC:TT], gcol, rs[:, LC:TT], ALU.mult, ALU.mult, reads=[rk, 'gn', 'rs'], writes=['sq'])
            for bi, (t0, tn) in enumerate(BLK[1:]):
                pb = 6 + bi % 2
                k = bi % 2
                P.mm(ps[pb][:, 0:tn], C[:, C_RM:C_RM + 128], sq[:, t0:t0 + tn], True, True, reads=['C', 'sq'], writes=['ps%d' % pb])
                P.dma('sp', csb[k][:, 0:tn], cosT[:, t0 - LC:t0 - LC + tn], writes=['cs%d' % k])
                P.dma('sp', snb[k][:, 0:tn], sinT[:, t0 - LC:t0 - LC + tn], writes=['sn%d' % k])
                P.tt('pool', w3b[k][:, 0:tn], sq[:, t0:t0 + tn], csb[k][:, 0:tn], ALU.mult, reads=['sq', 'cs%d' % k], writes=['w3_%d' % k])
                P.tt('dve', w0b[k][:, 0:tn], ps[pb][:, 0:tn], snb[k][:, 0:tn], ALU.mult, reads=['ps%d' % pb, 'sn%d' % k], writes=['w0_%d' % k])
                P.tt('dve', dst_fn(t0 // 128, tn // 128), v3(w3b[k][:, 0:tn]), v3(w0b[k][:, 0:tn]), ALU.add,
                     reads=['w3_%d' % k, 'w0_%d' % k], writes=[dkey])

        qknorm(0, lambda a, n: qa[:, a:a + n, 0, :], gn[:, 4:5], True, 'qa')
        qknorm(1, lambda a, n: qa[:, a:a + n, 1, :], gn[:, 4:5], True, 'qa')
        qknorm(2, lambda a, n: ka[:, a:a + n, :], gn[:, 1:2], True, 'ka')
        for h in range(2):
            qknorm(3 + h, lambda a, n, h=h: qn[:, h, a:a + n, :], gn[:, 5:6], False, 'qn')
            qknorm(5 + h, lambda a, n, h=h: kn[:, h, a:a + n, :], gn[:, 3:4], False, 'kn')

        cnt = {'s': 0, 'o': 0, 'st': 0}

        def attn(q_rhs, qkey, nq, chunks, sink_cols, out_aps, out_key):
            pos = []
            for i in range(nq):
                b = 2 + (cnt['o'] % 4)
                cnt['o'] += 1
                pos.append((ps[b], 'ps%d' % b))
            nch = len(chunks)
            for ci, (kT, kkey, v_aps, vkey, bias, bkey) in enumerate(chunks):
                sb = cnt['s'] % 2
                cnt['s'] += 1
                pss, pssk = ps[sb], 'ps%d' % sb
                P.mm(pss[:, 0:nq * 128], kT, q_rhs, True, bias is None, reads=[kkey, qkey], writes=[pssk])
                if bias is not None:
                    P.mm(pss[:, 0:nq * 128], identb[:], bias, False, True, reads=['identb', bkey], writes=[pssk])
                pt = pts[cnt['st'] % 3]
                ptk = 'pt%d' % (cnt['st'] % 3)
                cnt['st'] += 1
                P.act(pt[:, 0:nq * 128], pss[:, 0:nq * 128], AF.Exp, reads=[pssk], writes=[ptk])
                for i in range(nq):
                    P.mm(pos[i][0][:, 0:129], pt[:, i * 128:(i + 1) * 128], v_aps[i], ci == 0, ci == nch - 1,
                         reads=[ptk, vkey], writes=[pos[i][1]])
            for i in range(nq):
                po, pok = pos[i]
                if sink_cols is not None:
                    P.tt('dve', den[:, i:i + 1], po[:, 128:129], sink_cols[i], ALU.add, reads=[pok, 'es'], writes=['den%d' % i])
                    P.op('dve', lambda g, i=i: g.reciprocal(den[:, 4 + i:5 + i], den[:, i:i + 1]), reads=['den%d' % i], writes=['den%d' % i])
                else:
                    P.op('dve', lambda g, i=i, po=po: g.reciprocal(den[:, 4 + i:5 + i], po[:, 128:129]), reads=[pok], writes=['den%d' % i])
                P.op('act', lambda g, i=i, po=po: g.activation(out_aps[i], po[:, 0:128], AF.Copy, scale=den[:, 4 + i:5 + i]),
                     reads=[pok, 'den%d' % i], writes=[out_key])

        ost_n = 0
        for t in range(NTT):
            ost = osts[ost_n % 2]
            ostk = 'ost%d' % (ost_n % 2)
            ost_n += 1
            chunks = []
            if t >= 2:
                n = t - 2
                if n >= 1:
                    chunks.append((ka[:, t - 1, :], 'ka', [va[:, t - 1, 0:129]] * 2, 'va', mkb[:, 0:256], 'mkb'))
                chunks.append((ka[:, t, :], 'ka', [va[:, t, 0:129]] * 2, 'va', None, None))
                if n <= 30:
                    chunks.append((ka[:, t + 1, :], 'ka', [va[:, t + 1, 0:129]] * 2, 'va', mkb[:, 256:512], 'mkb'))
            for ct in range(2):
                chunks.append((ka[:, ct, :], 'ka', [va[:, ct, 0:129]] * 2, 'va', None, None))
            attn(qa[:, t, :, :], 'qa', 2, chunks, [es[:, 0:1], es[:, 1:2]], [ost[:, 0:128], ost[:, 128:256]], ostk)
            P.dma('sp', io.out_rows(t, 'swa'), ost[:], reads=[ostk], writes=['O'])
        for t in range(NTT):
            ost = osts[ost_n % 2]
            ostk = 'ost%d' % (ost_n % 2)
            ost_n += 1
            for h in range(2):
                chunks = []
                if t >= 2:
                    for (kt, ti) in na_chunks(t - 2):
                        chunks.append((kn[:, h, 2 + kt, :], 'kn', [vn[:, 2 + kt, h, 0:129]], 'vn', nab[:, h, ti, :], 'nab'))
                for ct in range(2):
                    chunks.append((kn[:, h, ct, :], 'kn', [vn[:, ct, h, 0:129]], 'vn', None, None))
                attn(qn[:, h, t, :], 'qn', 1, chunks, None, [ost[:, h * 128:(h + 1) * 128]], ostk)
            P.dma('sp', io.out_rows(t, 'na'), ost[:], reads=[ostk], writes=['O'])


def emit_gdn(P, io, convw, C, rp, eps_t, ps, ident, ones, gdn_heads, dbg=None, xinit=None):
    nc = P.nc
    NCH = 2 * NTT
    sfx = io.sfx
    Ld = nc.dram_tensor("Ld" + sfx, [2 * NCH, 64, 64], F32, kind="Internal").ap()
    Td = nc.dram_tensor("Td" + sfx, [2 * NCH, 64, 64], F32, kind="Internal").ap()
    Ldf = Ld.rearrange("q i j -> q (i j)")
    Tdf = Td.rearrange("q i j -> q (i j)")
    TriD = C[:, C_TRI:C_TRI + 256].rearrange("p (d i) -> p d i", d=2)
    UD = C[:, C_U:C_U + 256].rearrange("p (d i) -> p d i", d=2)
    SEL = C[:, C_SEL:C_SEL + 256].rearrange("p (c q) -> p c q", c=2)
    with ExitStack() as stack:
        def T(name, shape, dt=F32):
            return stack.enter_context(nc.sbuf_tensor(name + sfx, shape, dt))
        S = [T("S%d" % i, [128, NTT, 128]) for i in range(6)]
        SL = T("SL", [128, NTT, 2, 128])
        SI = T("SI", [128, NTT, 2, 128])
        cw = T("cw", [128, 12, 5])
        gw = T("gw", [128, 1024])
        w0 = gw[:, 0:512]
        w1 = gw[:, 512:1024]
        abt = T("abt", [128, NTT, 4])
        la = T("la", [128, 2, NTT])
        beta = T("beta", [128, 2, NTT])
        beg = T("beg", [128, 2, NTT])
        nA = T("nA", [128, 2])
        gx = T("gx", [128, 272])
        A_all = [T("A_all%d" % i, [128, 2, 128]) for i in range(2)]
        dec = [T("dec%d" % i, [128, 512]) for i in range(2)]
        t1 = [T("t1_%d" % i, [128, 2, 128]) for i in range(2)]
        t2 = [T("t2_%d" % i, [128, 2, 128]) for i in range(2)]
        tmpo = gw
        vb = [T("vb%d" % d, [128, 128]) for d in range(2)]
        kbg = [T("kbg%d" % d, [128, 128]) for d in range(2)]
        kdec = [[T("kdec%d_%d" % (d, i), [128, 128]) for i in range(2)] for d in range(2)]
        u_sb = [[T("u_sb%d_%d" % (d, i), [128, 128]) for i in range(2)] for d in range(2)]
        wT_sb = [[T("wT_sb%d_%d" % (d, i), [128, 128]) for i in range(2)] for d in range(2)]
        vnew = [T("vnew%d" % d, [128, 128]) for d in range(2)]
        Sst = [T("Sst%d" % d, [128, 128]) for d in range(2)]
        tmpq = [T("tmpq%d" % d, [128, 128]) for d in range(2)]
        o1 = [T("o1_%d" % d, [128, 128]) for d in range(2)]
        ssq = T("ssq", [128, 2, NTT])

        def SK(s, p):
            return 'S%d:%d' % (s, p)

        def SKall(s):
            return [SK(s, p) for p in range(NTT)]

        def flat(t):
            return t[:].rearrange("p t c -> p (t c)")

        P.dma('sp', cw[:], convw, writes=['cw'])
        for hh in gdn_heads:
            rawslot = (0, 4, 5)
            for ti in range(3):
                P.dma('sp', flat(S[rawslot[ti]]), io.fm_rows(7 + 4 * ti + hh), reads=['PT'], writes=SKall(rawslot[ti]))
            for ti in range(3):
                Xf = flat(S[rawslot[ti]])
                Yf = flat(S[1 + ti])
                ci = ti * 4 + hh
                P.ts('dve', Yf, Xf, cw[:, ci, 2:3], None, ALU.mult, None, reads=SKall(rawslot[ti]) + ['cw'], writes=SKall(1 + ti))
                for tap in (0, 1, 3, 4):
                    off = tap - 2
                    for (s0, s1) in ((0, LC), (LC, TT)):
                        lo = s0 + max(0, -off)
                        hi = s1 - max(0, off)
                        P.stt(Yf[:, lo:hi], Xf[:, lo + off:hi + off], cw[:, ci, tap:tap + 1], Yf[:, lo:hi], ALU.mult, ALU.add,
                              reads=SKall(rawslot[ti]) + ['cw'] + SKall(1 + ti), writes=SKall(1 + ti))
                P.act(Yf, Yf, AF.Silu, reads=SKall(1 + ti), writes=SKall(1 + ti))
            SLf = SL[:].rearrange("p t d c -> p (t d c)")[:, 0:TT]
            SIf = SI[:].rearrange("p t d c -> p (t d c)")[:, 0:TT]
            SLk = ['SL:%d' % p for p in range(NTT)]
            SIk = ['SI:%d' % p for p in range(NTT)]
            for ti in range(2):
                Yf = flat(S[1 + ti])
                P.act(SLf, Yf, AF.Square, reads=SKall(1 + ti), writes=SLk)
                for bi, (t0, tn) in enumerate(BLK):
                    pb = 6 + bi % 2
                    P.mm(ps[pb][:, 0:tn], ones, SLf[:, t0:t0 + tn], True, True, reads=['C'] + SLk, writes=['ps%d' % pb])
                    P.act(SIf[:, t0:t0 + tn], ps[pb][:, 0:tn], AF.Ln, reads=['ps%d' % pb, 'eps_t'], writes=SIk, bias=eps_t[:, 0:1], scale=1.0)
                if ti == 0:
                    P.act(SIf, SIf, AF.Exp, reads=SIk + ['eps_t'], writes=SIk, scale=-0.5, bias=eps_t[:, 1:2])
                else:
                    P.act(SIf, SIf, AF.Exp, reads=SIk, writes=SIk, scale=-0.5)
                P.tt('dve', Yf, Yf, SIf, ALU.mult, reads=SKall(1 + ti) + SIk, writes=SKall(1 + ti))
            P.marks.append(('A', dict(P.cnt)))
            if dbg == 'A':
                return
            ntr = 0
            for (src, dst) in ((2, 4), (3, 5)):
                for t0 in range(0, NTT, 4):
                    n = min(4, NTT - t0)
                    pb = 5 + (ntr % 2)
                    ntr += 1
                    for a in range(n):
                        P.tr(ps[pb][:, a * 128:(a + 1) * 128], S[src][:, t0 + a, :], ident, reads=[SK(src, t0 + a), 'C'], writes=['ps%d' % pb])
                    P.copy('act' if ntr % 2 == 0 else 'dve', S[dst][:, t0:t0 + n, :], ps[pb][:, 0:n * 128].rearrange("p (a b) -> p a b", b=128),
                           reads=['ps%d' % pb], writes=[SK(dst, t0 + a) for a in range(n)])
            P.marks.append(('B', dict(P.cnt)))
            if dbg == 'B':
                return
            if io.tm is not None:
                P.dma('sp', abt[:], io.tm_cols('ab', hh), reads=['PM'], writes=['abt'])
            else:
                with nc.allow_non_contiguous_dma(reason='tiny gate columns'):
                    for kk in range(4):
                        cc = 1280 + 8 * kk + 4 * io.g + hh
                        P.dma('sp', abt[:, :, kk:kk + 1], io.PM.rearrange("(t p) c -> p t c", p=128)[:, :, cc:cc + 1],
                              reads=['PM'], writes=['abt'])
            for d in range(2):
                P.ts('dve', la[:, d, :], abt[:, :, d], rp[:, RP_DTB + d * 4 + hh:RP_DTB + d * 4 + hh + 1], None, ALU.add, None,
                     reads=['abt', 'rp'], writes=['la'])
                P.act(nA[:, d:d + 1], rp[:, RP_ALOG + d * 4 + hh:RP_ALOG + d * 4 + hh + 1], AF.Exp, reads=['rp'], writes=['nA'])
                P.act(beta[:, d, :], abt[:, :, 2 + d], AF.Exp, reads=['abt'], writes=['beta'], scale=-1.0)
            P.act(la[:], la[:], AF.Exp, reads=['la'], writes=['la'])
            P.act(la[:], la[:], AF.Ln, reads=['la'], writes=['la'], bias=1.0, scale=1.0)
            for d in range(2):
                P.ts('dve', la[:, d, :], la[:, d, :], nA[:, d:d + 1], -1.0, ALU.mult, ALU.mult, reads=['la', 'nA'], writes=['la'])
            P.ts('dve', beta[:], beta[:], 1.0, None, ALU.add, None, reads=['beta'], writes=['beta'])
            P.op('dve', lambda g: g.reciprocal(beta[:], beta[:]), reads=['beta'], writes=['beta'])
            for d in range(2):
                P.mm(ps[7][:, d * 34:(d + 1) * 34], TriD[:, d, :], la[:, d, :], True, True, reads=['C', 'la'], writes=['ps7'])
                P.mm(ps[7][:, 68 + d * 34:68 + (d + 1) * 34], UD[:, d, :], la[:, d, :], True, True, reads=['C', 'la'], writes=['ps7'])
                for c in range(2):
                    o_ = 136 + (d * 2 + c) * 34
                    P.mm(ps[7][:, o_:o_ + 34], SEL[:, c, :], la[:, d, :], True, True, reads=['C', 'la'], writes=['ps7'])
            P.act(gx[:], ps[7][:, 0:272], AF.Exp, reads=['ps7'], writes=['gx'])
            P.tt('dve', beg[:], beta[:], gx[:, 0:68].rearrange("p (d t) -> p d t", d=2), ALU.mult, reads=['beta', 'gx'], writes=['beg'])

            def eg(d, p, r0=0, rn=128):
                return gx[r0:r0 + rn, d * 34 + p:d * 34 + p + 1]

            def ekd(d, p):
                return gx[:, 68 + d * 34 + p:68 + d * 34 + p + 1]

            def egl(d, c, p):
                o_ = 136 + (d * 2 + c) * 34 + p
                return gx[:, o_:o_ + 1]

            P.marks.append(('C', dict(P.cnt)))
            if dbg == 'C':
                return
            for p in range(NTT):
                b = p % 2
                pg, pgk = ps[0 + b], 'ps%d' % b
                pd, pdk = ps[2 + b], 'ps%d' % (2 + b)
                P.mm(pg[:, 0:128], S[2][:, p, :], S[2][:, p, :], True, True, reads=[SK(2, p)], writes=[pgk])
                P.mm(pg[:, 128:256], S[2][:, p, :], S[1][:, p, :], True, True, reads=[SK(2, p), SK(1, p)], writes=[pgk])
                P.tt('pool', A_all[b][:], la[:, :, p:p + 1].broadcast_to([128, 2, 128]), TriD, ALU.mult, reads=['la', 'C'], writes=['A_all%d' % b])
                for d in range(2):
                    P.mm(pd[:, d * 128:(d + 1) * 128], UD[:, d, :], A_all[b][:, d, :], True, True, reads=['C', 'A_all%d' % b], writes=[pdk])
                    P.mm(pd[:, 256 + d * 128:256 + (d + 1) * 128], A_all[b][:, d, :], UD[:, d, :], True, True, reads=['C', 'A_all%d' % b], writes=[pdk])
                P.act(dec[b][:], pd[:], AF.Exp, reads=[pdk], writes=['dec%d' % b])
                P.tt('dve', t1[b][:], pg[:, 128:256].unsqueeze(1).broadcast_to([128, 2, 128]), TriD, ALU.mult, reads=[pgk, 'C'], writes=['t1_%d' % b])
                P.tt('dve', SI[:, p, :, :], t1[b][:], dec[b][:, 0:256].rearrange("p (d i) -> p d i", d=2), ALU.mult,
                     reads=['t1_%d' % b, 'dec%d' % b], writes=['SI:%d' % p])
                P.tt('dve', t2[b][:], pg[:, 0:128].unsqueeze(1).broadcast_to([128, 2, 128]), UD, ALU.mult, reads=[pgk, 'C'], writes=['t2_%d' % b])
                for d in range(2):
                    P.stt(SL[:, p, d, :], t2[b][:, d, :], beta[:, d, p:p + 1], dec[b][:, 256 + d * 128:384 + d * 128], ALU.mult, ALU.mult,
                          reads=['t2_%d' % b, 'beta', 'dec%d' % b], writes=['SL:%d' % p])
            P.marks.append(('D', dict(P.cnt)))
            if dbg == 'D':
                return
            SLall = ['SL:%d' % p for p in range(NTT)]
            Ldv = Ld.rearrange("(d p c) i j -> d c i p j", d=2, c=2)
            Tdv = Td.rearrange("(d p c) j i -> d c j p i", d=2, c=2)
            for d in range(2):
                for c in range(2):
                    P.dma('sp', Ldv[d, c], SL[c * 64:(c + 1) * 64, :, d, c * 64:(c + 1) * 64], reads=SLall, writes=['Ld'])
            Lp = flat(S[0])[:, 0:4096].rearrange("p (i j) -> p i j", j=64)
            Xp = flat(S[3])[:, 0:4096].rearrange("p (m i) -> p m i", i=64)
            Xpf = flat(S[3])
            S0f = flat(S[0])
            S2f = flat(S[2])
            for (c0, ncz) in ((0, 64),):
                P.dma('sp', S0f[0:ncz, 0:4096], Ldf[c0:c0 + ncz, :], reads=['Ld'], writes=SKall(0))
                P.dma('sp', S2f[64:64 + ncz, 0:4096], Ldf[NCH + c0:NCH + c0 + ncz, :], reads=['Ld'], writes=SKall(2))
                P.copy('pool', S0f[64:64 + ncz, 0:4096], S2f[64:64 + ncz, 0:4096][:, ::-1], reads=SKall(2), writes=SKall(0))
                P.memset('dve', Xpf[:, 0:4096], 0.0, writes=SKall(3))
                P.memset('dve', Xpf[:, 0:4095].rearrange("p (m s) -> p m s", s=65)[:, :, 0], 1.0, writes=SKall(3))
                P.memset('dve', Xpf[:, 4095:4096], 1.0, writes=SKall(3))
                for j in range(63):
                    n_i = 63 - j
                    n_m = j + 1
                    a_bc = Lp[:, j + 1:64, j].unsqueeze(1).broadcast_to([128, n_m, n_i])
                    b_bc = Xp[:, 0:n_m, j].unsqueeze(2).broadcast_to([128, n_m, n_i])
                    tv = tmpo[:, 0:n_m * n_i].rearrange("p (m i) -> p m i", i=n_i)
                    blk = Xp[:, 0:n_m, j + 1:64]
                    P.tt('dve', tv, a_bc, b_bc, ALU.mult, reads=['S0:0', 'S3:0'], writes=['gw0', 'gw1'])
                    P.tt('dve', blk, blk, tv, ALU.subtract, reads=['gw0', 'gw1', 'S3:0'], writes=['S3:0'])
                P.dma('sp', Tdf[c0:c0 + ncz, :], Xpf[0:ncz, 0:4096], reads=SKall(3), writes=['Td'])
                P.copy('pool', S2f[64:64 + ncz, 0:4096], Xpf[64:64 + ncz, 0:4096][:, ::-1], reads=SKall(3), writes=SKall(2))
                P.dma('sp', Tdf[NCH + c0:NCH + c0 + ncz, :], S2f[64:64 + ncz, 0:4096], reads=SKall(2), writes=['Td'])
            c0, ncz = 64, NCH - 64
            X2 = Xpf[:, 0:256].rearrange("p (m i) -> p m i", i=64)
            for q in range(ncz):
                P.dma('sp', S0f[q * 16:(q + 1) * 16, 0:4096], Ldf[c0 + q:c0 + q + 1, :].partition_broadcast(16), reads=['Ld'], writes=SKall(0))
                P.dma('sp', S2f[64 + q * 16:64 + (q + 1) * 16, 0:4096], Ldf[NCH + c0 + q:NCH + c0 + q + 1, :].partition_broadcast(16),
                      reads=['Ld'], writes=SKall(2))
            P.copy('pool', S0f[64:128, 0:4096], S2f[64:128, 0:4096][:, ::-1], reads=SKall(2), writes=SKall(0))
            P.dma('sp', Xpf[:, 0:256], xinit, writes=SKall(3))
            for j in range(63):
                n_i = 63 - j
                a_bc = Lp[:, j + 1:64, j].unsqueeze(1).broadcast_to([128, 4, n_i])
                b_bc = X2[:, :, j].unsqueeze(2).broadcast_to([128, 4, n_i])
                tv = tmpo[:, 0:4 * n_i].rearrange("p (m i) -> p m i", i=n_i)
                blk = X2[:, :, j + 1:64]
                P.tt('dve', tv, a_bc, b_bc, ALU.mult, reads=['S0:0', 'S3:0'], writes=['gw0', 'gw1'])
                P.tt('dve', blk, blk, tv, ALU.subtract, reads=['gw0', 'gw1', 'S3:0'], writes=['S3:0'])
            P.dma('sp', Tdf[c0:c0 + ncz, :].rearrange("q (a b) -> (q a) b", a=16), Xpf[0:64, 0:256], reads=SKall(3), writes=['Td'])
            P.copy('pool', S2f[64:128, 0:256].rearrange("p (m i) -> p m i", i=64), X2[64:128, :, ::-1], reads=SKall(3), writes=SKall(2))
            P.dma('sp', Tdf[NCH + c0:NCH + c0 + ncz, :].rearrange("q (a b) -> (q a) b", a=16), S2f[64:128, 0:256], reads=SKall(2), writes=['Td'])
            P.marks.append(('E', dict(P.cnt)))
            if dbg == 'E':
                return
            for d in range(2):
                for c in range(2):
                    P.dma('sp', SL[c * 64:(c + 1) * 64, :, d, c * 64:(c + 1) * 64], Tdv[d, c], reads=['Td'], writes=SLall)
            P.marks.append(('F', dict(P.cnt)))
            if dbg == 'F':
                return
            P.dma('sp', S[2][:], io.tm_cols('z', hh), reads=['PM'], writes=SKall(2))
            P.memset('pool', flat(S[3]), 0.0, writes=SKall(3))
            for d in range(2):
                P.memset('pool', Sst[d][:], 0.0, writes=['Sst%d' % d])
                P.memset('pool', vnew[d][:], 0.0, writes=['vnew%d' % d])
            order = [list(range(0, 4)) + list(range(4, NCH)), list(range(3, -1, -1)) + list(range(NCH - 1, 3, -1))]

            def prep(d, p, sl):
                ppd, ppk = ps[0 + d], 'ps%d' % d
                P.ts('pool', vb[d][:], S[5][:, p, :], beta[:, d, p:p + 1], None, ALU.mult, None, reads=[SK(5, p), 'beta'], writes=['vb%d' % d])
                P.ts('pool', kbg[d][:], S[4][:, p, :], beg[:, d, p:p + 1], None, ALU.mult, None, reads=[SK(4, p), 'beg'], writes=['kbg%d' % d])
                P.ts('pool', kdec[d][sl][:], S[4][:, p, :], ekd(d, p), None, ALU.mult, None, reads=[SK(4, p), 'gx'], writes=['kdec%d_%d' % (d, sl)])
                P.mm(ppd[:, 0:128], SL[:, p, d, :], vb[d][:], True, True, reads=['SL:%d' % p, 'vb%d' % d], writes=[ppk])
                P.mm(ppd[:, 128:256], kbg[d][:], SL[:, p, d, :], True, True, reads=['SL:%d' % p, 'kbg%d' % d], writes=[ppk])
                P.copy('act', u_sb[d][sl][:], ppd[:, 0:128], reads=[ppk], writes=['u_sb%d_%d' % (d, sl)])
                P.copy('act', wT_sb[d][sl][:], ppd[:, 128:256], reads=[ppk], writes=['wT_sb%d_%d' % (d, sl)])

            for d in range(2):
                prep(d, order[d][0] // 2, 0)
            for s in range(NCH):
                sl = (s // 2) % 2
                info = []
                for d in range(2):
                    c = order[d][s]
                    info.append((c // 2, c % 2, (c % 2) * 64))
                for d in range(2):
                    p, hb, r0 = info[d]
                    psa, pak = ps[2 + d], 'ps%d' % (2 + d)
                    psb, pbk = ps[4 + 2 * d + (s % 2)], 'ps%d' % (4 + 2 * d + (s % 2))
                    P.mm(psa[:, 0:128], wT_sb[d][sl][:], Sst[d][:], True, True, reads=['wT_sb%d_%d' % (d, sl), 'Sst%d' % d], writes=[pak])
                    P.mm(psb[:, 0:128], S[1][:, p, :], Sst[d][:], True, True, reads=[SK(1, p), 'Sst%d' % d], writes=[pbk])
                for d in range(2):
                    p, hb, r0 = info[d]
                    psa, pak = ps[2 + d], 'ps%d' % (2 + d)
                    P.tt('dve', vnew[d][r0:r0 + 64, :], u_sb[d][sl][r0:r0 + 64, :], psa[r0:r0 + 64, 0:128], ALU.subtract,
                         reads=['u_sb%d_%d' % (d, sl), pak], writes=['vnew%d' % d])
                for d in range(2):
                    p, hb, r0 = info[d]
                    psa, pak = ps[2 + d], 'ps%d' % (2 + d)
                    psb, pbk = ps[4 + 2 * d + (s % 2)], 'ps%d' % (4 + 2 * d + (s % 2))
                    P.mm(psa[:, 128:256], kdec[d][sl][r0:r0 + 64, :], vnew[d][r0:r0 + 64, :], True, True,
                         reads=['kdec%d_%d' % (d, sl), 'vnew%d' % d], writes=[pak])
                    P.mm(psb[:, 128:256], SI[:, p, d, :], vnew[d][:], True, True, reads=['SI:%d' % p, 'vnew%d' % d], writes=[pbk])
                for d in range(2):
                    p, hb, r0 = info[d]
                    psa, pak = ps[2 + d], 'ps%d' % (2 + d)
                    P.stt(Sst[d][:], Sst[d][:], egl(d, hb, p), psa[:, 128:256], ALU.mult, ALU.add,
                          reads=['Sst%d' % d, 'gx', pak], writes=['Sst%d' % d])
                if s % 2 == 0 and s + 2 < NCH:
                    for d in range(2):
                        prep(d, order[d][s + 2] // 2, 1 - sl)
                for d in range(2):
                    p, hb, r0 = info[d]
                    psb, pbk = ps[4 + 2 * d + (s % 2)], 'ps%d' % (4 + 2 * d + (s % 2))
                    P.op('act', lambda g, d=d, p=p, r0=r0, psb=psb: g.activation(tmpq[d][r0:r0 + 64, :], psb[r0:r0 + 64, 0:128], AF.Copy,
                                                                                   scale=eg(d, p, r0, 64)),
                         reads=[pbk, 'gx'], writes=['tmpq%d' % d])
                    P.tt('dve', o1[d][r0:r0 + 64, :], tmpq[d][r0:r0 + 64, :], psb[r0:r0 + 64, 128:256], ALU.add,
                         reads=['tmpq%d' % d, pbk], writes=['o1_%d' % d])
                    P.tt('pool', S[3][r0:r0 + 64, p, :], S[3][r0:r0 + 64, p, :], o1[d][r0:r0 + 64, :], ALU.add,
                         reads=['o1_%d' % d, SK(3, p)], writes=[SK(3, p)])
            P.marks.append(('H', dict(P.cnt)))
            if dbg == 'H':
                return
            P.act(flat(S[2]), flat(S[2]), AF.Silu, reads=SKall(2), writes=SKall(2))
            for p in range(NTT):
                P.act(w0[:, 0:128], S[3][:, p, :], AF.Square, reads=[SK(3, p)], writes=['gw0', 'ssq'], accum_out=ssq[:, 0, p:p + 1])
            P.act(ssq[:, 1, :], ssq[:, 0, :], AF.Ln, reads=['ssq', 'eps_t'], writes=['ssq'], bias=eps_t[:, 0:1], scale=1.0 / HD)
            P.act(ssq[:, 1, :], ssq[:, 1, :], AF.Exp, reads=['ssq'], writes=['ssq'], scale=-0.5)
            for p in range(NTT):
                P.stt(S[4][:, p, :], S[3][:, p, :], ssq[:, 1, p:p + 1], rp[:, RP_OUTG:RP_OUTG + 128], ALU.mult, ALU.mult,
                      reads=[SK(3, p), 'ssq', 'rp'], writes=[SK(4, p)])
                P.tt('pool', S[4][:, p, :], S[4][:, p, :], S[2][:, p, :], ALU.mult, reads=[SK(4, p), SK(2, p)], writes=[SK(4, p)])
            P.dma('sp', io.out_gdn(hh), S[4][:], reads=SKall(4), writes=['O'])


def rope_tables_T():
    t = np.arange(L)
    row = (t // 64).astype(np.float32)
    col = (t % 64).astype(np.float32)
    nf = HD // 4
    inv = (np.float32(10000.0) ** (-np.arange(nf, dtype=np.float32) / np.float32(nf))).astype(np.float32)
    ar = row[:, None] * inv[None, :]
    ac = col[:, None] * inv[None, :]
    ang = np.concatenate([ar, ar, ac, ac], axis=-1).astype(np.float32)
    return np.ascontiguousarray(np.cos(ang).astype(np.float32).T), np.ascontiguousarray(np.sin(ang).astype(np.float32).T)


def k2_inputs(proj_b, projc_b, g, W, consts, cosT, sinT):
    full = np.concatenate([projc_b, proj_b], axis=0)
    cols = []
    cols += list(range(256 * g, 256 * g + 256))
    cols += list(range(512 + 128 * g, 512 + 128 * g + 128))
    cols += list(range(5152 + 256 * g, 5152 + 256 * g + 256))
    cols += list(range(5664 + 256 * g, 5664 + 256 * g + 256))
    for i in range(3):
        cols += list(range(1024 + 1024 * i + 512 * g, 1024 + 1024 * i + 512 * g + 512))
    fm = np.ascontiguousarray(full[:, cols].T)
    tcols = list(range(768 + 128 * g, 768 + 128 * g + 128)) + list(range(6176 + 256 * g, 6176 + 256 * g + 256)) \
        + list(range(4096 + 512 * g, 4096 + 512 * g + 512))
    for hh in range(4):
        h = 4 * g + hh
        tcols += [5120 + h, 5120 + 8 + h, 5120 + 16 + h, 5120 + 24 + h]
    tm = np.ascontiguousarray(full[:, tcols])
    gains = np.ascontiguousarray(np.stack([W['swa_q_g'], W['swa_k_g'], W['na_q_g'], W['na_k_g']], axis=1).astype(np.float32))
    cw = W['dn_conv_w']
    convw = np.zeros((128, 12, 5), np.float32)
    for i in range(3):
        for hh in range(4):
            c0 = i * 1024 + (4 * g + hh) * 128
            convw[:, i * 4 + hh, :] = cw[:, c0:c0 + 128].T
    rowp = np.zeros((1, RP_END), np.float32)
    rowp[0, RP_OUTG:RP_OUTG + 128] = W['dn_out_g']
    rowp[0, RP_ALOG:RP_ALOG + 8] = W['dn_A_log'][:, 4 * g:4 * g + 4].reshape(-1)
    rowp[0, RP_DTB:RP_DTB + 8] = W['dn_dt_bias'][:, 4 * g:4 * g + 4].reshape(-1)
    rowp[0, RP_SINK:RP_SINK + 2] = W['swa_sink'][2 * g:2 * g + 2]
    nab = np.ascontiguousarray(np.stack([na_tables(W['na_rpb'][2 * g + h]) for h in range(2)], axis=1))
    return {"fm": fm, "tm": tm, "gains": gains, "convw": convw, "rowp": rowp,
            "cosT": cosT, "sinT": sinT, "consts": consts, "nab": nab}


def run_k2(proj, projc, W):
    P = build_k2()
    consts = k2_consts()
    cosT, sinT = rope_tables_T()
    in_maps = [k2_inputs(proj[i // 2], projc[i // 2], i % 2, W, consts, cosT, sinT) for i in range(NCORES)]
    res = run_prog(P, in_maps)
    o_all = np.empty((B, L, D), np.float32)
    oc_all = np.empty((B, LC, D), np.float32)
    for i in range(NCORES):
        b, g = i // 2, i % 2
        o = res[i]["o_tm"]
        for (src, dst, n) in ((0, 256 * g, 256), (256, 512 + 512 * g, 512), (768, 1536 + 256 * g, 256)):
            oc_all[b, :, dst:dst + n] = o[:LC, src:src + n]
            o_all[b, :, dst:dst + n] = o[LC:, src:src + n]
    return o_all, oc_all


def kernel_unfused(x, c, ctx, c_ctx, w_ada, b_ada, norm1_g, norm2_g, w_in, swa_q_g, swa_k_g, swa_sink,
           dn_conv_w, dn_A_log, dn_dt_bias, dn_out_g, na_q_g, na_k_g, na_rpb, w_out, w_gate, w_up, w_down):
    f = lambda a: np.ascontiguousarray(np.asarray(a, dtype=np.float32))
    x, c, ctx, c_ctx = f(x), f(c), f(ctx), f(c_ctx)
    mod = run_k0(c, c_ctx, f(w_ada), f(b_ada))
    xa, xc = x, ctx
    for l in range(2):
        W = {'swa_q_g': f(swa_q_g[l]), 'swa_k_g': f(swa_k_g[l]), 'swa_sink': f(swa_sink[l]), 'dn_conv_w': f(dn_conv_w[l]),
             'dn_A_log': f(dn_A_log[l]), 'dn_dt_bias': f(dn_dt_bias[l]), 'dn_out_g': f(dn_out_g[l]),
             'na_q_g': f(na_q_g[l]), 'na_k_g': f(na_k_g[l]), 'na_rpb': f(na_rpb[l])}
        proj, projc = run_k1(xa, xc, mod[l], f(norm1_g[l]), f(w_in[l]))
        o_all, oc_all = run_k2(proj, projc, W)
        xa, xc = run_k3(xa, xc, o_all, oc_all, mod[l], f(norm2_g[l]), f(w_out[l]), f(w_gate[l]), f(w_up[l]), f(w_down[l]))
    return xa


def _coll(P, kind, in_ap, out_ap, groups, reads=(), writes=()):
    q = 'pool'
    reads = [P.key(k) for k in reads]
    writes = [P.key(k) for k in writes]
    eng = P.E[q]
    k = P.ndma[q]
    r = k % P.NS
    base = 16 * (k // P.NS)
    need = P._deps(q, reads, writes)
    sk_ring = ('ring', q, r)
    if k >= P.NS and P.seen[q].get(sk_ring, 0) < base:
        need[sk_ring] = max(need.get(sk_ring, 0), base)
    for sk, val in need.items():
        eng.wait_ge(P.semobj[sk], val)
        P.seen[q][sk] = val
    ins = eng.collective_compute(kind, ALU.bypass, replica_groups=groups, ins=[in_ap], outs=[out_ap])
    ins.then_inc(P.ring[q][r], 16)
    P.ndma[q] += 1
    P._record((sk_ring, base + 16, 'dma'), reads, writes)
    return ins


Prog.coll = _coll
PAIRS = [[0, 1], [2, 3], [4, 5], [6, 7]]


def urow(hf, t):
    return 128 * hf if t == 16 else LC + 2048 * hf + 128 * t


def emit_k0(P, cT2, w_ada, b_ada, modd, ps):
    nc = P.nc
    with ExitStack() as stack:
        def T(name, shape, dt=F32):
            return stack.enter_context(nc.sbuf_tensor(name, shape, dt))
        cs = T("k0cs", [128, 16, 2])
        w = [T("k0w%d" % i, [128, 16, 512]) for i in range(2)]
        bb = T("k0bb", [2, 6 * D])
        res = T("k0res", [2, 6 * D])
        with nc.allow_non_contiguous_dma(reason="tiny transposed load"):
            P.dma('sp', cs[:], cT2.rearrange("(k p) r -> p k r", p=128), writes=['k0cs'])
        P.act(cs[:], cs[:], AF.Silu, reads=['k0cs'], writes=['k0cs'])
        n = 0
        for l in range(2):
            P.dma('sp', bb[:], b_ada[l:l + 1, :].partition_broadcast(2), writes=['k0bb'])
            wv = w_ada[l].rearrange("(k p) n -> p k n", p=128)
            for s in range(6 * D // 512):
                wt = w[n % 2]
                wk = 'k0w%d' % (n % 2)
                pt, pk = ps[n % 2], 'ps%d' % (n % 2)
                n += 1
                for kh in range(2):
                    P.dma('sp', wt[:, kh * 8:(kh + 1) * 8, :], wv[:, kh * 8:(kh + 1) * 8, s * 512:(s + 1) * 512], writes=[wk])
                for k in range(16):
                    P.mm(pt[0:2, :], cs[:, k, :], wt[:, k, :], k == 0, k == 15, reads=['k0cs', wk], writes=[pk])
                P.tt('dve', res[:, s * 512:(s + 1) * 512], pt[0:2, :], bb[:, s * 512:(s + 1) * 512], ALU.add,
                     reads=[pk, 'k0bb'], writes=['k0res'])
            P.dma('sp', modd[l], res[:], reads=['k0res'], writes=['modd'])
    P.barrier()


def emit_k1(P, xsrc, xkey, modd, l, n1g, w_in_l, PT, PM, hf, ps, ones, eps_t, tag):
    nc = P.nc
    with ExitStack() as stack:
        def T(name, shape, dt=F32):
            return stack.enter_context(nc.sbuf_tensor(name + tag, shape, dt))
        hT = T("hT", [128, 16, RT], BF16)
        V1 = T("V1", [128, 5, 16])
        A1 = T("A1", [128, 2, 16])
        xc = [T("k1xc%d" % i, [128, RT]) for i in range(2)]
        tb = [T("k1tb%d" % i, [128, RT]) for i in range(2)]
        rstd = T("k1rstd", [128, RT])
        slab = [T("k1slab%d" % i, [128, 16, 512], BF16) for i in range(2)]
        stf = [T("k1stf%d" % i, [128, RT]) for i in range(2)]
        stt_ = [T("k1stt%d" % i, [128, 512]) for i in range(2)]
        with nc.allow_non_contiguous_dma(reason="small per-feature vectors, feature-major"):
            P.dma('sp', V1[:, 0:2, :], modd[l, 0, 0:2 * D].rearrange("(r c p) -> p r c", r=2, p=128), reads=['modd'], writes=['V1'])
            P.dma('sp', V1[:, 2:4, :], modd[l, 1, 0:2 * D].rearrange("(r c p) -> p r c", r=2, p=128), reads=['modd'], writes=['V1'])
            P.dma('sp', V1[:, 4, :], n1g.rearrange("(c p) -> p c", p=128), writes=['V1'])
        P.stt(A1[:, 0, :], V1[:, 1, :], 1.0, V1[:, 4, :], ALU.add, ALU.mult, reads=['V1'], writes=['A1'])
        P.stt(A1[:, 1, :], V1[:, 3, :], 1.0, V1[:, 4, :], ALU.add, ALU.mult, reads=['V1'], writes=['A1'])
        nld = 0
        for c in range(16):
            xt, xk = xc[nld % 2], 'k1xc%d' % (nld % 2)
            tt_, tk = tb[nld % 2], 'k1tb%d' % (nld % 2)
            nld += 1
            P.dma('sp', xt[:], xsrc[c * 128:(c + 1) * 128, :], reads=[xkey], writes=[xk])
            P.act(tt_[:], xt[:], AF.Square, reads=[xk], writes=[tk])
            for gi, (t0, tn) in enumerate(TOKG):
                P.mm(ps[gi][:, 0:tn], ones, tt_[:, t0:t0 + tn], c == 0, c == 15, reads=['C', tk], writes=['ps%d' % gi])
        for gi, (t0, tn) in enumerate(TOKG):
            P.act(rstd[:, t0:t0 + tn], ps[gi][:, 0:tn], AF.Ln, reads=['ps%d' % gi, 'eps_t'], writes=['k1rstd'],
                  bias=eps_t[:, 0:1], scale=1.0 / D)
        P.act(rstd[:], rstd[:], AF.Exp, reads=['k1rstd'], writes=['k1rstd'], scale=-0.5)
        for c in range(16):
            xt, xk = xc[nld % 2], 'k1xc%d' % (nld % 2)
            tt_, tk = tb[nld % 2], 'k1tb%d' % (nld % 2)
            nld += 1
            P.dma('sp', xt[:], xsrc[c * 128:(c + 1) * 128, :], reads=[xkey], writes=[xk])
            P.tt('dve', tt_[:], xt[:], rstd[:], ALU.mult, reads=[xk, 'k1rstd'], writes=[tk])
            P.ts('pool', hT[:, c, 0:2048], tt_[:, 0:2048], A1[:, 0, c:c + 1], V1[:, 0, c:c + 1], ALU.mult, ALU.add,
                 reads=[tk, 'A1', 'V1'], writes=['hT'])
            P.ts('pool', hT[:, c, 2048:RT], tt_[:, 2048:RT], A1[:, 1, c:c + 1], V1[:, 2, c:c + 1], ALU.mult, ALU.add,
                 reads=[tk, 'A1', 'V1'], writes=['hT'])
        w_v = w_in_l.rearrange("(k p) n -> p k n", p=128)
        fm_off = 0
        tm_off = 0
        ev = 0
        nstf = 0
        nstt = 0
        psm = ps[4:8]
        for si, (c0, ncol, kind) in enumerate(SLABS):
            sl = slab[si % 2]
            sk = 'k1slab%d' % (si % 2)
            for kh in range(2):
                P.dma('pool', sl[:, kh * 8:(kh + 1) * 8, 0:ncol], w_v[:, kh * 8:(kh + 1) * 8, c0:c0 + ncol], writes=[sk])
            if kind == 'F':
                for ct in range(ncol // 128):
                    st = stf[nstf % 2]
                    stk = 'k1stf%d' % (nstf % 2)
                    nstf += 1
                    for (t0, tn) in TOKG:
                        pq = psm[ev % 4]
                        pk = 'ps%d' % (4 + ev % 4)
                        for k in range(16):
                            P.mm(pq[:, 0:tn], sl[:, k, ct * 128:(ct + 1) * 128], hT[:, k, t0:t0 + tn], k == 0, k == 15,
                                 reads=[sk, 'hT'], writes=[pk])
                        P.copy('act' if ev % 2 == 0 else 'dve', st[:, t0:t0 + tn], pq[:, 0:tn], reads=[pk], writes=[stk])
                        ev += 1
                    P.dma('sp', PT[fm_off:fm_off + 128, LC + 2048 * hf:LC + 2048 * hf + 2048], st[:, 0:2048], reads=[stk], writes=['PT'])
                    P.dma('sp', PT[fm_off:fm_off + 128, 128 * hf:128 * hf + 128], st[:, 2048:RT], reads=[stk], writes=['PT'])
                    fm_off += 128
            else:
                for t in range(NT):
                    st = stt_[nstt % 2]
                    stk = 'k1stt%d' % (nstt % 2)
                    nstt += 1
                    pq = psm[ev % 4]
                    pk = 'ps%d' % (4 + ev % 4)
                    for k in range(16):
                        P.mm(pq[:, 0:ncol], hT[:, k, t * 128:(t + 1) * 128], sl[:, k, 0:ncol], k == 0, k == 15,
                             reads=[sk, 'hT'], writes=[pk])
                    P.copy('act' if ev % 2 == 0 else 'dve', st[:, 0:ncol], pq[:, 0:ncol], reads=[pk], writes=[stk])
                    ev += 1
                    r = urow(hf, t)
                    P.dma('sp', PM[r:r + 128, tm_off:tm_off + ncol], st[:, 0:ncol], reads=[stk], writes=['PM'])
                tm_off += ncol
    P.barrier()


def emit_k3(P, xsrc, xkey, O, modd, l, n2g, w_out, w_gate, w_up, w_down, xdst, xdkey, xmT, hf, ps, ones, eps_t, identb, tag):
    nc = P.nc
    with ExitStack() as stack0:
        V = stack0.enter_context(nc.sbuf_tensor("V" + tag, [128, 9, 16], F32))
        A2 = stack0.enter_context(nc.sbuf_tensor("A2" + tag, [128, 2, 16], F32))
        with nc.allow_non_contiguous_dma(reason="small per-feature vectors, feature-major"):
            P.dma('sp', V[:, 0:4, :], modd[l, 0, 2 * D:6 * D].rearrange("(r c p) -> p r c", r=4, p=128), reads=['modd'], writes=['V'])
            P.dma('sp', V[:, 4:8, :], modd[l, 1, 2 * D:6 * D].rearrange("(r c p) -> p r c", r=4, p=128), reads=['modd'], writes=['V'])
            P.dma('sp', V[:, 8, :], n2g.rearrange("(c p) -> p c", p=128), writes=['V'])
        P.stt(A2[:, 0, :], V[:, 2, :], 1.0, V[:, 8, :], ALU.add, ALU.mult, reads=['V'], writes=['A2'])
        P.stt(A2[:, 1, :], V[:, 6, :], 1.0, V[:, 8, :], ALU.add, ALU.mult, reads=['V'], writes=['A2'])
        with ExitStack() as stack:
            def T(name, shape, dt=F32):
                return stack.enter_context(nc.sbuf_tensor(name + tag, shape, dt))
            oTb = T("oTb", [128, 16, RT], BF16)
            wo = [T("wo%d" % i, [128, 16, 128], BF16) for i in range(2)]
            xa = [T("xa%d" % i, [128, RT]) for i in range(2)]
            xb = [T("xb%d" % i, [128, RT]) for i in range(2)]
            ot = [T("ot%d" % i, [128, D]) for i in range(2)]
            ob = T("ob", [128, D], BF16)
            pstb = [ps[6][:].bitcast(BF16), ps[7][:].bitcast(BF16)]
            for t in range(NT):
                o_t, otk = ot[t % 2], 'ot%d' % (t % 2)
                r = urow(hf, t)
                P.dma('sp', o_t[:], O[r:r + 128, :], reads=['O'], writes=[otk])
                P.copy('pool', ob[:], o_t[:], reads=[otk], writes=['ob'])
                for half in range(2):
                    pt, pk = pstb[half], 'ps%d' % (6 + half)
                    for j in range(8):
                        k = half * 8 + j
                        P.tr(pt[:, j * 128:(j + 1) * 128], ob[:, k * 128:(k + 1) * 128], identb[:], reads=['ob', 'identb'], writes=[pk])
                    P.copy('act' if half == 0 else 'dve', oTb[:, half * 8:(half + 1) * 8, t * 128:(t + 1) * 128],
                           pt[:].rearrange("p (j n) -> p j n", j=8), reads=[pk], writes=['oTb'])
            wo_v = w_out.rearrange("(k p) n -> p k n", p=128)
            ev = 0
            for c in range(16):
                w, wk = wo[c % 2], 'wo%d' % (c % 2)
                P.dma('pool', w[:], wo_v[:, :, c * 128:(c + 1) * 128], writes=[wk])
                xin, xik = xa[c % 2], 'xa%d' % (c % 2)
                xout, xok = xb[c % 2], 'xb%d' % (c % 2)
                P.dma('sp', xin[:], xsrc[c * 128:(c + 1) * 128, :], reads=[xkey], writes=[xik])
                for (t0, tn) in TOKG:
                    pt, pk = ps[ev % 2], 'ps%d' % (ev % 2)
                    ev += 1
                    for k in range(16):
                        P.mm(pt[:, 0:tn], w[:, k, :], oTb[:, k, t0:t0 + tn], k == 0, k == 15, reads=[wk, 'oTb'], writes=[pk])
                    gcol = V[:, 0, c:c + 1] if t0 < 2048 else V[:, 4, c:c + 1]
                    P.stt(xout[:, t0:t0 + tn], pt[:, 0:tn], gcol, xin[:, t0:t0 + tn], ALU.mult, ALU.add,
                          reads=[pk, 'V', xik], writes=[xok])
                P.dma('sp', xmT[c * 128:(c + 1) * 128, :], xout[:], reads=[xok], writes=['xmT%d' % c])
        P.barrier()
        SGMAX = 1152
        with ExitStack() as stack:
            def T(name, shape, dt=F32):
                return stack.enter_context(nc.sbuf_tensor(name + tag, shape, dt))
            h2T = T("h2T", [128, 16, SGMAX], BF16)
            aT = T("aT", [128, NH, SGMAX], BF16)
            rstd = T("rstd", [128, SGMAX])
            xc = [T("xc%d" % i, [128, SGMAX]) for i in range(2)]
            tb = [T("t%d" % i, [128, SGMAX]) for i in range(2)]
            wg = [T("wg%d" % i, [128, 16, 128], BF16) for i in range(2)]
            wu = [T("wu%d" % i, [128, 16, 128], BF16) for i in range(2)]
            wd = [T("wd%d" % i, [128, NH, 128], BF16) for i in range(2)]
            sl = [T("sl%d" % i, [128, 512]) for i in range(2)]
            wg_v = w_gate.rearrange("(k p) n -> p k n", p=128)
            wu_v = w_up.rearrange("(k p) n -> p k n", p=128)
            wd_v = w_down.rearrange("(j p) n -> p j n", p=128)
            nld = 0
            for sg in SGS:
                s0 = sg[0][0]
                sn = sum(tn for (_, tn) in sg)
                nlat = sum(tn for (t0, tn) in sg if t0 < 2048)
                for c in range(16):
                    xt, xk = xc[nld % 2], 'xc%d' % (nld % 2)
                    tt_, tk = tb[nld % 2], 't%d' % (nld % 2)
                    nld += 1
                    P.dma('sp', xt[:, 0:sn], xmT[c * 128:(c + 1) * 128, s0:s0 + sn], reads=['xmT%d' % c], writes=[xk])
                    P.act(tt_[:, 0:sn], xt[:, 0:sn], AF.Square, reads=[xk], writes=[tk])
                    for gi, (t0, tn) in enumerate(sg):
                        P.mm(ps[2 + gi][:, 0:tn], ones, tt_[:, t0 - s0:t0 - s0 + tn], c == 0, c == 15,
                             reads=['C', tk], writes=['ps%d' % (2 + gi)])
                for gi, (t0, tn) in enumerate(sg):
                    P.act(rstd[:, t0 - s0:t0 - s0 + tn], ps[2 + gi][:, 0:tn], AF.Ln, reads=['ps%d' % (2 + gi), 'eps_t'],
                          writes=['rstd'], bias=eps_t[:, 0:1], scale=1.0 / D)
                P.act(rstd[:, 0:sn], rstd[:, 0:sn], AF.Exp, reads=['rstd'], writes=['rstd'], scale=-0.5)
                for c in range(16):
                    xt, xk = xc[nld % 2], 'xc%d' % (nld % 2)
                    tt_, tk = tb[nld % 2], 't%d' % (nld % 2)
                    nld += 1
                    P.dma('sp', xt[:, 0:sn], xmT[c * 128:(c + 1) * 128, s0:s0 + sn], reads=['xmT%d' % c], writes=[xk])
                    P.tt('dve', tt_[:, 0:sn], xt[:, 0:sn], rstd[:, 0:sn], ALU.mult, reads=[xk, 'rstd'], writes=[tk])
                    P.ts('pool', h2T[:, c, 0:nlat], tt_[:, 0:nlat], A2[:, 0, c:c + 1], V[:, 1, c:c + 1], ALU.mult, ALU.add,
                         reads=[tk, 'A2', 'V'], writes=['h2T'])
                    if nlat < sn:
                        P.ts('pool', h2T[:, c, nlat:sn], tt_[:, nlat:sn], A2[:, 1, c:c + 1], V[:, 5, c:c + 1], ALU.mult, ALU.add,
                             reads=[tk, 'A2', 'V'], writes=['h2T'])
                ev = 0
                for j in range(NH):
                    g_, gk = wg[j % 2], 'wg%d' % (j % 2)
                    u_, uk = wu[j % 2], 'wu%d' % (j % 2)
                    P.dma('pool', g_[:], wg_v[:, :, j * 128:(j + 1) * 128], writes=[gk])
                    P.dma('pool', u_[:], wu_v[:, :, j * 128:(j + 1) * 128], writes=[uk])
                    for (t0, tn) in sg:
                        pg, pgk = ps[ev % 2], 'ps%d' % (ev % 2)
                        pu, puk = ps[5 + ev % 2], 'ps%d' % (5 + ev % 2)
                        s_, sk = sl[ev % 2], 'sl%d' % (ev % 2)
                        ev += 1
                        for k in range(16):
                            P.mm(pg[:, 0:tn], g_[:, k, :], h2T[:, k, t0 - s0:t0 - s0 + tn], k == 0, k == 15, reads=[gk, 'h2T'], writes=[pgk])
                        for k in range(16):
                            P.mm(pu[:, 0:tn], u_[:, k, :], h2T[:, k, t0 - s0:t0 - s0 + tn], k == 0, k == 15, reads=[uk, 'h2T'], writes=[puk])
                        P.act(s_[:, 0:tn], pg[:, 0:tn], AF.Silu, reads=[pgk], writes=[sk])
                        P.tt('dve', aT[:, j, t0 - s0:t0 - s0 + tn], pu[:, 0:tn], s_[:, 0:tn], ALU.mult, reads=[puk, sk], writes=['aT'])
                ev = 0
                for c in range(16):
                    w, wk = wd[c % 2], 'wd%d' % (c % 2)
                    for jh in range(4):
                        P.dma('pool', w[:, jh * 11:(jh + 1) * 11, :], wd_v[:, jh * 11:(jh + 1) * 11, c * 128:(c + 1) * 128], writes=[wk])
                    xt, xk = xc[nld % 2], 'xc%d' % (nld % 2)
                    tt_, tk = tb[nld % 2], 't%d' % (nld % 2)
                    nld += 1
                    P.dma('sp', xt[:, 0:sn], xmT[c * 128:(c + 1) * 128, s0:s0 + sn], reads=['xmT%d' % c], writes=[xk])
                    for (t0, tn) in sg:
                        pt, pk = ps[3 + ev % 2], 'ps%d' % (3 + ev % 2)
                        ev += 1
                        for j in range(NH):
                            P.mm(pt[:, 0:tn], w[:, j, :], aT[:, j, t0 - s0:t0 - s0 + tn], j == 0, j == NH - 1, reads=[wk, 'aT'], writes=[pk])
                        gcol = V[:, 3, c:c + 1] if t0 < 2048 else V[:, 7, c:c + 1]
                        P.stt(tt_[:, t0 - s0:t0 - s0 + tn], pt[:, 0:tn], gcol, xt[:, t0 - s0:t0 - s0 + tn], ALU.mult, ALU.add,
                              reads=[pk, 'V', xk], writes=[tk])
                    P.dma('sp', xdst[c * 128:(c + 1) * 128, s0:s0 + sn], tt_[:, 0:sn], reads=[tk], writes=[xdkey])
    P.barrier()


def build_fused(layers=(0, 1), do_k2=True):
    P = Prog()
    nc = P.nc
    dt_in = lambda name, shape: nc.dram_tensor(name, shape, F32, kind="ExternalInput").ap()
    xT = dt_in("xT", [2, D, RT])
    cT2 = dt_in("cT2", [D, 2])
    w_ada = dt_in("w_ada", [2, D, 6 * D])
    b_ada = dt_in("b_ada", [2, 6 * D])
    n1g = dt_in("norm1_g", [2, D])
    n2g = dt_in("norm2_g", [2, D])
    w_in = dt_in("w_in", [2, D, NIN])
    w_out = dt_in("w_out", [2, D, D])
    w_gate = dt_in("w_gate", [2, D, FFN])
    w_up = dt_in("w_up", [2, D, FFN])
    w_down = dt_in("w_down", [2, FFN, D])
    gains = dt_in("gains", [2, 128, 4])
    convw = dt_in("convw", [2, 2, 128, 12, 5])
    rowp = dt_in("rowp", [2, 2, 1, RP_END])
    nabd = dt_in("nab", [2, 2, 128, 2, 21, 128])
    cosT = dt_in("cosT", [128, L])
    sinT = dt_in("sinT", [128, L])
    constd = dt_in("consts", [128, C_END])
    xoT = nc.dram_tensor("xoT", [2, D, RT], F32, kind="ExternalOutput").ap()
    di = lambda name, shape: nc.dram_tensor(name, shape, F32, kind="Internal").ap()
    modd = di("modd", [2, 2, 6 * D])
    PT = di("PT", [NFM, TT])
    PM = di("PM", [TT, NTM])
    O = di("O", [TT, D])
    XT1 = di("XT1", [2, D, RT])
    xmT = di("xmT", [D, RT])

    C = nc.alloc_sbuf_tensor("C", [128, C_XI], F32)
    identb = nc.alloc_sbuf_tensor("identb", [128, 128], BF16)
    eps_t = nc.alloc_sbuf_tensor("eps_t", [128, 2], F32)
    rp = nc.alloc_sbuf_tensor("rp", [128, RP_END], F32)
    gn = nc.alloc_sbuf_tensor("gn", [128, 8], F32)
    ps = [nc.alloc_psum_tensor("ps%d" % i, [128, 512], F32) for i in range(8)]
    P.dma('sp', C[:], constd[:, 0:C_XI], writes=['C'])
    P.memset('dve', eps_t[:, 0:1], EPS, writes=['eps_t'])
    P.memset('dve', eps_t[:, 1:2], float(np.log(HD ** -0.5)), writes=['eps_t'])
    P.copy('dve', identb[:], C[:, C_ID:C_ID + 128], reads=['C'], writes=['identb'])
    ident = C[:, C_ID:C_ID + 128]
    ones = C[:, C_ONES:C_ONES + 128]

    emit_k0(P, cT2, w_ada, b_ada, modd, ps)
    for l in layers:
        for hf in range(2):
            xsrc, xkey = (xT[hf], 'xT') if l == 0 else (XT1[hf], 'XT1_%d' % hf)
            emit_k1(P, xsrc, xkey, modd, l, n1g[l], w_in[l], PT, PM, hf, ps, ones, eps_t, "_k1_%d%d" % (l, hf))
        if do_k2:
            for g in range(2):
                sfx = "_k2_%d%d" % (l, g)
                io = K2IO(PT=PT, PM=PM, O=O, g=g, sfx=sfx)
                P.dma('sp', rp[:], rowp[l, g].partition_broadcast(128), writes=['rp'])
                P.dma('sp', gn[:, 0:4], gains[l], writes=['gn'])
                P.ts('dve', gn[:, 4:5], gn[:, 0:1], HD ** -0.5, None, ALU.mult, None, reads=['gn'], writes=['gn'])
                P.ts('dve', gn[:, 5:6], gn[:, 2:3], HD ** -0.5, None, ALU.mult, None, reads=['gn'], writes=['gn'])
                emit_attention(P, io, cosT, sinT, nabd[l, g], C, rp, gn, eps_t, ps, ident, ones, identb)
                P.barrier()
                emit_gdn(P, io, convw[l, g], C, rp, eps_t, ps, ident, ones, (0, 1, 2, 3), xinit=constd[:, C_XI:C_END])
                P.barrier()
        for hf in range(2):
            xsrc, xkey = (xT[hf], 'xT') if l == 0 else (XT1[hf], 'XT1_%d' % hf)
            last = (l == layers[-1]) and (l == 1)
            xdst, xdkey = (xoT[hf], 'xoT') if (l == 1) else (XT1[hf], 'XT1_%d' % hf)
            emit_k3(P, xsrc, xkey, O, modd, l, n2g[l], w_out[l], w_gate[l], w_up[l], w_down[l], xdst, xdkey, xmT, hf,
                    ps, ones, eps_t, identb, "_k3_%d%d" % (l, hf))
    return P


def fused_inputs(inp, b):
    f = lambda a: np.ascontiguousarray(np.asarray(a, dtype=np.float32))
    x, ctx = inp['x'], inp['ctx']
    xT = np.stack([np.concatenate([x[b, hf * 2048:(hf + 1) * 2048], ctx[b, hf * 128:(hf + 1) * 128]], axis=0).T for hf in range(2)], axis=0)
    cT2 = np.stack([inp['c'][b], inp['c_ctx']], axis=1)
    gains = np.stack([np.stack([inp['swa_q_g'][l], inp['swa_k_g'][l], inp['na_q_g'][l], inp['na_k_g'][l]], axis=1) for l in range(2)], axis=0)
    convw = np.zeros((2, 2, 128, 12, 5), np.float32)
    rowp = np.zeros((2, 2, 1, RP_END), np.float32)
    nab = np.zeros((2, 2, 128, 2, 21, 128), np.float32)
    for l in range(2):
        cw = inp['dn_conv_w'][l]
        for g in range(2):
            for i in range(3):
                for hh in range(4):
                    c0 = i * 1024 + (4 * g + hh) * 128
                    convw[l, g, :, i * 4 + hh, :] = cw[:, c0:c0 + 128].T
            rowp[l, g, 0, RP_OUTG:RP_OUTG + 128] = inp['dn_out_g'][l]
            rowp[l, g, 0, RP_ALOG:RP_ALOG + 8] = inp['dn_A_log'][l][:, 4 * g:4 * g + 4].reshape(-1)
            rowp[l, g, 0, RP_DTB:RP_DTB + 8] = inp['dn_dt_bias'][l][:, 4 * g:4 * g + 4].reshape(-1)
            rowp[l, g, 0, RP_SINK:RP_SINK + 2] = inp['swa_sink'][l][2 * g:2 * g + 2]
            nab[l, g] = np.stack([na_tables(inp['na_rpb'][l][2 * g + h]) for h in range(2)], axis=1)
    cosT, sinT = rope_tables_T()
    return {"xT": f(xT), "cT2": f(cT2), "w_ada": inp['w_ada'], "b_ada": inp['b_ada'], "norm1_g": inp['norm1_g'],
            "norm2_g": inp['norm2_g'], "w_in": inp['w_in'], "w_out": inp['w_out'], "w_gate": inp['w_gate'], "w_up": inp['w_up'],
            "w_down": inp['w_down'], "gains": f(gains), "convw": convw, "rowp": rowp, "nab": nab, "cosT": cosT, "sinT": sinT,
            "consts": k2_consts()}


def kernel_fused(**inp):
    f = lambda a: np.ascontiguousarray(np.asarray(a, dtype=np.float32))
    inp = {k: f(v) for k, v in inp.items()}
    P = build_fused()
    P.finish()
    in_maps = [fused_inputs(inp, b) for b in range(B)]
    res = run_bass_kernel_spmd(P.nc, in_maps, core_ids=list(range(B))).results
    out = np.empty((B, L, D), np.float32)
    for b in range(B):
        for hf in range(2):
            out[b, hf * 2048:(hf + 1) * 2048] = res[b]["xoT"][hf].T[:2048]
    return out


def kernel(**inputs):
    return kernel_fused(**inputs)
```

```python
import numpy as np
import concourse.bass as bass
import concourse.mybir as mybir
from concourse.bass_utils import run_bass_kernel_spmd
from contextlib import ExitStack

F32 = mybir.dt.float32
BF16 = mybir.dt.bfloat16
AF = mybir.ActivationFunctionType
ALU = mybir.AluOpType
AX = mybir.AxisListType

D = 2048
B = 4
L = 4096
LC = 256
HD = 128
NIN = 6688
FFN = 5632
EPS = 1e-6
NCORES = 8
SAME_ENGINE_SYNC = True


class Prog:
    def __init__(self):
        self.nc = bass.Bass("TRN2", target_bir_lowering=False)
        nc = self.nc
        self.E = {'pe': nc.tensor, 'dve': nc.vector, 'act': nc.scalar, 'pool': nc.gpsimd, 'sp': nc.sync}
        self.sem = {e: nc.alloc_semaphore('sem_' + e) for e in self.E}
        self.cnt = {e: 0 for e in self.E}
        self.NS = 8
        self.ring = {q: [nc.alloc_semaphore('ring_%s_%d' % (q, i)) for i in range(self.NS)] for q in ('sp', 'pool')}
        self.ndma = {q: 0 for q in self.ring}
        self.seen = {e: {} for e in self.E}
        self.lastw = {}
        self.readers = {}
        self.semobj = {}
        for e in self.E:
            self.semobj[('eng', e)] = self.sem[e]
        for q in self.ring:
            for i in range(self.NS):
                self.semobj[('ring', q, i)] = self.ring[q][i]
        self.n_inst = 0
        self.marks = []

    @staticmethod
    def key(k):
        if isinstance(k, (str, tuple)):
            return k
        if hasattr(k, 'tensor'):
            return k.tensor.name
        return k.name

    def _deps(self, e, reads, writes, inorder=False):
        need = {}

        def add(tok):
            sk, val, peng = tok
            if peng == e and (e == 'pe' or inorder or not SAME_ENGINE_SYNC):
                return
            if self.seen[e].get(sk, 0) >= val:
                return
            if need.get(sk, 0) < val:
                need[sk] = val

        for k in reads:
            t = self.lastw.get(k)
            if t is not None:
                add(t)
        for k in writes:
            t = self.lastw.get(k)
            if t is not None:
                add(t)
            for sk, (val, peng) in self.readers.get(k, {}).items():
                add((sk, val, peng))
        return need

    def _record(self, tok, reads, writes):
        sk, val, peng = tok
        for k in writes:
            self.lastw[k] = tok
            self.readers[k] = {}
        for k in reads:
            r = self.readers.setdefault(k, {})
            r[sk] = (val, peng)

    def op(self, e, fn, reads=(), writes=(), inorder=False):
        reads = [self.key(k) for k in reads]
        writes = [self.key(k) for k in writes]
        eng = self.E[e]
        need = self._deps(e, reads, writes, inorder)
        items = list(need.items())
        for sk, val in items[:-1]:
            eng.wait_ge(self.semobj[sk], val)
            self.seen[e][sk] = val
        ins = fn(eng)
        if items:
            sk, val = items[-1]
            ins.wait_op(self.semobj[sk], val, 'sem-ge')
            self.seen[e][sk] = val
        self.cnt[e] += 1
        ins.then_inc(self.sem[e], 1)
        self._record((('eng', e), self.cnt[e], e), reads, writes)
        self.n_inst += 1
        return ins

    def dma(self, q, out, in_, reads=(), writes=(), **kw):
        reads = [self.key(k) for k in reads]
        writes = [self.key(k) for k in writes]
        eng = self.E[q]
        k = self.ndma[q]
        r = k % self.NS
        base = 16 * (k // self.NS)
        need = self._deps(q, reads, writes)
        sk_ring = ('ring', q, r)
        if k >= self.NS and self.seen[q].get(sk_ring, 0) < base:
            need[sk_ring] = max(need.get(sk_ring, 0), base)
        for sk, val in need.items():
            eng.wait_ge(self.semobj[sk], val)
            self.seen[q][sk] = val
        ins = eng.dma_start(out=out, in_=in_, **kw)
        ins.then_inc(self.ring[q][r], 16)
        self.ndma[q] += 1
        self._record((sk_ring, base + 16, 'dma'), reads, writes)
        self.n_inst += 1
        return ins

    def finish(self):
        sp = self.E['sp']
        for q in self.ring:
            k = self.ndma[q]
            for r in range(self.NS):
                n = (k - r + self.NS - 1) // self.NS if k > r else 0
                if n > 0:
                    sp.wait_ge(self.ring[q][r], 16 * n)
        for e in self.E:
            if self.cnt[e] > 0:
                sp.wait_ge(self.sem[e], self.cnt[e])

    def mm(self, out, lhsT, rhs, start, stop, reads, writes):
        return self.op('pe', lambda g: g.matmul(out, lhsT, rhs, start=start, stop=stop), reads, writes)

    def tr(self, out, in_, ident, reads, writes):
        return self.op('pe', lambda g: g.transpose(out, in_, ident), reads, writes)

    def act(self, out, in_, func, reads, writes, bias=None, scale=None, accum_out=None, e='act'):
        kw = {}
        if bias is not None:
            kw['bias'] = bias
        if scale is not None:
            kw['scale'] = scale
        if accum_out is not None:
            kw['accum_out'] = accum_out
        return self.op('act', lambda g: g.activation(out, in_, func, **kw), reads, writes)

    def tt(self, e, out, in0, in1, op, reads, writes, inorder=False):
        return self.op(e, lambda g: g.tensor_tensor(out, in0, in1, op), reads, writes, inorder)

    def ts(self, e, out, in0, s1, s2, op0, op1, reads, writes):
        if op1 is None:
            return self.op(e, lambda g: g.tensor_scalar(out, in0, s1, None, op0), reads, writes)
        return self.op(e, lambda g: g.tensor_scalar(out, in0, s1, s2, op0, op1), reads, writes)

    def stt(self, out, in0, scalar, in1, op0, op1, reads, writes):
        return self.op('dve', lambda g: g.scalar_tensor_tensor(out, in0, scalar, in1, op0, op1), reads, writes)

    def copy(self, e, out, in_, reads, writes):
        if e == 'act':
            return self.op('act', lambda g: g.copy(out, in_), reads, writes)
        return self.op(e, lambda g: g.tensor_copy(out, in_), reads, writes)

    def memset(self, e, ap, val, writes):
        return self.op(e, lambda g: g.memset(ap, val), (), writes)


def run_prog(P, in_maps):
    P.finish()
    res = run_bass_kernel_spmd(P.nc, in_maps, core_ids=list(range(NCORES)))
    return res.results


NC0 = 6 * D // NCORES


def build_k0():
    P = Prog()
    nc = P.nc
    cT = nc.dram_tensor("cT", [D, 5], F32, kind="ExternalInput").ap()
    wa = nc.dram_tensor("wa", [2, D, NC0], F32, kind="ExternalInput").ap()
    ba = nc.dram_tensor("ba", [2, NC0], F32, kind="ExternalInput").ap()
    mod = nc.dram_tensor("mod", [2, 5, NC0], F32, kind="ExternalOutput").ap()
    cs = nc.alloc_sbuf_tensor("cs", [128, 16, 5], F32)
    w = [nc.alloc_sbuf_tensor("w%d" % i, [128, 8, NC0], F32) for i in range(2)]
    bb = nc.alloc_sbuf_tensor("bb", [5, 2, NC0], F32)
    res = nc.alloc_sbuf_tensor("res", [5, 2, NC0], F32)
    ps = [nc.alloc_psum_tensor("ps%d" % i, [128, 512], F32) for i in range(3)]
    with nc.allow_non_contiguous_dma(reason="tiny transposed load"):
        P.dma('sp', cs[:], cT.rearrange("(k p) r -> p k r", p=128), writes=['cs'])
    for l in range(2):
        P.dma('sp', bb[:, l, :], ba[l:l + 1, :].partition_broadcast(5), writes=['bb%d' % l])
    P.act(cs[:], cs[:], AF.Silu, reads=['cs'], writes=['cs'])
    for l in range(2):
        for half in range(2):
            wt = w[half]
            wk = 'w%d' % half
            P.dma('sp', wt[:], wa[l, half * 1024:(half + 1) * 1024, :].rearrange("(k p) n -> p k n", p=128),
                  writes=[wk])
            for n in range(3):
                for k in range(8):
                    kk = half * 8 + k
                    P.mm(ps[n][0:5, :], cs[:, kk, :], wt[:, k, n * 512:(n + 1) * 512],
                         start=(kk == 0), stop=(kk == 15), reads=['cs', wk], writes=['ps%d' % n])
        for n in range(3):
            P.tt('dve', res[:, l, n * 512:(n + 1) * 512], ps[n][0:5, :], bb[:, l, n * 512:(n + 1) * 512], ALU.add,
                 reads=['ps%d' % n, 'bb%d' % l], writes=['res%d' % l])
        P.dma('pool', mod[l], res[:, l, :], reads=['res%d' % l])
    return P


def run_k0(c, c_ctx, w_ada, b_ada):
    P = build_k0()
    cT = np.ascontiguousarray(np.concatenate([c, c_ctx[None, :]], axis=0).T)
    in_maps = []
    for i in range(NCORES):
        sl = slice(i * NC0, (i + 1) * NC0)
        in_maps.append({"cT": cT, "wa": np.ascontiguousarray(w_ada[:, :, sl]),
                        "ba": np.ascontiguousarray(b_ada[:, sl])})
    res = run_prog(P, in_maps)
    return np.concatenate([r["mod"] for r in res], axis=2)


RT = 2176
NT = RT // 128
SLABS = [(0, 512, 'F'), (512, 256, 'F'), (768, 256, 'T'),
         (1024, 512, 'F'), (1536, 512, 'F'), (2048, 512, 'F'), (2560, 512, 'F'), (3072, 512, 'F'), (3584, 512, 'F'),
         (4096, 512, 'T'), (4608, 512, 'T'), (5120, 32, 'T'),
         (5152, 512, 'F'), (5664, 512, 'F'), (6176, 512, 'T')]
FM_COLS = [c for (s, n, k) in SLABS if k == 'F' for c in range(s, s + n)]
TM_COLS = [c for (s, n, k) in SLABS if k == 'T' for c in range(s, s + n)]
NFM = len(FM_COLS)
NTM = len(TM_COLS)
TOKG = [(0, 512), (512, 512), (1024, 512), (1536, 512), (2048, 128)]


def emit_norm_mod_T(P, xt, xk, A_t, B_t, Ak, tmp, hb, hT, t, ident, pst, pstk, junk_scale):
    nc = P.nc
    ss = P.ss
    P.act(tmp[:], xt[:], AF.Square, reads=[xk], writes=['tmp', 'ss'], accum_out=ss[:, 0:1])
    P.act(ss[:, 1:2], ss[:, 0:1], AF.Sqrt, reads=['ss'], writes=['ss'], bias=P.eps_t[:, 0:1], scale=1.0 / D)
    P.op('dve', lambda g: g.reciprocal(ss[:, 2:3], ss[:, 1:2]), reads=['ss'], writes=['ss'])
    P.stt(tmp[:], xt[:], ss[:, 2:3], A_t[:], ALU.mult, ALU.mult, reads=[xk, 'ss', Ak], writes=['tmp'])
    P.tt('pool', hb[:], tmp[:], B_t[:], ALU.add, reads=['tmp', Ak], writes=['hb'])
    for half in range(2):
        pt = pst[half]
        pk = pstk[half]
        for j in range(8):
            k = half * 8 + j
            P.tr(pt[:, j * 128:(j + 1) * 128], hb[:, k * 128:(k + 1) * 128], ident[:], reads=['hb', 'ident'], writes=[pk])
        P.copy('act' if half == 0 else 'dve', hT[:, half * 8:(half + 1) * 8, t * 128:(t + 1) * 128],
               pt[:].rearrange("p (j n) -> p j n", j=8), reads=[pk], writes=['hT'])


def build_k1():
    P = Prog()
    nc = P.nc
    xr = nc.dram_tensor("xr", [RT, D], F32, kind="ExternalInput").ap()
    modv = nc.dram_tensor("modv", [4, D], F32, kind="ExternalInput").ap()
    g1 = nc.dram_tensor("g1", [1, D], F32, kind="ExternalInput").ap()
    w_in = nc.dram_tensor("w_in", [D, NIN], F32, kind="ExternalInput").ap()
    identd = nc.dram_tensor("identd", [128, 128], F32, kind="ExternalInput").ap()
    pT = nc.dram_tensor("pT", [NFM, RT], F32, kind="ExternalOutput").ap()
    pM = nc.dram_tensor("pM", [RT, NTM], F32, kind="ExternalOutput").ap()

    hT = nc.alloc_sbuf_tensor("hT", [128, 16, RT], BF16)
    A_lat = nc.alloc_sbuf_tensor("A_lat", [128, D], F32)
    B_lat = nc.alloc_sbuf_tensor("B_lat", [128, D], F32)
    A_ctx = nc.alloc_sbuf_tensor("A_ctx", [128, D], F32)
    B_ctx = nc.alloc_sbuf_tensor("B_ctx", [128, D], F32)
    xts = [nc.alloc_sbuf_tensor("xt%d" % i, [128, D], F32) for i in range(2)]
    tmp = nc.alloc_sbuf_tensor("tmp", [128, D], F32)
    hb = nc.alloc_sbuf_tensor("hb", [128, D], BF16)
    identf = nc.alloc_sbuf_tensor("identf", [128, 128], F32)
    ident = nc.alloc_sbuf_tensor("ident", [128, 128], BF16)
    P.ss = nc.alloc_sbuf_tensor("ss", [128, 4], F32)
    P.eps_t = nc.alloc_sbuf_tensor("eps_t", [128, 1], F32)
    slab = [nc.alloc_sbuf_tensor("slab%d" % i, [128, 16, 512], BF16) for i in range(2)]
    stf = [nc.alloc_sbuf_tensor("stf%d" % i, [128, RT], F32) for i in range(2)]
    stt_ = [nc.alloc_sbuf_tensor("stt%d" % i, [128, 512], F32) for i in range(2)]
    pst = [nc.alloc_psum_tensor("pst%d" % i, [128, 1024], BF16) for i in range(2)]
    psm = [nc.alloc_psum_tensor("psm%d" % i, [128, 512], F32) for i in range(4)]

    P.memset('dve', P.eps_t[:], EPS, writes=['eps_t'])
    P.dma('sp', identf[:], identd, writes=['identf'])
    P.copy('dve', ident[:], identf[:], reads=['identf'], writes=['ident'])
    P.dma('sp', tmp[:], g1[0:1, :].partition_broadcast(128), writes=['tmp'])
    for (A_t, B_t, r0, nm) in ((A_lat, B_lat, 0, 'A_lat'), (A_ctx, B_ctx, 2, 'A_ctx')):
        P.dma('sp', B_t[:], modv[r0:r0 + 1, :].partition_broadcast(128), writes=[nm])
        P.dma('sp', A_t[:], modv[r0 + 1:r0 + 2, :].partition_broadcast(128), writes=[nm])
        P.stt(A_t[:], A_t[:], 1.0, tmp[:], ALU.add, ALU.mult, reads=[nm, 'tmp'], writes=[nm])

    for t in range(NT):
        xt = xts[t % 2]
        xk = 'xt%d' % (t % 2)
        P.dma('sp', xt[:], xr[t * 128:(t + 1) * 128, :], writes=[xk])
        if t < 16:
            emit_norm_mod_T(P, xt, xk, A_lat, B_lat, 'A_lat', tmp, hb, hT, t, ident, pst, ['pst0', 'pst1'], None)
        else:
            emit_norm_mod_T(P, xt, xk, A_ctx, B_ctx, 'A_ctx', tmp, hb, hT, t, ident, pst, ['pst0', 'pst1'], None)

    w_v = w_in.rearrange("(k p) n -> p k n", p=128)
    fm_off = 0
    tm_off = 0
    ev = 0
    nstf = 0
    nstt = 0
    for si, (c0, ncol, kind) in enumerate(SLABS):
        sl = slab[si % 2]
        sk = 'slab%d' % (si % 2)
        for kh in range(2):
            P.dma('pool', sl[:, kh * 8:(kh + 1) * 8, 0:ncol], w_v[:, kh * 8:(kh + 1) * 8, c0:c0 + ncol], writes=[sk])
        if kind == 'F':
            for ct in range(ncol // 128):
                st = stf[nstf % 2]
                stk = 'stf%d' % (nstf % 2)
                nstf += 1
                for (t0, tn) in TOKG:
                    ps = psm[ev % 4]
                    pk = 'psm%d' % (ev % 4)
                    for k in range(16):
                        P.mm(ps[:, 0:tn], sl[:, k, ct * 128:(ct + 1) * 128], hT[:, k, t0:t0 + tn],
                             start=(k == 0), stop=(k == 15), reads=[sk, 'hT'], writes=[pk])
                    P.copy('act' if ev % 2 == 0 else 'dve', st[:, t0:t0 + tn], ps[:, 0:tn], reads=[pk], writes=[stk])
                    ev += 1
                P.dma('sp', pT[fm_off:fm_off + 128, :], st[:], reads=[stk])
                fm_off += 128
        else:
            for t in range(NT):
                st = stt_[nstt % 2]
                stk = 'stt%d' % (nstt % 2)
                nstt += 1
                ps = psm[ev % 4]
                pk = 'psm%d' % (ev % 4)
                for k in range(16):
                    P.mm(ps[:, 0:ncol], hT[:, k, t * 128:(t + 1) * 128], sl[:, k, 0:ncol],
                         start=(k == 0), stop=(k == 15), reads=[sk, 'hT'], writes=[pk])
                P.copy('act' if ev % 2 == 0 else 'dve', st[:, 0:ncol], ps[:, 0:ncol], reads=[pk], writes=[stk])
                ev += 1
                P.dma('sp', pM[t * 128:(t + 1) * 128, tm_off:tm_off + ncol], st[:, 0:ncol], reads=[stk])
            tm_off += ncol
    return P


_PROG_CACHE = {}


def get_prog(name, builder):
    return builder()


def core_rows(x_all, xc_all, i):
    b, hf = i // 2, i % 2
    return np.concatenate([x_all[b, hf * 2048:(hf + 1) * 2048], xc_all[b, hf * 128:(hf + 1) * 128]], axis=0)


def run_k1(x_all, xc_all, mod_l, g1_l, w_in_l):
    P = build_k1()
    ident = np.eye(128, dtype=np.float32)
    in_maps = []
    for i in range(NCORES):
        b = i // 2
        modv = np.stack([mod_l[b, 0:D], mod_l[b, D:2 * D], mod_l[4, 0:D], mod_l[4, D:2 * D]], axis=0)
        in_maps.append({"xr": np.ascontiguousarray(core_rows(x_all, xc_all, i)), "modv": np.ascontiguousarray(modv),
                        "g1": np.ascontiguousarray(g1_l[None, :]), "w_in": w_in_l, "identd": ident})
    res = run_prog(P, in_maps)
    proj = np.empty((B, L, NIN), np.float32)
    projc = np.empty((B, LC, NIN), np.float32)
    for i in range(NCORES):
        b, hf = i // 2, i % 2
        full = np.empty((RT, NIN), np.float32)
        full[:, FM_COLS] = res[i]["pT"].T
        full[:, TM_COLS] = res[i]["pM"]
        proj[b, hf * 2048:(hf + 1) * 2048] = full[:2048]
        projc[b, hf * 128:(hf + 1) * 128] = full[2048:]
    return proj, projc


def barrier(P):
    for e in P.E:
        eng = P.E[e]
        for e2 in P.E:
            if e2 != e and P.cnt[e2] > 0 and P.seen[e].get(('eng', e2), 0) < P.cnt[e2]:
                eng.wait_ge(P.sem[e2], P.cnt[e2])
                P.seen[e][('eng', e2)] = P.cnt[e2]
        for q in P.ring:
            k = P.ndma[q]
            for r in range(P.NS):
                n = (k - r + P.NS - 1) // P.NS if k > r else 0
                if n > 0 and P.seen[e].get(('ring', q, r), 0) < 16 * n:
                    eng.wait_ge(P.ring[q][r], 16 * n)
                    P.seen[e][('ring', q, r)] = 16 * n


    if P.cnt['pe'] > 30000:
        try:
            new = P.nc.alloc_semaphore('sem_pe_%d' % len(P.marks) + '_%d' % P.n_inst)
        except Exception:
            new = None
        if new is not None:
            sk = ('eng', 'pe')
            P.sem['pe'] = new
            P.semobj[sk] = new
            P.cnt['pe'] = 0
            for e in P.E:
                P.seen[e].pop(sk, None)
            for k in [k for k, t in P.lastw.items() if t[0] == sk]:
                del P.lastw[k]
            for k in P.readers:
                P.readers[k].pop(sk, None)


Prog.barrier = barrier

SGS = [[(0, 512), (512, 512)], [(1024, 512), (1536, 512), (2048, 128)]]
NH = FFN // 128


def build_k3():
    P = Prog()
    nc = P.nc
    xT = nc.dram_tensor("xT", [D, RT], F32, kind="ExternalInput").ap()
    oT = nc.dram_tensor("oT", [D, RT], F32, kind="ExternalInput").ap()
    vec = nc.dram_tensor("vec", [9, D], F32, kind="ExternalInput").ap()
    w_out = nc.dram_tensor("w_out", [D, D], F32, kind="ExternalInput").ap()
    w_gate = nc.dram_tensor("w_gate", [D, FFN], F32, kind="ExternalInput").ap()
    w_up = nc.dram_tensor("w_up", [D, FFN], F32, kind="ExternalInput").ap()
    w_down = nc.dram_tensor("w_down", [FFN, D], F32, kind="ExternalInput").ap()
    xoT = nc.dram_tensor("xoT", [D, RT], F32, kind="ExternalOutput").ap()
    xmT = nc.dram_tensor("xmT", [D, RT], F32, kind="Internal").ap()

    V = nc.alloc_sbuf_tensor("V", [128, 9, 16], F32)
    A2 = nc.alloc_sbuf_tensor("A2", [128, 2, 16], F32)
    ones = nc.alloc_sbuf_tensor("ones", [128, 128], F32)
    eps_t = nc.alloc_sbuf_tensor("eps_t", [128, 1], F32)
    ps = [nc.alloc_psum_tensor("ps%d" % i, [128, 512], F32) for i in range(8)]
    with nc.allow_non_contiguous_dma(reason="small per-feature vectors, feature-major"):
        P.dma('sp', V[:], vec.rearrange("r (c p) -> p r c", p=128), writes=['V'])
    P.memset('dve', ones[:], 1.0, writes=['ones'])
    P.memset('dve', eps_t[:], EPS, writes=['eps_t'])
    P.stt(A2[:, 0, :], V[:, 2, :], 1.0, V[:, 8, :], ALU.add, ALU.mult, reads=['V'], writes=['A2'])
    P.stt(A2[:, 1, :], V[:, 6, :], 1.0, V[:, 8, :], ALU.add, ALU.mult, reads=['V'], writes=['A2'])

    with nc.sbuf_tensor("oTb", [128, 16, RT], BF16) as oTb, \
            nc.sbuf_tensor("wo0", [128, 16, 128], BF16) as wo0, nc.sbuf_tensor("wo1", [128, 16, 128], BF16) as wo1, \
            nc.sbuf_tensor("xa0", [128, RT], F32) as xa0, nc.sbuf_tensor("xa1", [128, RT], F32) as xa1, \
            nc.sbuf_tensor("xb0", [128, RT], F32) as xb0, nc.sbuf_tensor("xb1", [128, RT], F32) as xb1:
        wo = [wo0, wo1]
        xa = [xa0, xa1]
        xb = [xb0, xb1]
        oT_v = oT.rearrange("(k p) r -> p k r", p=128)
        for kq in range(4):
            P.dma('pool', oTb[:, kq * 4:(kq + 1) * 4, :], oT_v[:, kq * 4:(kq + 1) * 4, :], writes=['oTb'])
        wo_v = w_out.rearrange("(k p) n -> p k n", p=128)
        ev = 0
        for c in range(16):
            w = wo[c % 2]
            wk = 'wo%d' % (c % 2)
            P.dma('pool', w[:], wo_v[:, :, c * 128:(c + 1) * 128], writes=[wk])
            xin = xa[c % 2]
            xik = 'xa%d' % (c % 2)
            xout = xb[c % 2]
            xok = 'xb%d' % (c % 2)
            P.dma('sp', xin[:], xT[c * 128:(c + 1) * 128, :], writes=[xik])
            for (t0, tn) in TOKG:
                pt = ps[ev % 2]
                pk = 'ps%d' % (ev % 2)
                ev += 1
                for k in range(16):
                    P.mm(pt[:, 0:tn], w[:, k, :], oTb[:, k, t0:t0 + tn], start=(k == 0), stop=(k == 15),
                         reads=[wk, 'oTb'], writes=[pk])
                gcol = V[:, 0, c:c + 1] if t0 < 2048 else V[:, 4, c:c + 1]
                P.stt(xout[:, t0:t0 + tn], pt[:, 0:tn], gcol, xin[:, t0:t0 + tn], ALU.mult, ALU.add,
                      reads=[pk, 'V', xik], writes=[xok])
            P.dma('sp', xmT[c * 128:(c + 1) * 128, :], xout[:], reads=[xok], writes=['xmT%d' % c])
    P.barrier()

    SGMAX = 1152
    with nc.sbuf_tensor("h2T", [128, 16, SGMAX], BF16) as h2T, nc.sbuf_tensor("aT", [128, NH, SGMAX], BF16) as aT, \
            nc.sbuf_tensor("rstd", [128, SGMAX], F32) as rstd, \
            nc.sbuf_tensor("xc0", [128, SGMAX], F32) as xc0, nc.sbuf_tensor("xc1", [128, SGMAX], F32) as xc1, \
            nc.sbuf_tensor("t0", [128, SGMAX], F32) as tb0, nc.sbuf_tensor("t1", [128, SGMAX], F32) as tb1, \
            nc.sbuf_tensor("wg0", [128, 16, 128], BF16) as wg0, nc.sbuf_tensor("wg1", [128, 16, 128], BF16) as wg1, \
            nc.sbuf_tensor("wu0", [128, 16, 128], BF16) as wu0, nc.sbuf_tensor("wu1", [128, 16, 128], BF16) as wu1, \
            nc.sbuf_tensor("wd0", [128, NH, 128], BF16) as wd0, nc.sbuf_tensor("wd1", [128, NH, 128], BF16) as wd1, \
            nc.sbuf_tensor("sl0", [128, 512], F32) as sl0, nc.sbuf_tensor("sl1", [128, 512], F32) as sl1:
        xc = [xc0, xc1]
        tb = [tb0, tb1]
        wg = [wg0, wg1]
        wu = [wu0, wu1]
        wd = [wd0, wd1]
        sl = [sl0, sl1]
        wg_v = w_gate.rearrange("(k p) n -> p k n", p=128)
        wu_v = w_up.rearrange("(k p) n -> p k n", p=128)
        wd_v = w_down.rearrange("(j p) n -> p j n", p=128)
        nld = 0
        for sg in SGS:
            s0 = sg[0][0]
            sn = sum(tn for (_, tn) in sg)
            nlat = sum(tn for (t0, tn) in sg if t0 < 2048)
            for c in range(16):
                xt = xc[nld % 2]
                xk = 'xc%d' % (nld % 2)
                tt_ = tb[nld % 2]
                tk = 't%d' % (nld % 2)
                nld += 1
                P.dma('sp', xt[:, 0:sn], xmT[c * 128:(c + 1) * 128, s0:s0 + sn], reads=['xmT%d' % c], writes=[xk])
                P.act(tt_[:, 0:sn], xt[:, 0:sn], AF.Square, reads=[xk], writes=[tk])
                for gi, (t0, tn) in enumerate(sg):
                    P.mm(ps[2 + gi][:, 0:tn], ones[:], tt_[:, t0 - s0:t0 - s0 + tn], start=(c == 0), stop=(c == 15),
                         reads=['ones', tk], writes=['ps%d' % (2 + gi)])
            for gi, (t0, tn) in enumerate(sg):
                P.act(rstd[:, t0 - s0:t0 - s0 + tn], ps[2 + gi][:, 0:tn], AF.Ln, reads=['ps%d' % (2 + gi), 'eps_t'],
                      writes=['rstd'], bias=eps_t[:, 0:1], scale=1.0 / D)
            P.act(rstd[:, 0:sn], rstd[:, 0:sn], AF.Exp, reads=['rstd'], writes=['rstd'], scale=-0.5)
            for c in range(16):
                xt = xc[nld % 2]
                xk = 'xc%d' % (nld % 2)
                tt_ = tb[nld % 2]
                tk = 't%d' % (nld % 2)
                nld += 1
                P.dma('sp', xt[:, 0:sn], xmT[c * 128:(c + 1) * 128, s0:s0 + sn], reads=['xmT%d' % c], writes=[xk])
                P.tt('dve', tt_[:, 0:sn], xt[:, 0:sn], rstd[:, 0:sn], ALU.mult, reads=[xk, 'rstd'], writes=[tk])
                P.ts('pool', h2T[:, c, 0:nlat], tt_[:, 0:nlat], A2[:, 0, c:c + 1], V[:, 1, c:c + 1], ALU.mult, ALU.add,
                     reads=[tk, 'A2', 'V'], writes=['h2T'])
                if nlat < sn:
                    P.ts('pool', h2T[:, c, nlat:sn], tt_[:, nlat:sn], A2[:, 1, c:c + 1], V[:, 5, c:c + 1], ALU.mult, ALU.add,
                         reads=[tk, 'A2', 'V'], writes=['h2T'])
            ev = 0
            for j in range(NH):
                g_ = wg[j % 2]
                gk = 'wg%d' % (j % 2)
                u_ = wu[j % 2]
                uk = 'wu%d' % (j % 2)
                P.dma('pool', g_[:], wg_v[:, :, j * 128:(j + 1) * 128], writes=[gk])
                P.dma('pool', u_[:], wu_v[:, :, j * 128:(j + 1) * 128], writes=[uk])
                for (t0, tn) in sg:
                    pg = ps[ev % 2]
                    pgk = 'ps%d' % (ev % 2)
                    pu = ps[5 + ev % 2]
                    puk = 'ps%d' % (5 + ev % 2)
                    s_ = sl[ev % 2]
                    sk = 'sl%d' % (ev % 2)
                    ev += 1
                    for k in range(16):
                        P.mm(pg[:, 0:tn], g_[:, k, :], h2T[:, k, t0 - s0:t0 - s0 + tn], start=(k == 0), stop=(k == 15),
                             reads=[gk, 'h2T'], writes=[pgk])
                    for k in range(16):
                        P.mm(pu[:, 0:tn], u_[:, k, :], h2T[:, k, t0 - s0:t0 - s0 + tn], start=(k == 0), stop=(k == 15),
                             reads=[uk, 'h2T'], writes=[puk])
                    P.act(s_[:, 0:tn], pg[:, 0:tn], AF.Silu, reads=[pgk], writes=[sk])
                    P.tt('dve', aT[:, j, t0 - s0:t0 - s0 + tn], pu[:, 0:tn], s_[:, 0:tn], ALU.mult, reads=[puk, sk], writes=['aT'])
            ev = 0
            for c in range(16):
                w = wd[c % 2]
                wk = 'wd%d' % (c % 2)
                for jh in range(4):
                    P.dma('pool', w[:, jh * 11:(jh + 1) * 11, :], wd_v[:, jh * 11:(jh + 1) * 11, c * 128:(c + 1) * 128], writes=[wk])
                xt = xc[nld % 2]
                xk = 'xc%d' % (nld % 2)
                tt_ = tb[nld % 2]
                tk = 't%d' % (nld % 2)
                nld += 1
                P.dma('sp', xt[:, 0:sn], xmT[c * 128:(c + 1) * 128, s0:s0 + sn], reads=['xmT%d' % c], writes=[xk])
                for (t0, tn) in sg:
                    pt = ps[3 + ev % 2]
                    pk = 'ps%d' % (3 + ev % 2)
                    ev += 1
                    for j in range(NH):
                        P.mm(pt[:, 0:tn], w[:, j, :], aT[:, j, t0 - s0:t0 - s0 + tn], start=(j == 0), stop=(j == NH - 1),
                             reads=[wk, 'aT'], writes=[pk])
                    gcol = V[:, 3, c:c + 1] if t0 < 2048 else V[:, 7, c:c + 1]
                    P.stt(tt_[:, t0 - s0:t0 - s0 + tn], pt[:, 0:tn], gcol, xt[:, t0 - s0:t0 - s0 + tn], ALU.mult, ALU.add,
                          reads=[pk, 'V', xk], writes=[tk])
                P.dma('sp', xoT[c * 128:(c + 1) * 128, s0:s0 + sn], tt_[:, 0:sn], reads=[tk])
    return P


def run_k3(x_all, xc_all, o_all, oc_all, mod_l, g2n_l, w_out_l, w_gate_l, w_up_l, w_down_l):
    P = build_k3()
    in_maps = []
    for i in range(NCORES):
        b = i // 2
        m, mc = mod_l[b], mod_l[4]
        vec = np.stack([m[2 * D:3 * D], m[3 * D:4 * D], m[4 * D:5 * D], m[5 * D:6 * D],
                        mc[2 * D:3 * D], mc[3 * D:4 * D], mc[4 * D:5 * D], mc[5 * D:6 * D], g2n_l], axis=0)
        in_maps.append({"xT": np.ascontiguousarray(core_rows(x_all, xc_all, i).T),
                        "oT": np.ascontiguousarray(core_rows(o_all, oc_all, i).T),
                        "vec": np.ascontiguousarray(vec), "w_out": w_out_l, "w_gate": w_gate_l,
                        "w_up": w_up_l, "w_down": w_down_l})
    res = run_prog(P, in_maps)
    xo = np.empty((B, L, D), np.float32)
    xco = np.empty((B, LC, D), np.float32)
    for i in range(NCORES):
        b, hf = i // 2, i % 2
        r = res[i]["xoT"].T
        xo[b, hf * 2048:(hf + 1) * 2048] = r[:2048]
        xco[b, hf * 128:(hf + 1) * 128] = r[2048:]
    return xo, xco


TT = LC + L
NTT = TT // 128
NEG = -30000.0
C_ID, C_ONES, C_RM, C_TRI, C_U, C_SEL, C_MP, C_MN, C_XI, C_END = 0, 128, 256, 384, 640, 896, 1152, 1408, 1664, 1920
RP_OUTG, RP_ALOG, RP_DTB, RP_SINK, RP_END = 0, 128, 136, 144, 146
BLK = [(0, 256)] + [(256 + 512 * i, 512) for i in range(8)]


def k2_consts():
    c = np.zeros((128, C_END), np.float32)
    idx = np.arange(128)
    same = (idx[:, None] // 64) == (idx[None, :] // 64)
    c[:, C_ID:C_ID + 128] = np.eye(128)
    c[:, C_ONES:C_ONES + 128] = 1.0
    rm = np.zeros((128, 128), np.float32)
    for m in range(128):
        blk = m // 32
        if blk % 2 == 0:
            rm[m + 32, m] = -1.0
        else:
            rm[m - 32, m] = 1.0
    c[:, C_RM:C_RM + 128] = rm
    c[:, C_TRI:C_TRI + 128] = (idx[:, None] <= idx[None, :]) & same
    c[:, C_TRI + 128:C_TRI + 256] = (idx[:, None] >= idx[None, :]) & same
    c[:, C_U:C_U + 128] = (idx[:, None] > idx[None, :]) & same
    c[:, C_U + 128:C_U + 256] = (idx[:, None] < idx[None, :]) & same
    c[:, C_SEL:C_SEL + 128] = (idx[:, None] < 64)
    c[:, C_SEL + 128:C_SEL + 256] = (idx[:, None] >= 64)
    mp = np.where(idx[:, None] >= idx[None, :], 0.0, NEG)
    mn = np.where(idx[:, None] <= idx[None, :], 0.0, NEG)
    c[:, C_MP:C_MP + 256] = np.concatenate([mp, mp], axis=1)
    c[:, C_MN:C_MN + 256] = np.concatenate([mn, mn], axis=1)
    for p in range(128):
        cg = (p % 64) % 16
        for mm in range(4):
            m = 4 * cg + mm
            i = m if p < 64 else 63 - m
            c[p, C_XI + mm * 64 + i] = 1.0
    return c


def na_tables(rpb_h):
    W, R = 64, 64
    out = np.full((128, 21, 128), NEG, np.float32)
    kp = np.arange(128)
    qp = np.arange(128)

    def table(kt, pr):
        rk = 2 * kt + kp // 64
        ck = kp % 64
        rq = 2 * pr + qp // 64
        cq = qp % 64
        rs = np.clip(rq - 4, 0, R - 8)
        cs = np.clip(cq - 8, 0, W - 16)
        valid = (rk[:, None] >= rs[None, :]) & (rk[:, None] < rs[None, :] + 8) & \
                (ck[:, None] >= cs[None, :]) & (ck[:, None] < cs[None, :] + 16)
        dr = np.clip(rk[:, None] - rq[None, :] + 7, 0, 14)
        dc = np.clip(ck[:, None] - cq[None, :], -15, 15) + 15
        return np.where(valid, rpb_h[dr, dc], np.float32(NEG)).astype(np.float32)

    for i in range(5):
        out[:, i, :] = table(i, 2)
    for a, pr in enumerate((0, 1)):
        for i in range(4):
            out[:, 5 + a * 4 + i, :] = table(i, pr)
    for a, pr in enumerate((30, 31)):
        for i in range(4):
            out[:, 13 + a * 4 + i, :] = table(28 + i, pr)
    return out


def na_chunks(pr):
    if pr <= 1:
        return [(i, 5 + pr * 4 + i) for i in range(4)]
    if pr >= 30:
        return [(28 + i, 13 + (pr - 30) * 4 + i) for i in range(4)]
    return [(pr - 2 + i, i) for i in range(5)]


def build_k2(do_attn=True, do_gdn=True, gdn_heads=(0, 1, 2, 3), dbg=None):
    P = Prog()
    nc = P.nc
    fm = nc.dram_tensor("fm", [19 * 128, TT], F32, kind="ExternalInput").ap()
    tm = nc.dram_tensor("tm", [TT, 912], F32, kind="ExternalInput").ap()
    gains = nc.dram_tensor("gains", [128, 4], F32, kind="ExternalInput").ap()
    convw = nc.dram_tensor("convw", [128, 12, 5], F32, kind="ExternalInput").ap()
    rowp = nc.dram_tensor("rowp", [1, RP_END], F32, kind="ExternalInput").ap()
    cosT = nc.dram_tensor("cosT", [128, L], F32, kind="ExternalInput").ap()
    sinT = nc.dram_tensor("sinT", [128, L], F32, kind="ExternalInput").ap()
    constd = nc.dram_tensor("consts", [128, C_END], F32, kind="ExternalInput").ap()
    nabd = nc.dram_tensor("nab", [128, 2, 21, 128], F32, kind="ExternalInput").ap()
    o_tm = nc.dram_tensor("o_tm", [TT, 1024], F32, kind="ExternalOutput").ap()

    C = nc.alloc_sbuf_tensor("C", [128, C_XI], F32)
    identb = nc.alloc_sbuf_tensor("identb", [128, 128], BF16)
    rp = nc.alloc_sbuf_tensor("rp", [128, RP_END], F32)
    gn = nc.alloc_sbuf_tensor("gn", [128, 8], F32)
    eps_t = nc.alloc_sbuf_tensor("eps_t", [128, 2], F32)
    ps = [nc.alloc_psum_tensor("ps%d" % i, [128, 512], F32) for i in range(8)]
    P.dma('sp', C[:], constd[:, 0:C_XI], writes=['C'])
    P.dma('sp', rp[:], rowp.partition_broadcast(128), writes=['rp'])
    P.dma('sp', gn[:, 0:4], gains, writes=['gn'])
    P.memset('dve', eps_t[:, 0:1], EPS, writes=['eps_t'])
    P.memset('dve', eps_t[:, 1:2], float(np.log(HD ** -0.5)), writes=['eps_t'])
    P.copy('dve', identb[:], C[:, C_ID:C_ID + 128], reads=['C'], writes=['identb'])
    P.ts('dve', gn[:, 4:5], gn[:, 0:1], HD ** -0.5, None, ALU.mult, None, reads=['gn'], writes=['gn'])
    P.ts('dve', gn[:, 5:6], gn[:, 2:3], HD ** -0.5, None, ALU.mult, None, reads=['gn'], writes=['gn'])
    ident = C[:, C_ID:C_ID + 128]
    ones = C[:, C_ONES:C_ONES + 128]

    io = K2IO(fm=fm, tm=tm, o_tm=o_tm)
    if do_attn:
        emit_attention(P, io, cosT, sinT, nabd, C, rp, gn, eps_t, ps, ident, ones, identb)
        P.barrier()
    if do_gdn:
        emit_gdn(P, io, convw, C, rp, eps_t, ps, ident, ones, gdn_heads, dbg, xinit=constd[:, C_XI:C_END])
    return P


class K2IO:
    def __init__(self, fm=None, tm=None, o_tm=None, PT=None, PM=None, O=None, g=0, sfx=''):
        self.fm, self.tm, self.o_tm, self.PT, self.PM, self.O, self.g, self.sfx = fm, tm, o_tm, PT, PM, O, g, sfx

    def fm_rows(self, tile):
        if self.fm is not None:
            return self.fm[tile * 128:(tile + 1) * 128, :]
        g = self.g
        if tile < 2:
            r = 256 * g + 128 * tile
        elif tile == 2:
            r = 512 + 128 * g
        elif tile < 5:
            r = 3840 + 256 * g + 128 * (tile - 3)
        elif tile < 7:
            r = 4352 + 256 * g + 128 * (tile - 5)
        elif tile < 11:
            r = 768 + 512 * g + 128 * (tile - 7)
        elif tile < 15:
            r = 1792 + 512 * g + 128 * (tile - 11)
        else:
            r = 2816 + 512 * g + 128 * (tile - 15)
        return self.PT[r:r + 128, :]

    def tm_cols(self, what, i=0):
        g = self.g
        if self.tm is not None:
            v = self.tm.rearrange("(t p) c -> p t c", p=128)
            if what == 'sv':
                return v[:, :, 0:128]
            if what == 'nv':
                return v[:, :, 128 + i * 128:256 + i * 128]
            if what == 'z':
                return v[:, :, 384 + i * 128:512 + i * 128]
            return v[:, :, 896 + 4 * i:900 + 4 * i]
        v = self.PM.rearrange("(t p) c -> p t c", p=128)
        if what == 'sv':
            return v[:, :, 128 * g:128 * g + 128]
        if what == 'nv':
            c = 1312 + 256 * g + 128 * i
            return v[:, :, c:c + 128]
        if what == 'z':
            c = 256 + 512 * g + 128 * i
            return v[:, :, c:c + 128]
        c = 1280 + 4 * g + i
        return v[:, :, c:c + 32:8]

    def out_rows(self, t, what):
        if self.o_tm is not None:
            c = 0 if what == 'swa' else 768
            return self.o_tm[t * 128:(t + 1) * 128, c:c + 256]
        c = 256 * self.g if what == 'swa' else 1536 + 256 * self.g
        return self.O[t * 128:(t + 1) * 128, c:c + 256]

    def out_gdn(self, hh):
        if self.o_tm is not None:
            return self.o_tm.rearrange("(t p) c -> p t c", p=128)[:, :, 256 + hh * 128:384 + hh * 128]
        c = 512 + 512 * self.g + 128 * hh
        return self.O.rearrange("(t p) c -> p t c", p=128)[:, :, c:c + 128]


def emit_attention(P, io, cosT, sinT, nabd, C, rp, gn, eps_t, ps, ident, ones, identb):
    nc = P.nc
    with ExitStack() as stack:
        def T(name, shape, dt):
            return stack.enter_context(nc.sbuf_tensor(name + io.sfx, shape, dt))
        qa = T("qa", [128, NTT, 2, 128], BF16)
        ka = T("ka", [128, NTT, 128], BF16)
        qn = T("qn", [128, 2, NTT, 128], BF16)
        kn = T("kn", [128, 2, NTT, 128], BF16)
        va = T("va", [128, NTT, 132], BF16)
        vn = T("vn", [128, NTT, 2, 132], BF16)
        nab = T("nab_sb", [128, 2, 21, 128], BF16)
        mkb = T("mkb", [128, 512], BF16)
        bulk = [T("bulk%d" % i, [128, TT], F32) for i in range(2)]
        sq = T("sq", [128, TT], F32)
        rs = T("rs", [128, TT], F32)
        w0b = [T("w0b%d" % i, [128, 512], F32) for i in range(2)]
        w3b = [T("w3b%d" % i, [128, 512], F32) for i in range(2)]
        csb = [T("csb%d" % i, [128, 512], F32) for i in range(2)]
        snb = [T("snb%d" % i, [128, 512], F32) for i in range(2)]
        pt0 = T("pt0", [128, 256], BF16)
        pt1 = T("pt1", [128, 256], BF16)
        pt2 = T("pt2", [128, 256], BF16)
        ost0 = T("ost0", [128, 256], F32)
        ost1 = T("ost1", [128, 256], F32)
        den = T("den", [128, 8], F32)
        es = T("es", [128, 2], F32)
        pts = [pt0, pt1, pt2]
        osts = [ost0, ost1]
        P.dma("pool", nab[:], nabd, writes=["nab"])
        P.copy('dve', mkb[:], C[:, C_MP:C_MP + 512], reads=['C'], writes=['mkb'])
        P.act(es[:], rp[:, RP_SINK:RP_SINK + 2], AF.Exp, reads=['rp'], writes=['es'])
        P.dma('pool', va[:, :, 0:128], io.tm_cols('sv'), reads=['PM'], writes=['va'])
        for h in range(2):
            P.dma('pool', vn[:, :, h, 0:128], io.tm_cols('nv', h), reads=['PM'], writes=['vn'])
        P.memset('dve', va[:, :, 128:129], 1.0, writes=['va'])
        P.memset('dve', vn[:, :, :, 128:129], 1.0, writes=['vn'])

        nb = 0

        def qknorm(row_tile, dst_fn, gcol, rope, dkey):
            nonlocal nb
            raw = bulk[nb % 2]
            rk = 'bulk%d' % (nb % 2)
            nb += 1
            P.dma('sp', raw[:], io.fm_rows(row_tile), reads=['PT'], writes=[rk])
            P.act(sq[:], raw[:], AF.Square, reads=[rk], writes=['sq'])
            for bi, (t0, tn) in enumerate(BLK):
                pb = 6 + bi % 2
                P.mm(ps[pb][:, 0:tn], ones, sq[:, t0:t0 + tn], True, True, reads=['C', 'sq'], writes=['ps%d' % pb])
                P.act(rs[:, t0:t0 + tn], ps[pb][:, 0:tn], AF.Ln, reads=['ps%d' % pb, 'eps_t'], writes=['rs'],
                      bias=eps_t[:, 0:1], scale=1.0 / HD)
            P.act(rs[:], rs[:], AF.Exp, reads=['rs'], writes=['rs'], scale=-0.5)
            v3 = lambda ap: ap.rearrange("p (a b) -> p a b", b=128)
            if not rope:
                P.stt(dst_fn(0, NTT), v3(raw[:]), gcol, v3(rs[:]), ALU.mult, ALU.mult, reads=[rk, 'gn', 'rs'], writes=[dkey])
                return
            P.stt(dst_fn(0, 2), v3(raw[:, 0:LC]), gcol, v3(rs[:, 0:LC]), ALU.mult, ALU.mult, reads=[rk, 'gn', 'rs'], writes=[dkey])
            P.stt(sq[:, LC:TT], raw[:, LC:TT], gcol, rs[:, LC:TT], ALU.mult, ALU.mult, reads=[rk, 'gn', 'rs'], writes=['sq'])
            for bi, (t0, tn) in enumerate(BLK[1:]):
                pb = 6 + bi % 2
                k = bi % 2
                P.mm(ps[pb][:, 0:tn], C[:, C_RM:C_RM + 128], sq[:, t0:t0 + tn], True, True, reads=['C', 'sq'], writes=['ps%d' % pb])
                P.dma('sp', csb[k][:, 0:tn], cosT[:, t0 - LC:t0 - LC + tn], writes=['cs%d' % k])
                P.dma('sp', snb[k][:, 0:tn], sinT[:, t0 - LC:t0 - LC + tn], writes=['sn%d' % k])
                P.tt('pool', w3b[k][:, 0:tn], sq[:, t0:t0 + tn], csb[k][:, 0:tn], ALU.mult, reads=['sq', 'cs%d' % k], writes=['w3_%d' % k])
                P.tt('dve', w0b[k][:, 0:tn], ps[pb][:, 0:tn], snb[k][:, 0:tn], ALU.mult, reads=['ps%d' % pb, 'sn%d' % k], writes=['w0_%d' % k])
                P.tt('dve', dst_fn(t0 // 128, tn // 128), v3(w3b[k][:, 0:tn]), v3(w0b[k][:, 0:tn]), ALU.add,
                     reads=['w3_%d' % k, 'w0_%d' % k], writes=[dkey])

        qknorm(0, lambda a, n: qa[:, a:a + n, 0, :], gn[:, 4:5], True, 'qa')
        qknorm(1, lambda a, n: qa[:, a:a + n, 1, :], gn[:, 4:5], True, 'qa')
        qknorm(2, lambda a, n: ka[:, a:a + n, :], gn[:, 1:2], True, 'ka')
        for h in range(2):
            qknorm(3 + h, lambda a, n, h=h: qn[:, h, a:a + n, :], gn[:, 5:6], False, 'qn')
            qknorm(5 + h, lambda a, n, h=h: kn[:, h, a:a + n, :], gn[:, 3:4], False, 'kn')

        cnt = {'s': 0, 'o': 0, 'st': 0}

        def attn(q_rhs, qkey, nq, chunks, sink_cols, out_aps, out_key):
            pos = []
            for i in range(nq):
                b = 2 + (cnt['o'] % 4)
                cnt['o'] += 1
                pos.append((ps[b], 'ps%d' % b))
            nch = len(chunks)
            for ci, (kT, kkey, v_aps, vkey, bias, bkey) in enumerate(chunks):
                sb = cnt['s'] % 2
                cnt['s'] += 1
                pss, pssk = ps[sb], 'ps%d' % sb
                P.mm(pss[:, 0:nq * 128], kT, q_rhs, True, bias is None, reads=[kkey, qkey], writes=[pssk])
                if bias is not None:
                    P.mm(pss[:, 0:nq * 128], identb[:], bias, False, True, reads=['identb', bkey], writes=[pssk])
                pt = pts[cnt['st'] % 3]
                ptk = 'pt%d' % (cnt['st'] % 3)
                cnt['st'] += 1
                P.act(pt[:, 0:nq * 128], pss[:, 0:nq * 128], AF.Exp, reads=[pssk], writes=[ptk])
                for i in range(nq):
                    P.mm(pos[i][0][:, 0:129], pt[:, i * 128:(i + 1) * 128], v_aps[i], ci == 0, ci == nch - 1,
                         reads=[ptk, vkey], writes=[pos[i][1]])
            for i in range(nq):
                po, pok = pos[i]
                if sink_cols is not None:
                    P.tt('dve', den[:, i:i + 1], po[:, 128:129], sink_cols[i], ALU.add, reads=[pok, 'es'], writes=['den%d' % i])
                    P.op('dve', lambda g, i=i: g.reciprocal(den[:, 4 + i:5 + i], den[:, i:i + 1]), reads=['den%d' % i], writes=['den%d' % i])
                else:
                    P.op('dve', lambda g, i=i, po=po: g.reciprocal(den[:, 4 + i:5 + i], po[:, 128:129]), reads=[pok], writes=['den%d' % i])
                P.op('act', lambda g, i=i, po=po: g.activation(out_aps[i], po[:, 0:128], AF.Copy, scale=den[:, 4 + i:5 + i]),
                     reads=[pok, 'den%d' % i], writes=[out_key])

        ost_n = 0
        for t in range(NTT):
            ost = osts[ost_n % 2]
            ostk = 'ost%d' % (ost_n % 2)
            ost_n += 1
            chunks = []
            if t >= 2:
                n = t - 2
                if n >= 1:
                    chunks.append((ka[:, t - 1, :], 'ka', [va[:, t - 1, 0:129]] * 2, 'va', mkb[:, 0:256], 'mkb'))
                chunks.append((ka[:, t, :], 'ka', [va[:, t, 0:129]] * 2, 'va', None, None))
                if n <= 30:
                    chunks.append((ka[:, t + 1, :], 'ka', [va[:, t + 1, 0:129]] * 2, 'va', mkb[:, 256:512], 'mkb'))
            for ct in range(2):
                chunks.append((ka[:, ct, :], 'ka', [va[:, ct, 0:129]] * 2, 'va', None, None))
            attn(qa[:, t, :, :], 'qa', 2, chunks, [es[:, 0:1], es[:, 1:2]], [ost[:, 0:128], ost[:, 128:256]], ostk)
            P.dma('sp', io.out_rows(t, 'swa'), ost[:], reads=[ostk], writes=['O'])
        for t in range(NTT):
            ost = osts[ost_n % 2]
            ostk = 'ost%d' % (ost_n % 2)
            ost_n += 1
            for h in range(2):
                chunks = []
                if t >= 2:
                    for (kt, ti) in na_chunks(t - 2):
                        chunks.append((kn[:, h, 2 + kt, :], 'kn', [vn[:, 2 + kt, h, 0:129]], 'vn', nab[:, h, ti, :], 'nab'))
                for ct in range(2):
                    chunks.append((kn[:, h, ct, :], 'kn', [vn[:, ct, h, 0:129]], 'vn', None, None))
                attn(qn[:, h, t, :], 'qn', 1, chunks, None, [ost[:, h * 128:(h + 1) * 128]], ostk)
            P.dma('sp', io.out_rows(t, 'na'), ost[:], reads=[ostk], writes=['O'])


def emit_gdn(P, io, convw, C, rp, eps_t, ps, ident, ones, gdn_heads, dbg=None, xinit=None):
    nc = P.nc
    NCH = 2 * NTT
    sfx = io.sfx
    Ld = nc.dram_tensor("Ld" + sfx, [2 * NCH, 64, 64], F32, kind="Internal").ap()
    Td = nc.dram_tensor("Td" + sfx, [2 * NCH, 64, 64], F32, kind="Internal").ap()
    Ldf = Ld.rearrange("q i j -> q (i j)")
    Tdf = Td.rearrange("q i j -> q (i j)")
    TriD = C[:, C_TRI:C_TRI + 256].rearrange("p (d i) -> p d i", d=2)
    UD = C[:, C_U:C_U + 256].rearrange("p (d i) -> p d i", d=2)
    SEL = C[:, C_SEL:C_SEL + 256].rearrange("p (c q) -> p c q", c=2)
    with ExitStack() as stack:
        def T(name, shape, dt=F32):
            return stack.enter_context(nc.sbuf_tensor(name + sfx, shape, dt))
        S = [T("S%d" % i, [128, NTT, 128]) for i in range(6)]
        SL = T("SL", [128, NTT, 2, 128])
        SI = T("SI", [128, NTT, 2, 128])
        cw = T("cw", [128, 12, 5])
        gw = T("gw", [128, 1024])
        w0 = gw[:, 0:512]
        w1 = gw[:, 512:1024]
        abt = T("abt", [128, NTT, 4])
        la = T("la", [128, 2, NTT])
        beta = T("beta", [128, 2, NTT])
        beg = T("beg", [128, 2, NTT])
        nA = T("nA", [128, 2])
        gx = T("gx", [128, 272])
        A_all = [T("A_all%d" % i, [128, 2, 128]) for i in range(2)]
        dec = [T("dec%d" % i, [128, 512]) for i in range(2)]
        t1 = [T("t1_%d" % i, [128, 2, 128]) for i in range(2)]
        t2 = [T("t2_%d" % i, [128, 2, 128]) for i in range(2)]
        tmpo = gw
        vb = [T("vb%d" % d, [128, 128]) for d in range(2)]
        kbg = [T("kbg%d" % d, [128, 128]) for d in range(2)]
        kdec = [[T("kdec%d_%d" % (d, i), [128, 128]) for i in range(2)] for d in range(2)]
        u_sb = [[T("u_sb%d_%d" % (d, i), [128, 128]) for i in range(2)] for d in range(2)]
        wT_sb = [[T("wT_sb%d_%d" % (d, i), [128, 128]) for i in range(2)] for d in range(2)]
        vnew = [T("vnew%d" % d, [128, 128]) for d in range(2)]
        Sst = [T("Sst%d" % d, [128, 128]) for d in range(2)]
        tmpq = [T("tmpq%d" % d, [128, 128]) for d in range(2)]
        o1 = [T("o1_%d" % d, [128, 128]) for d in range(2)]
        ssq = T("ssq", [128, 2, NTT])

        def SK(s, p):
            return 'S%d:%d' % (s, p)

        def SKall(s):
            return [SK(s, p) for p in range(NTT)]

        def flat(t):
            return t[:].rearrange("p t c -> p (t c)")

        P.dma('sp', cw[:], convw, writes=['cw'])
        for hh in gdn_heads:
            rawslot = (0, 4, 5)
            for ti in range(3):
                P.dma('sp', flat(S[rawslot[ti]]), io.fm_rows(7 + 4 * ti + hh), reads=['PT'], writes=SKall(rawslot[ti]))
            for ti in range(3):
                Xf = flat(S[rawslot[ti]])
                Yf = flat(S[1 + ti])
                ci = ti * 4 + hh
                P.ts('dve', Yf, Xf, cw[:, ci, 2:3], None, ALU.mult, None, reads=SKall(rawslot[ti]) + ['cw'], writes=SKall(1 + ti))
                for tap in (0, 1, 3, 4):
                    off = tap - 2
                    for (s0, s1) in ((0, LC), (LC, TT)):
                        lo = s0 + max(0, -off)
                        hi = s1 - max(0, off)
                        P.stt(Yf[:, lo:hi], Xf[:, lo + off:hi + off], cw[:, ci, tap:tap + 1], Yf[:, lo:hi], ALU.mult, ALU.add,
                              reads=SKall(rawslot[ti]) + ['cw'] + SKall(1 + ti), writes=SKall(1 + ti))
                P.act(Yf, Yf, AF.Silu, reads=SKall(1 + ti), writes=SKall(1 + ti))
            SLf = SL[:].rearrange("p t d c -> p (t d c)")[:, 0:TT]
            SIf = SI[:].rearrange("p t d c -> p (t d c)")[:, 0:TT]
            SLk = ['SL:%d' % p for p in range(NTT)]
            SIk = ['SI:%d' % p for p in range(NTT)]
            for ti in range(2):
                Yf = flat(S[1 + ti])
                P.act(SLf, Yf, AF.Square, reads=SKall(1 + ti), writes=SLk)
                for bi, (t0, tn) in enumerate(BLK):
                    pb = 6 + bi % 2
                    P.mm(ps[pb][:, 0:tn], ones, SLf[:, t0:t0 + tn], True, True, reads=['C'] + SLk, writes=['ps%d' % pb])
                    P.act(SIf[:, t0:t0 + tn], ps[pb][:, 0:tn], AF.Ln, reads=['ps%d' % pb, 'eps_t'], writes=SIk, bias=eps_t[:, 0:1], scale=1.0)
                if ti == 0:
                    P.act(SIf, SIf, AF.Exp, reads=SIk + ['eps_t'], writes=SIk, scale=-0.5, bias=eps_t[:, 1:2])
                else:
                    P.act(SIf, SIf, AF.Exp, reads=SIk, writes=SIk, scale=-0.5)
                P.tt('dve', Yf, Yf, SIf, ALU.mult, reads=SKall(1 + ti) + SIk, writes=SKall(1 + ti))
            P.marks.append(('A', dict(P.cnt)))
            if dbg == 'A':
                return
            ntr = 0
            for (src, dst) in ((2, 4), (3, 5)):
                for t0 in range(0, NTT, 4):
                    n = min(4, NTT - t0)
                    pb = 5 + (ntr % 2)
                    ntr += 1
                    for a in range(n):
                        P.tr(ps[pb][:, a * 128:(a + 1) * 128], S[src][:, t0 + a, :], ident, reads=[SK(src, t0 + a), 'C'], writes=['ps%d' % pb])
                    P.copy('act' if ntr % 2 == 0 else 'dve', S[dst][:, t0:t0 + n, :], ps[pb][:, 0:n * 128].rearrange("p (a b) -> p a b", b=128),
                           reads=['ps%d' % pb], writes=[SK(dst, t0 + a) for a in range(n)])
            P.marks.append(('B', dict(P.cnt)))
            if dbg == 'B':
                return
            if io.tm is not None:
                P.dma('sp', abt[:], io.tm_cols('ab', hh), reads=['PM'], writes=['abt'])
            else:
                with nc.allow_non_contiguous_dma(reason='tiny gate columns'):
                    for kk in range(4):
                        cc = 1280 + 8 * kk + 4 * io.g + hh
                        P.dma('sp', abt[:, :, kk:kk + 1], io.PM.rearrange("(t p) c -> p t c", p=128)[:, :, cc:cc + 1],
                              reads=['PM'], writes=['abt'])
            for d in range(2):
                P.ts('dve', la[:, d, :], abt[:, :, d], rp[:, RP_DTB + d * 4 + hh:RP_DTB + d * 4 + hh + 1], None, ALU.add, None,
                     reads=['abt', 'rp'], writes=['la'])
                P.act(nA[:, d:d + 1], rp[:, RP_ALOG + d * 4 + hh:RP_ALOG + d * 4 + hh + 1], AF.Exp, reads=['rp'], writes=['nA'])
                P.act(beta[:, d, :], abt[:, :, 2 + d], AF.Exp, reads=['abt'], writes=['beta'], scale=-1.0)
            P.act(la[:], la[:], AF.Exp, reads=['la'], writes=['la'])
            P.act(la[:], la[:], AF.Ln, reads=['la'], writes=['la'], bias=1.0, scale=1.0)
            for d in range(2):
                P.ts('dve', la[:, d, :], la[:, d, :], nA[:, d:d + 1], -1.0, ALU.mult, ALU.mult, reads=['la', 'nA'], writes=['la'])
            P.ts('dve', beta[:], beta[:], 1.0, None, ALU.add, None, reads=['beta'], writes=['beta'])
            P.op('dve', lambda g: g.reciprocal(beta[:], beta[:]), reads=['beta'], writes=['beta'])
            for d in range(2):
                P.mm(ps[7][:, d * 34:(d + 1) * 34], TriD[:, d, :], la[:, d, :], True, True, reads=['C', 'la'], writes=['ps7'])
                P.mm(ps[7][:, 68 + d * 34:68 + (d + 1) * 34], UD[:, d, :], la[:, d, :], True, True, reads=['C', 'la'], writes=['ps7'])
                for c in range(2):
                    o_ = 136 + (d * 2 + c) * 34
                    P.mm(ps[7][:, o_:o_ + 34], SEL[:, c, :], la[:, d, :], True, True, reads=['C', 'la'], writes=['ps7'])
            P.act(gx[:], ps[7][:, 0:272], AF.Exp, reads=['ps7'], writes=['gx'])
            P.tt('dve', beg[:], beta[:], gx[:, 0:68].rearrange("p (d t) -> p d t", d=2), ALU.mult, reads=['beta', 'gx'], writes=['beg'])

            def eg(d, p, r0=0, rn=128):
                return gx[r0:r0 + rn, d * 34 + p:d * 34 + p + 1]

            def ekd(d, p):
                return gx[:, 68 + d * 34 + p:68 + d * 34 + p + 1]

            def egl(d, c, p):
                o_ = 136 + (d * 2 + c) * 34 + p
                return gx[:, o_:o_ + 1]

            P.marks.append(('C', dict(P.cnt)))
            if dbg == 'C':
                return
            for p in range(NTT):
                b = p % 2
                pg, pgk = ps[0 + b], 'ps%d' % b
                pd, pdk = ps[2 + b], 'ps%d' % (2 + b)
                P.mm(pg[:, 0:128], S[2][:, p, :], S[2][:, p, :], True, True, reads=[SK(2, p)], writes=[pgk])
                P.mm(pg[:, 128:256], S[2][:, p, :], S[1][:, p, :], True, True, reads=[SK(2, p), SK(1, p)], writes=[pgk])
                P.tt('pool', A_all[b][:], la[:, :, p:p + 1].broadcast_to([128, 2, 128]), TriD, ALU.mult, reads=['la', 'C'], writes=['A_all%d' % b])
                for d in range(2):
                    P.mm(pd[:, d * 128:(d + 1) * 128], UD[:, d, :], A_all[b][:, d, :], True, True, reads=['C', 'A_all%d' % b], writes=[pdk])
                    P.mm(pd[:, 256 + d * 128:256 + (d + 1) * 128], A_all[b][:, d, :], UD[:, d, :], True, True, reads=['C', 'A_all%d' % b], writes=[pdk])
                P.act(dec[b][:], pd[:], AF.Exp, reads=[pdk], writes=['dec%d' % b])
                P.tt('dve', t1[b][:], pg[:, 128:256].unsqueeze(1).broadcast_to([128, 2, 128]), TriD, ALU.mult, reads=[pgk, 'C'], writes=['t1_%d' % b])
                P.tt('dve', SI[:, p, :, :], t1[b][:], dec[b][:, 0:256].rearrange("p (d i) -> p d i", d=2), ALU.mult,
                     reads=['t1_%d' % b, 'dec%d' % b], writes=['SI:%d' % p])
                P.tt('dve', t2[b][:], pg[:, 0:128].unsqueeze(1).broadcast_to([128, 2, 128]), UD, ALU.mult, reads=[pgk, 'C'], writes=['t2_%d' % b])
                for d in range(2):
                    P.stt(SL[:, p, d, :], t2[b][:, d, :], beta[:, d, p:p + 1], dec[b][:, 256 + d * 128:384 + d * 128], ALU.mult, ALU.mult,
                          reads=['t2_%d' % b, 'beta', 'dec%d' % b], writes=['SL:%d' % p])
            P.marks.append(('D', dict(P.cnt)))
            if dbg == 'D':
                return
            SLall = ['SL:%d' % p for p in range(NTT)]
            Ldv = Ld.rearrange("(d p c) i j -> d c i p j", d=2, c=2)
            Tdv = Td.rearrange("(d p c) j i -> d c j p i", d=2, c=2)
            for d in range(2):
                for c in range(2):
                    P.dma('sp', Ldv[d, c], SL[c * 64:(c + 1) * 64, :, d, c * 64:(c + 1) * 64], reads=SLall, writes=['Ld'])
            Lp = flat(S[0])[:, 0:4096].rearrange("p (i j) -> p i j", j=64)
            Xp = flat(S[3])[:, 0:4096].rearrange("p (m i) -> p m i", i=64)
            Xpf = flat(S[3])
            S0f = flat(S[0])
            S2f = flat(S[2])
            for (c0, ncz) in ((0, 64),):
                P.dma('sp', S0f[0:ncz, 0:4096], Ldf[c0:c0 + ncz, :], reads=['Ld'], writes=SKall(0))
                P.dma('sp', S2f[64:64 + ncz, 0:4096], Ldf[NCH + c0:NCH + c0 + ncz, :], reads=['Ld'], writes=SKall(2))
                P.copy('pool', S0f[64:64 + ncz, 0:4096], S2f[64:64 + ncz, 0:4096][:, ::-1], reads=SKall(2), writes=SKall(0))
                P.memset('dve', Xpf[:, 0:4096], 0.0, writes=SKall(3))
                P.memset('dve', Xpf[:, 0:4095].rearrange("p (m s) -> p m s", s=65)[:, :, 0], 1.0, writes=SKall(3))
                P.memset('dve', Xpf[:, 4095:4096], 1.0, writes=SKall(3))
                for j in range(63):
                    n_i = 63 - j
                    n_m = j + 1
                    a_bc = Lp[:, j + 1:64, j].unsqueeze(1).broadcast_to([128, n_m, n_i])
                    b_bc = Xp[:, 0:n_m, j].unsqueeze(2).broadcast_to([128, n_m, n_i])
                    tv = tmpo[:, 0:n_m * n_i].rearrange("p (m i) -> p m i", i=n_i)
                    blk = Xp[:, 0:n_m, j + 1:64]
                    P.tt('dve', tv, a_bc, b_bc, ALU.mult, reads=['S0:0', 'S3:0'], writes=['gw0', 'gw1'])
                    P.tt('dve', blk, blk, tv, ALU.subtract, reads=['gw0', 'gw1', 'S3:0'], writes=['S3:0'])
                P.dma('sp', Tdf[c0:c0 + ncz, :], Xpf[0:ncz, 0:4096], reads=SKall(3), writes=['Td'])
                P.copy('pool', S2f[64:64 + ncz, 0:4096], Xpf[64:64 + ncz, 0:4096][:, ::-1], reads=SKall(3), writes=SKall(2))
                P.dma('sp', Tdf[NCH + c0:NCH + c0 + ncz, :], S2f[64:64 + ncz, 0:4096], reads=SKall(2), writes=['Td'])
            c0, ncz = 64, NCH - 64
            X2 = Xpf[:, 0:256].rearrange("p (m i) -> p m i", i=64)
            for q in range(ncz):
                P.dma('sp', S0f[q * 16:(q + 1) * 16, 0:4096], Ldf[c0 + q:c0 + q + 1, :].partition_broadcast(16), reads=['Ld'], writes=SKall(0))
                P.dma('sp', S2f[64 + q * 16:64 + (q + 1) * 16, 0:4096], Ldf[NCH + c0 + q:NCH + c0 + q + 1, :].partition_broadcast(16),
                      reads=['Ld'], writes=SKall(2))
            P.copy('pool', S0f[64:128, 0:4096], S2f[64:128, 0:4096][:, ::-1], reads=SKall(2), writes=SKall(0))
            P.dma('sp', Xpf[:, 0:256], xinit, writes=SKall(3))
            for j in range(63):
                n_i = 63 - j
                a_bc = Lp[:, j + 1:64, j].unsqueeze(1).broadcast_to([128, 4, n_i])
                b_bc = X2[:, :, j].unsqueeze(2).broadcast_to([128, 4, n_i])
                tv = tmpo[:, 0:4 * n_i].rearrange("p (m i) -> p m i", i=n_i)
                blk = X2[:, :, j + 1:64]
                P.tt('dve', tv, a_bc, b_bc, ALU.mult, reads=['S0:0', 'S3:0'], writes=['gw0', 'gw1'])
                P.tt('dve', blk, blk, tv, ALU.subtract, reads=['gw0', 'gw1', 'S3:0'], writes=['S3:0'])
            P.dma('sp', Tdf[c0:c0 + ncz, :].rearrange("q (a b) -> (q a) b", a=16), Xpf[0:64, 0:256], reads=SKall(3), writes=['Td'])
            P.copy('pool', S2f[64:128, 0:256].rearrange("p (m i) -> p m i", i=64), X2[64:128, :, ::-1], reads=SKall(3), writes=SKall(2))
            P.dma('sp', Tdf[NCH + c0:NCH + c0 + ncz, :].rearrange("q (a b) -> (q a) b", a=16), S2f[64:128, 0:256], reads=SKall(2), writes=['Td'])
            P.marks.append(('E', dict(P.cnt)))
            if dbg == 'E':
                return
            for d in range(2):
                for c in range(2):
                    P.dma('sp', SL[c * 64:(c + 1) * 64, :, d, c * 64:(c + 1) * 64], Tdv[d, c], reads=['Td'], writes=SLall)
            P.marks.append(('F', dict(P.cnt)))
            if dbg == 'F':
                return
            P.dma('sp', S[2][:], io.tm_cols('z', hh), reads=['PM'], writes=SKall(2))
            P.memset('pool', flat(S[3]), 0.0, writes=SKall(3))
            for d in range(2):
                P.memset('pool', Sst[d][:], 0.0, writes=['Sst%d' % d])
                P.memset('pool', vnew[d][:], 0.0, writes=['vnew%d' % d])
            order = [list(range(0, 4)) + list(range(4, NCH)), list(range(3, -1, -1)) + list(range(NCH - 1, 3, -1))]

            def prep(d, p, sl):
                ppd, ppk = ps[0 + d], 'ps%d' % d
                P.ts('pool', vb[d][:], S[5][:, p, :], beta[:, d, p:p + 1], None, ALU.mult, None, reads=[SK(5, p), 'beta'], writes=['vb%d' % d])
                P.ts('pool', kbg[d][:], S[4][:, p, :], beg[:, d, p:p + 1], None, ALU.mult, None, reads=[SK(4, p), 'beg'], writes=['kbg%d' % d])
                P.ts('pool', kdec[d][sl][:], S[4][:, p, :], ekd(d, p), None, ALU.mult, None, reads=[SK(4, p), 'gx'], writes=['kdec%d_%d' % (d, sl)])
                P.mm(ppd[:, 0:128], SL[:, p, d, :], vb[d][:], True, True, reads=['SL:%d' % p, 'vb%d' % d], writes=[ppk])
                P.mm(ppd[:, 128:256], kbg[d][:], SL[:, p, d, :], True, True, reads=['SL:%d' % p, 'kbg%d' % d], writes=[ppk])
                P.copy('act', u_sb[d][sl][:], ppd[:, 0:128], reads=[ppk], writes=['u_sb%d_%d' % (d, sl)])
                P.copy('act', wT_sb[d][sl][:], ppd[:, 128:256], reads=[ppk], writes=['wT_sb%d_%d' % (d, sl)])

            for d in range(2):
                prep(d, order[d][0] // 2, 0)
            for s in range(NCH):
                sl = (s // 2) % 2
                info = []
                for d in range(2):
                    c = order[d][s]
                    info.append((c // 2, c % 2, (c % 2) * 64))
                for d in range(2):
                    p, hb, r0 = info[d]
                    psa, pak = ps[2 + d], 'ps%d' % (2 + d)
                    psb, pbk = ps[4 + 2 * d + (s % 2)], 'ps%d' % (4 + 2 * d + (s % 2))
                    P.mm(psa[:, 0:128], wT_sb[d][sl][:], Sst[d][:], True, True, reads=['wT_sb%d_%d' % (d, sl), 'Sst%d' % d], writes=[pak])
                    P.mm(psb[:, 0:128], S[1][:, p, :], Sst[d][:], True, True, reads=[SK(1, p), 'Sst%d' % d], writes=[pbk])
                for d in range(2):
                    p, hb, r0 = info[d]
                    psa, pak = ps[2 + d], 'ps%d' % (2 + d)
                    P.tt('dve', vnew[d][r0:r0 + 64, :], u_sb[d][sl][r0:r0 + 64, :], psa[r0:r0 + 64, 0:128], ALU.subtract,
                         reads=['u_sb%d_%d' % (d, sl), pak], writes=['vnew%d' % d])
                for d in range(2):
                    p, hb, r0 = info[d]
                    psa, pak = ps[2 + d], 'ps%d' % (2 + d)
                    psb, pbk = ps[4 + 2 * d + (s % 2)], 'ps%d' % (4 + 2 * d + (s % 2))
                    P.mm(psa[:, 128:256], kdec[d][sl][r0:r0 + 64, :], vnew[d][r0:r0 + 64, :], True, True,
                         reads=['kdec%d_%d' % (d, sl), 'vnew%d' % d], writes=[pak])
                    P.mm(psb[:, 128:256], SI[:, p, d, :], vnew[d][:], True, True, reads=['SI:%d' % p, 'vnew%d' % d], writes=[pbk])
                for d in range(2):
                    p, hb, r0 = info[d]
                    psa, pak = ps[2 + d], 'ps%d' % (2 + d)
                    P.stt(Sst[d][:], Sst[d][:], egl(d, hb, p), psa[:, 128:256], ALU.mult, ALU.add,
                          reads=['Sst%d' % d, 'gx', pak], writes=['Sst%d' % d])
                if s % 2 == 0 and s + 2 < NCH:
                    for d in range(2):
                        prep(d, order[d][s + 2] // 2, 1 - sl)
                for d in range(2):
                    p, hb, r0 = info[d]
                    psb, pbk = ps[4 + 2 * d + (s % 2)], 'ps%d' % (4 + 2 * d + (s % 2))
                    P.op('act', lambda g, d=d, p=p, r0=r0, psb=psb: g.activation(tmpq[d][r0:r0 + 64, :], psb[r0:r0 + 64, 0:128], AF.Copy,
                                                                                   scale=eg(d, p, r0, 64)),
                         reads=[pbk, 'gx'], writes=['tmpq%d' % d])
                    P.tt('dve', o1[d][r0:r0 + 64, :], tmpq[d][r0:r0 + 64, :], psb[r0:r0 + 64, 128:256], ALU.add,
                         reads=['tmpq%d' % d, pbk], writes=['o1_%d' % d])
                    P.tt('pool', S[3][r0:r0 + 64, p, :], S[3][r0:r0 + 64, p, :], o1[d][r0:r0 + 64, :], ALU.add,
                         reads=['o1_%d' % d, SK(3, p)], writes=[SK(3, p)])
            P.marks.append(('H', dict(P.cnt)))
            if dbg == 'H':
                return
            P.act(flat(S[2]), flat(S[2]), AF.Silu, reads=SKall(2), writes=SKall(2))
            for p in range(NTT):
                P.act(w0[:, 0:128], S[3][:, p, :], AF.Square, reads=[SK(3, p)], writes=['gw0', 'ssq'], accum_out=ssq[:, 0, p:p + 1])
            P.act(ssq[:, 1, :], ssq[:, 0, :], AF.Ln, reads=['ssq', 'eps_t'], writes=['ssq'], bias=eps_t[:, 0:1], scale=1.0 / HD)
            P.act(ssq[:, 1, :], ssq[:, 1, :], AF.Exp, reads=['ssq'], writes=['ssq'], scale=-0.5)
            for p in range(NTT):
                P.stt(S[4][:, p, :], S[3][:, p, :], ssq[:, 1, p:p + 1], rp[:, RP_OUTG:RP_OUTG + 128], ALU.mult, ALU.mult,
                      reads=[SK(3, p), 'ssq', 'rp'], writes=[SK(4, p)])
                P.tt('pool', S[4][:, p, :], S[4][:, p, :], S[2][:, p, :], ALU.mult, reads=[SK(4, p), SK(2, p)], writes=[SK(4, p)])
            P.dma('sp', io.out_gdn(hh), S[4][:], reads=SKall(4), writes=['O'])


def rope_tables_T():
    t = np.arange(L)
    row = (t // 64).astype(np.float32)
    col = (t % 64).astype(np.float32)
    nf = HD // 4
    inv = (np.float32(10000.0) ** (-np.arange(nf, dtype=np.float32) / np.float32(nf))).astype(np.float32)
    ar = row[:, None] * inv[None, :]
    ac = col[:, None] * inv[None, :]
    ang = np.concatenate([ar, ar, ac, ac], axis=-1).astype(np.float32)
    return np.ascontiguousarray(np.cos(ang).astype(np.float32).T), np.ascontiguousarray(np.sin(ang).astype(np.float32).T)


def k2_inputs(proj_b, projc_b, g, W, consts, cosT, sinT):
    full = np.concatenate([projc_b, proj_b], axis=0)
    cols = []
    cols += list(range(256 * g, 256 * g + 256))
    cols += list(range(512 + 128 * g, 512 + 128 * g + 128))
    cols += list(range(5152 + 256 * g, 5152 + 256 * g + 256))
    cols += list(range(5664 + 256 * g, 5664 + 256 * g + 256))
    for i in range(3):
        cols += list(range(1024 + 1024 * i + 512 * g, 1024 + 1024 * i + 512 * g + 512))
    fm = np.ascontiguousarray(full[:, cols].T)
    tcols = list(range(768 + 128 * g, 768 + 128 * g + 128)) + list(range(6176 + 256 * g, 6176 + 256 * g + 256)) \
        + list(range(4096 + 512 * g, 4096 + 512 * g + 512))
    for hh in range(4):
        h = 4 * g + hh
        tcols += [5120 + h, 5120 + 8 + h, 5120 + 16 + h, 5120 + 24 + h]
    tm = np.ascontiguousarray(full[:, tcols])
    gains = np.ascontiguousarray(np.stack([W['swa_q_g'], W['swa_k_g'], W['na_q_g'], W['na_k_g']], axis=1).astype(np.float32))
    cw = W['dn_conv_w']
    convw = np.zeros((128, 12, 5), np.float32)
    for i in range(3):
        for hh in range(4):
            c0 = i * 1024 + (4 * g + hh) * 128
            convw[:, i * 4 + hh, :] = cw[:, c0:c0 + 128].T
    rowp = np.zeros((1, RP_END), np.float32)
    rowp[0, RP_OUTG:RP_OUTG + 128] = W['dn_out_g']
    rowp[0, RP_ALOG:RP_ALOG + 8] = W['dn_A_log'][:, 4 * g:4 * g + 4].reshape(-1)
    rowp[0, RP_DTB:RP_DTB + 8] = W['dn_dt_bias'][:, 4 * g:4 * g + 4].reshape(-1)
    rowp[0, RP_SINK:RP_SINK + 2] = W['swa_sink'][2 * g:2 * g + 2]
    nab = np.ascontiguousarray(np.stack([na_tables(W['na_rpb'][2 * g + h]) for h in range(2)], axis=1))
    return {"fm": fm, "tm": tm, "gains": gains, "convw": convw, "rowp": rowp,
            "cosT": cosT, "sinT": sinT, "consts": consts, "nab": nab}


def run_k2(proj, projc, W):
    P = build_k2()
    consts = k2_consts()
    cosT, sinT = rope_tables_T()
    in_maps = [k2_inputs(proj[i // 2], projc[i // 2], i % 2, W, consts, cosT, sinT) for i in range(NCORES)]
    res = run_prog(P, in_maps)
    o_all = np.empty((B, L, D), np.float32)
    oc_all = np.empty((B, LC, D), np.float32)
    for i in range(NCORES):
        b, g = i // 2, i % 2
        o = res[i]["o_tm"]
        for (src, dst, n) in ((0, 256 * g, 256), (256, 512 + 512 * g, 512), (768, 1536 + 256 * g, 256)):
            oc_all[b, :, dst:dst + n] = o[:LC, src:src + n]
            o_all[b, :, dst:dst + n] = o[LC:, src:src + n]
    return o_all, oc_all


def kernel_unfused(x, c, ctx, c_ctx, w_ada, b_ada, norm1_g, norm2_g, w_in, swa_q_g, swa_k_g, swa_sink,
           dn_conv_w, dn_A_log, dn_dt_bias, dn_out_g, na_q_g, na_k_g, na_rpb, w_out, w_gate, w_up, w_down):
    f = lambda a: np.ascontiguousarray(np.asarray(a, dtype=np.float32))
    x, c, ctx, c_ctx = f(x), f(c), f(ctx), f(c_ctx)
    mod = run_k0(c, c_ctx, f(w_ada), f(b_ada))
    xa, xc = x, ctx
    for l in range(2):
        W = {'swa_q_g': f(swa_q_g[l]), 'swa_k_g': f(swa_k_g[l]), 'swa_sink': f(swa_sink[l]), 'dn_conv_w': f(dn_conv_w[l]),
             'dn_A_log': f(dn_A_log[l]), 'dn_dt_bias': f(dn_dt_bias[l]), 'dn_out_g': f(dn_out_g[l]),
             'na_q_g': f(na_q_g[l]), 'na_k_g': f(na_k_g[l]), 'na_rpb': f(na_rpb[l])}
        proj, projc = run_k1(xa, xc, mod[l], f(norm1_g[l]), f(w_in[l]))
        o_all, oc_all = run_k2(proj, projc, W)
        xa, xc = run_k3(xa, xc, o_all, oc_all, mod[l], f(norm2_g[l]), f(w_out[l]), f(w_gate[l]), f(w_up[l]), f(w_down[l]))
    return xa


def _coll(P, kind, in_ap, out_ap, groups, reads=(), writes=()):
    q = 'pool'
    reads = [P.key(k) for k in reads]
    writes = [P.key(k) for k in writes]
    eng = P.E[q]
    k = P.ndma[q]
    r = k % P.NS
    base = 16 * (k // P.NS)
    need = P._deps(q, reads, writes)
    sk_ring = ('ring', q, r)
    if k >= P.NS and P.seen[q].get(sk_ring, 0) < base:
        need[sk_ring] = max(need.get(sk_ring, 0), base)
    for sk, val in need.items():
        eng.wait_ge(P.semobj[sk], val)
        P.seen[q][sk] = val
    ins = eng.collective_compute(kind, ALU.bypass, replica_groups=groups, ins=[in_ap], outs=[out_ap])
    ins.then_inc(P.ring[q][r], 16)
    P.ndma[q] += 1
    P._record((sk_ring, base + 16, 'dma'), reads, writes)
    return ins


Prog.coll = _coll
PAIRS = [[0, 1], [2, 3], [4, 5], [6, 7]]


def urow(hf, t):
    return 128 * hf if t == 16 else LC + 2048 * hf + 128 * t


def emit_k0(P, cT2, w_ada, b_ada, modd, ps):
    nc = P.nc
    with ExitStack() as stack:
        def T(name, shape, dt=F32):
            return stack.enter_context(nc.sbuf_tensor(name, shape, dt))
        cs = T("k0cs", [128, 16, 2])
        w = [T("k0w%d" % i, [128, 16, 512]) for i in range(2)]
        bb = T("k0bb", [2, 6 * D])
        res = T("k0res", [2, 6 * D])
        with nc.allow_non_contiguous_dma(reason="tiny transposed load"):
            P.dma('sp', cs[:], cT2.rearrange("(k p) r -> p k r", p=128), writes=['k0cs'])
        P.act(cs[:], cs[:], AF.Silu, reads=['k0cs'], writes=['k0cs'])
        n = 0
        for l in range(2):
            P.dma('sp', bb[:], b_ada[l:l + 1, :].partition_broadcast(2), writes=['k0bb'])
            wv = w_ada[l].rearrange("(k p) n -> p k n", p=128)
            for s in range(6 * D // 512):
                wt = w[n % 2]
                wk = 'k0w%d' % (n % 2)
                pt, pk = ps[n % 2], 'ps%d' % (n % 2)
                n += 1
                for kh in range(2):
                    P.dma('sp', wt[:, kh * 8:(kh + 1) * 8, :], wv[:, kh * 8:(kh + 1) * 8, s * 512:(s + 1) * 512], writes=[wk])
                for k in range(16):
                    P.mm(pt[0:2, :], cs[:, k, :], wt[:, k, :], k == 0, k == 15, reads=['k0cs', wk], writes=[pk])
                P.tt('dve', res[:, s * 512:(s + 1) * 512], pt[0:2, :], bb[:, s * 512:(s + 1) * 512], ALU.add,
                     reads=[pk, 'k0bb'], writes=['k0res'])
            P.dma('sp', modd[l], res[:], reads=['k0res'], writes=['modd'])
    P.barrier()


def emit_k1(P, xsrc, xkey, modd, l, n1g, w_in_l, PT, PM, hf, ps, ones, eps_t, tag):
    nc = P.nc
    with ExitStack() as stack:
        def T(name, shape, dt=F32):
            return stack.enter_context(nc.sbuf_tensor(name + tag, shape, dt))
        hT = T("hT", [128, 16, RT], BF16)
        V1 = T("V1", [128, 5, 16])
        A1 = T("A1", [128, 2, 16])
        xc = [T("k1xc%d" % i, [128, RT]) for i in range(2)]
        tb = [T("k1tb%d" % i, [128, RT]) for i in range(2)]
        rstd = T("k1rstd", [128, RT])
        slab = [T("k1slab%d" % i, [128, 16, 512], BF16) for i in range(2)]
        stf = [T("k1stf%d" % i, [128, RT]) for i in range(2)]
        stt_ = [T("k1stt%d" % i, [128, 512]) for i in range(2)]
        with nc.allow_non_contiguous_dma(reason="small per-feature vectors, feature-major"):
            P.dma('sp', V1[:, 0:2, :], modd[l, 0, 0:2 * D].rearrange("(r c p) -> p r c", r=2, p=128), reads=['modd'], writes=['V1'])
            P.dma('sp', V1[:, 2:4, :], modd[l, 1, 0:2 * D].rearrange("(r c p) -> p r c", r=2, p=128), reads=['modd'], writes=['V1'])
            P.dma('sp', V1[:, 4, :], n1g.rearrange("(c p) -> p c", p=128), writes=['V1'])
        P.stt(A1[:, 0, :], V1[:, 1, :], 1.0, V1[:, 4, :], ALU.add, ALU.mult, reads=['V1'], writes=['A1'])
        P.stt(A1[:, 1, :], V1[:, 3, :], 1.0, V1[:, 4, :], ALU.add, ALU.mult, reads=['V1'], writes=['A1'])
        nld = 0
        for c in range(16):
            xt, xk = xc[nld % 2], 'k1xc%d' % (nld % 2)
            tt_, tk = tb[nld % 2], 'k1tb%d' % (nld % 2)
            nld += 1
            P.dma('sp', xt[:], xsrc[c * 128:(c + 1) * 128, :], reads=[xkey], writes=[xk])
            P.act(tt_[:], xt[:], AF.Square, reads=[xk], writes=[tk])
            for gi, (t0, tn) in enumerate(TOKG):
                P.mm(ps[gi][:, 0:tn], ones, tt_[:, t0:t0 + tn], c == 0, c == 15, reads=['C', tk], writes=['ps%d' % gi])
        for gi, (t0, tn) in enumerate(TOKG):
            P.act(rstd[:, t0:t0 + tn], ps[gi][:, 0:tn], AF.Ln, reads=['ps%d' % gi, 'eps_t'], writes=['k1rstd'],
                  bias=eps_t[:, 0:1], scale=1.0 / D)
        P.act(rstd[:], rstd[:], AF.Exp, reads=['k1rstd'], writes=['k1rstd'], scale=-0.5)
        for c in range(16):
            xt, xk = xc[nld % 2], 'k1xc%d' % (nld % 2)
            tt_, tk = tb[nld % 2], 'k1tb%d' % (nld % 2)
            nld += 1
            P.dma('sp', xt[:], xsrc[c * 128:(c + 1) * 128, :], reads=[xkey], writes=[xk])
            P.tt('dve', tt_[:], xt[:], rstd[:], ALU.mult, reads=[xk, 'k1rstd'], writes=[tk])
            P.ts('pool', hT[:, c, 0:2048], tt_[:, 0:2048], A1[:, 0, c:c + 1], V1[:, 0, c:c + 1], ALU.mult, ALU.add,
                 reads=[tk, 'A1', 'V1'], writes=['hT'])
            P.ts('pool', hT[:, c, 2048:RT], tt_[:, 2048:RT], A1[:, 1, c:c + 1], V1[:, 2, c:c + 1], ALU.mult, ALU.add,
                 reads=[tk, 'A1', 'V1'], writes=['hT'])
        w_v = w_in_l.rearrange("(k p) n -> p k n", p=128)
        fm_off = 0
        tm_off = 0
        ev = 0
        nstf = 0
        nstt = 0
        psm = ps[4:8]
        for si, (c0, ncol, kind) in enumerate(SLABS):
            sl = slab[si % 2]
            sk = 'k1slab%d' % (si % 2)
            for kh in range(2):
                P.dma('pool', sl[:, kh * 8:(kh + 1) * 8, 0:ncol], w_v[:, kh * 8:(kh + 1) * 8, c0:c0 + ncol], writes=[sk])
            if kind == 'F':
                for ct in range(ncol // 128):
                    st = stf[nstf % 2]
                    stk = 'k1stf%d' % (nstf % 2)
                    nstf += 1
                    for (t0, tn) in TOKG:
                        pq = psm[ev % 4]
                        pk = 'ps%d' % (4 + ev % 4)
                        for k in range(16):
                            P.mm(pq[:, 0:tn], sl[:, k, ct * 128:(ct + 1) * 128], hT[:, k, t0:t0 + tn], k == 0, k == 15,
                                 reads=[sk, 'hT'], writes=[pk])
                        P.copy('act' if ev % 2 == 0 else 'dve', st[:, t0:t0 + tn], pq[:, 0:tn], reads=[pk], writes=[stk])
                        ev += 1
                    P.dma('sp', PT[fm_off:fm_off + 128, LC + 2048 * hf:LC + 2048 * hf + 2048], st[:, 0:2048], reads=[stk], writes=['PT'])
                    P.dma('sp', PT[fm_off:fm_off + 128, 128 * hf:128 * hf + 128], st[:, 2048:RT], reads=[stk], writes=['PT'])
                    fm_off += 128
            else:
                for t in range(NT):
                    st = stt_[nstt % 2]
                    stk = 'k1stt%d' % (nstt % 2)
                    nstt += 1
                    pq = psm[ev % 4]
                    pk = 'ps%d' % (4 + ev % 4)
                    for k in range(16):
                        P.mm(pq[:, 0:ncol], hT[:, k, t * 128:(t + 1) * 128], sl[:, k, 0:ncol], k == 0, k == 15,
                             reads=[sk, 'hT'], writes=[pk])
                    P.copy('act' if ev % 2 == 0 else 'dve', st[:, 0:ncol], pq[:, 0:ncol], reads=[pk], writes=[stk])
                    ev += 1
                    r = urow(hf, t)
                    P.dma('sp', PM[r:r + 128, tm_off:tm_off + ncol], st[:, 0:ncol], reads=[stk], writes=['PM'])
                tm_off += ncol
    P.barrier()


def emit_k3(P, xsrc, xkey, O, modd, l, n2g, w_out, w_gate, w_up, w_down, xdst, xdkey, xmT, hf, ps, ones, eps_t, identb, tag):
    nc = P.nc
    with ExitStack() as stack0:
        V = stack0.enter_context(nc.sbuf_tensor("V" + tag, [128, 9, 16], F32))
        A2 = stack0.enter_context(nc.sbuf_tensor("A2" + tag, [128, 2, 16], F32))
        with nc.allow_non_contiguous_dma(reason="small per-feature vectors, feature-major"):
            P.dma('sp', V[:, 0:4, :], modd[l, 0, 2 * D:6 * D].rearrange("(r c p) -> p r c", r=4, p=128), reads=['modd'], writes=['V'])
            P.dma('sp', V[:, 4:8, :], modd[l, 1, 2 * D:6 * D].rearrange("(r c p) -> p r c", r=4, p=128), reads=['modd'], writes=['V'])
            P.dma('sp', V[:, 8, :], n2g.rearrange("(c p) -> p c", p=128), writes=['V'])
        P.stt(A2[:, 0, :], V[:, 2, :], 1.0, V[:, 8, :], ALU.add, ALU.mult, reads=['V'], writes=['A2'])
        P.stt(A2[:, 1, :], V[:, 6, :], 1.0, V[:, 8, :], ALU.add, ALU.mult, reads=['V'], writes=['A2'])
        with ExitStack() as stack:
            def T(name, shape, dt=F32):
                return stack.enter_context(nc.sbuf_tensor(name + tag, shape, dt))
            oTb = T("oTb", [128, 16, RT], BF16)
            wo = [T("wo%d" % i, [128, 16, 128], BF16) for i in range(2)]
            xa = [T("xa%d" % i, [128, RT]) for i in range(2)]
            xb = [T("xb%d" % i, [128, RT]) for i in range(2)]
            ot = [T("ot%d" % i, [128, D]) for i in range(2)]
            ob = T("ob", [128, D], BF16)
            pstb = [ps[6][:].bitcast(BF16), ps[7][:].bitcast(BF16)]
            for t in range(NT):
                o_t, otk = ot[t % 2], 'ot%d' % (t % 2)
                r = urow(hf, t)
                P.dma('sp', o_t[:], O[r:r + 128, :], reads=['O'], writes=[otk])
                P.copy('pool', ob[:], o_t[:], reads=[otk], writes=['ob'])
                for half in range(2):
                    pt, pk = pstb[half], 'ps%d' % (6 + half)
                    for j in range(8):
                        k = half * 8 + j
                        P.tr(pt[:, j * 128:(j + 1) * 128], ob[:, k * 128:(k + 1) * 128], identb[:], reads=['ob', 'identb'], writes=[pk])
                    P.copy('act' if half == 0 else 'dve', oTb[:, half * 8:(half + 1) * 8, t * 128:(t + 1) * 128],
                           pt[:].rearrange("p (j n) -> p j n", j=8), reads=[pk], writes=['oTb'])
            wo_v = w_out.rearrange("(k p) n -> p k n", p=128)
            ev = 0
            for c in range(16):
                w, wk = wo[c % 2], 'wo%d' % (c % 2)
                P.dma('pool', w[:], wo_v[:, :, c * 128:(c + 1) * 128], writes=[wk])
                xin, xik = xa[c % 2], 'xa%d' % (c % 2)
                xout, xok = xb[c % 2], 'xb%d' % (c % 2)
                P.dma('sp', xin[:], xsrc[c * 128:(c + 1) * 128, :], reads=[xkey], writes=[xik])
                for (t0, tn) in TOKG:
                    pt, pk = ps[ev % 2], 'ps%d' % (ev % 2)
                    ev += 1
                    for k in range(16):
                        P.mm(pt[:, 0:tn], w[:, k, :], oTb[:, k, t0:t0 + tn], k == 0, k == 15, reads=[wk, 'oTb'], writes=[pk])
                    gcol = V[:, 0, c:c + 1] if t0 < 2048 else V[:, 4, c:c + 1]
                    P.stt(xout[:, t0:t0 + tn], pt[:, 0:tn], gcol, xin[:, t0:t0 + tn], ALU.mult, ALU.add,
                          reads=[pk, 'V', xik], writes=[xok])
                P.dma('sp', xmT[c * 128:(c + 1) * 128, :], xout[:], reads=[xok], writes=['xmT%d' % c])
        P.barrier()
        SGMAX = 1152
        with ExitStack() as stack:
            def T(name, shape, dt=F32):
                return stack.enter_context(nc.sbuf_tensor(name + tag, shape, dt))
            h2T = T("h2T", [128, 16, SGMAX], BF16)
            aT = T("aT", [128, NH, SGMAX], BF16)
            rstd = T("rstd", [128, SGMAX])
            xc = [T("xc%d" % i, [128, SGMAX]) for i in range(2)]
            tb = [T("t%d" % i, [128, SGMAX]) for i in range(2)]
            wg = [T("wg%d" % i, [128, 16, 128], BF16) for i in range(2)]
            wu = [T("wu%d" % i, [128, 16, 128], BF16) for i in range(2)]
            wd = [T("wd%d" % i, [128, NH, 128], BF16) for i in range(2)]
            sl = [T("sl%d" % i, [128, 512]) for i in range(2)]
            wg_v = w_gate.rearrange("(k p) n -> p k n", p=128)
            wu_v = w_up.rearrange("(k p) n -> p k n", p=128)
            wd_v = w_down.rearrange("(j p) n -> p j n", p=128)
            nld = 0
            for sg in SGS:
                s0 = sg[0][0]
                sn = sum(tn for (_, tn) in sg)
                nlat = sum(tn for (t0, tn) in sg if t0 < 2048)
                for c in range(16):
                    xt, xk = xc[nld % 2], 'xc%d' % (nld % 2)
                    tt_, tk = tb[nld % 2], 't%d' % (nld % 2)
                    nld += 1
                    P.dma('sp', xt[:, 0:sn], xmT[c * 128:(c + 1) * 128, s0:s0 + sn], reads=['xmT%d' % c], writes=[xk])
                    P.act(tt_[:, 0:sn], xt[:, 0:sn], AF.Square, reads=[xk], writes=[tk])
                    for gi, (t0, tn) in enumerate(sg):
                        P.mm(ps[2 + gi][:, 0:tn], ones, tt_[:, t0 - s0:t0 - s0 + tn], c == 0, c == 15,
                             reads=['C', tk], writes=['ps%d' % (2 + gi)])
                for gi, (t0, tn) in enumerate(sg):
                    P.act(rstd[:, t0 - s0:t0 - s0 + tn], ps[2 + gi][:, 0:tn], AF.Ln, reads=['ps%d' % (2 + gi), 'eps_t'],
                          writes=['rstd'], bias=eps_t[:, 0:1], scale=1.0 / D)
                P.act(rstd[:, 0:sn], rstd[:, 0:sn], AF.Exp, reads=['rstd'], writes=['rstd'], scale=-0.5)
                for c in range(16):
                    xt, xk = xc[nld % 2], 'xc%d' % (nld % 2)
                    tt_, tk = tb[nld % 2], 't%d' % (nld % 2)
                    nld += 1
                    P.dma('sp', xt[:, 0:sn], xmT[c * 128:(c + 1) * 128, s0:s0 + sn], reads=['xmT%d' % c], writes=[xk])
                    P.tt('dve', tt_[:, 0:sn], xt[:, 0:sn], rstd[:, 0:sn], ALU.mult, reads=[xk, 'rstd'], writes=[tk])
                    P.ts('pool', h2T[:, c, 0:nlat], tt_[:, 0:nlat], A2[:, 0, c:c + 1], V[:, 1, c:c + 1], ALU.mult, ALU.add,
                         reads=[tk, 'A2', 'V'], writes=['h2T'])
                    if nlat < sn:
                        P.ts('pool', h2T[:, c, nlat:sn], tt_[:, nlat:sn], A2[:, 1, c:c + 1], V[:, 5, c:c + 1], ALU.mult, ALU.add,
                             reads=[tk, 'A2', 'V'], writes=['h2T'])
                ev = 0
                for j in range(NH):
                    g_, gk = wg[j % 2], 'wg%d' % (j % 2)
                    u_, uk = wu[j % 2], 'wu%d' % (j % 2)
                    P.dma('pool', g_[:], wg_v[:, :, j * 128:(j + 1) * 128], writes=[gk])
                    P.dma('pool', u_[:], wu_v[:, :, j * 128:(j + 1) * 128], writes=[uk])
                    for (t0, tn) in sg:
                        pg, pgk = ps[ev % 2], 'ps%d' % (ev % 2)
                        pu, puk = ps[5 + ev % 2], 'ps%d' % (5 + ev % 2)
                        s_, sk = sl[ev % 2], 'sl%d' % (ev % 2)
                        ev += 1
                        for k in range(16):
                            P.mm(pg[:, 0:tn], g_[:, k, :], h2T[:, k, t0 - s0:t0 - s0 + tn], k == 0, k == 15, reads=[gk, 'h2T'], writes=[pgk])
                        for k in range(16):
                            P.mm(pu[:, 0:tn], u_[:, k, :], h2T[:, k, t0 - s0:t0 - s0 + tn], k == 0, k == 15, reads=[uk, 'h2T'], writes=[puk])
                        P.act(s_[:, 0:tn], pg[:, 0:tn], AF.Silu, reads=[pgk], writes=[sk])
                        P.tt('dve', aT[:, j, t0 - s0:t0 - s0 + tn], pu[:, 0:tn], s_[:, 0:tn], ALU.mult, reads=[puk, sk], writes=['aT'])
                ev = 0
                for c in range(16):
                    w, wk = wd[c % 2], 'wd%d' % (c % 2)
                    for jh in range(4):
                        P.dma('pool', w[:, jh * 11:(jh + 1) * 11, :], wd_v[:, jh * 11:(jh + 1) * 11, c * 128:(c + 1) * 128], writes=[wk])
                    xt, xk = xc[nld % 2], 'xc%d' % (nld % 2)
                    tt_, tk = tb[nld % 2], 't%d' % (nld % 2)
                    nld += 1
                    P.dma('sp', xt[:, 0:sn], xmT[c * 128:(c + 1) * 128, s0:s0 + sn], reads=['xmT%d' % c], writes=[xk])
                    for (t0, tn) in sg:
                        pt, pk = ps[3 + ev % 2], 'ps%d' % (3 + ev % 2)
                        ev += 1
                        for j in range(NH):
                            P.mm(pt[:, 0:tn], w[:, j, :], aT[:, j, t0 - s0:t0 - s0 + tn], j == 0, j == NH - 1, reads=[wk, 'aT'], writes=[pk])
                        gcol = V[:, 3, c:c + 1] if t0 < 2048 else V[:, 7, c:c + 1]
                        P.stt(tt_[:, t0 - s0:t0 - s0 + tn], pt[:, 0:tn], gcol, xt[:, t0 - s0:t0 - s0 + tn], ALU.mult, ALU.add,
                              reads=[pk, 'V', xk], writes=[tk])
                    P.dma('sp', xdst[c * 128:(c + 1) * 128, s0:s0 + sn], tt_[:, 0:sn], reads=[tk], writes=[xdkey])
    P.barrier()


def build_fused(layers=(0, 1), do_k2=True):
    P = Prog()
    nc = P.nc
    dt_in = lambda name, shape: nc.dram_tensor(name, shape, F32, kind="ExternalInput").ap()
    xT = dt_in("xT", [2, D, RT])
    cT2 = dt_in("cT2", [D, 2])
    w_ada = dt_in("w_ada", [2, D, 6 * D])
    b_ada = dt_in("b_ada", [2, 6 * D])
    n1g = dt_in("norm1_g", [2, D])
    n2g = dt_in("norm2_g", [2, D])
    w_in = dt_in("w_in", [2, D, NIN])
    w_out = dt_in("w_out", [2, D, D])
    w_gate = dt_in("w_gate", [2, D, FFN])
    w_up = dt_in("w_up", [2, D, FFN])
    w_down = dt_in("w_down", [2, FFN, D])
    gains = dt_in("gains", [2, 128, 4])
    convw = dt_in("convw", [2, 2, 128, 12, 5])
    rowp = dt_in("rowp", [2, 2, 1, RP_END])
    nabd = dt_in("nab", [2, 2, 128, 2, 21, 128])
    cosT = dt_in("cosT", [128, L])
    sinT = dt_in("sinT", [128, L])
    constd = dt_in("consts", [128, C_END])
    xoT = nc.dram_tensor("xoT", [2, D, RT], F32, kind="ExternalOutput").ap()
    di = lambda name, shape: nc.dram_tensor(name, shape, F32, kind="Internal").ap()
    modd = di("modd", [2, 2, 6 * D])
    PT = di("PT", [NFM, TT])
    PM = di("PM", [TT, NTM])
    O = di("O", [TT, D])
    XT1 = di("XT1", [2, D, RT])
    xmT = di("xmT", [D, RT])

    C = nc.alloc_sbuf_tensor("C", [128, C_XI], F32)
    identb = nc.alloc_sbuf_tensor("identb", [128, 128], BF16)
    eps_t = nc.alloc_sbuf_tensor("eps_t", [128, 2], F32)
    rp = nc.alloc_sbuf_tensor("rp", [128, RP_END], F32)
    gn = nc.alloc_sbuf_tensor("gn", [128, 8], F32)
    ps = [nc.alloc_psum_tensor("ps%d" % i, [128, 512], F32) for i in range(8)]
    P.dma('sp', C[:], constd[:, 0:C_XI], writes=['C'])
    P.memset('dve', eps_t[:, 0:1], EPS, writes=['eps_t'])
    P.memset('dve', eps_t[:, 1:2], float(np.log(HD ** -0.5)), writes=['eps_t'])
    P.copy('dve', identb[:], C[:, C_ID:C_ID + 128], reads=['C'], writes=['identb'])
    ident = C[:, C_ID:C_ID + 128]
    ones = C[:, C_ONES:C_ONES + 128]

    emit_k0(P, cT2, w_ada, b_ada, modd, ps)
    for l in layers:
        for hf in range(2):
            xsrc, xkey = (xT[hf], 'xT') if l == 0 else (XT1[hf], 'XT1_%d' % hf)
            emit_k1(P, xsrc, xkey, modd, l, n1g[l], w_in[l], PT, PM, hf, ps, ones, eps_t, "_k1_%d%d" % (l, hf))
        if do_k2:
            for g in range(2):
                sfx = "_k2_%d%d" % (l, g)
                io = K2IO(PT=PT, PM=PM, O=O, g=g, sfx=sfx)
                P.dma('sp', rp[:], rowp[l, g].partition_broadcast(128), writes=['rp'])
                P.dma('sp', gn[:, 0:4], gains[l], writes=['gn'])
                P.ts('dve', gn[:, 4:5], gn[:, 0:1], HD ** -0.5, None, ALU.mult, None, reads=['gn'], writes=['gn'])
                P.ts('dve', gn[:, 5:6], gn[:, 2:3], HD ** -0.5, None, ALU.mult, None, reads=['gn'], writes=['gn'])
                emit_attention(P, io, cosT, sinT, nabd[l, g], C, rp, gn, eps_t, ps, ident, ones, identb)
                P.barrier()
                emit_gdn(P, io, convw[l, g], C, rp, eps_t, ps, ident, ones, (0, 1, 2, 3), xinit=constd[:, C_XI:C_END])
                P.barrier()
        for hf in range(2):
            xsrc, xkey = (xT[hf], 'xT') if l == 0 else (XT1[hf], 'XT1_%d' % hf)
            last = (l == layers[-1]) and (l == 1)
            xdst, xdkey = (xoT[hf], 'xoT') if (l == 1) else (XT1[hf], 'XT1_%d' % hf)
            emit_k3(P, xsrc, xkey, O, modd, l, n2g[l], w_out[l], w_gate[l], w_up[l], w_down[l], xdst, xdkey, xmT, hf,
                    ps, ones, eps_t, identb, "_k3_%d%d" % (l, hf))
    return P


def fused_inputs(inp, b):
    f = lambda a: np.ascontiguousarray(np.asarray(a, dtype=np.float32))
    x, ctx = inp['x'], inp['ctx']
    xT = np.stack([np.concatenate([x[b, hf * 2048:(hf + 1) * 2048], ctx[b, hf * 128:(hf + 1) * 128]], axis=0).T for hf in range(2)], axis=0)
    cT2 = np.stack([inp['c'][b], inp['c_ctx']], axis=1)
    gains = np.stack([np.stack([inp['swa_q_g'][l], inp['swa_k_g'][l], inp['na_q_g'][l], inp['na_k_g'][l]], axis=1) for l in range(2)], axis=0)
    convw = np.zeros((2, 2, 128, 12, 5), np.float32)
    rowp = np.zeros((2, 2, 1, RP_END), np.float32)
    nab = np.zeros((2, 2, 128, 2, 21, 128), np.float32)
    for l in range(2):
        cw = inp['dn_conv_w'][l]
        for g in range(2):
            for i in range(3):
                for hh in range(4):
                    c0 = i * 1024 + (4 * g + hh) * 128
                    convw[l, g, :, i * 4 + hh, :] = cw[:, c0:c0 + 128].T
            rowp[l, g, 0, RP_OUTG:RP_OUTG + 128] = inp['dn_out_g'][l]
            rowp[l, g, 0, RP_ALOG:RP_ALOG + 8] = inp['dn_A_log'][l][:, 4 * g:4 * g + 4].reshape(-1)
            rowp[l, g, 0, RP_DTB:RP_DTB + 8] = inp['dn_dt_bias'][l][:, 4 * g:4 * g + 4].reshape(-1)
            rowp[l, g, 0, RP_SINK:RP_SINK + 2] = inp['swa_sink'][l][2 * g:2 * g + 2]
            nab[l, g] = np.stack([na_tables(inp['na_rpb'][l][2 * g + h]) for h in range(2)], axis=1)
    cosT, sinT = rope_tables_T()
    return {"xT": f(xT), "cT2": f(cT2), "w_ada": inp['w_ada'], "b_ada": inp['b_ada'], "norm1_g": inp['norm1_g'],
            "norm2_g": inp['norm2_g'], "w_in": inp['w_in'], "w_out": inp['w_out'], "w_gate": inp['w_gate'], "w_up": inp['w_up'],
            "w_down": inp['w_down'], "gains": f(gains), "convw": convw, "rowp": rowp, "nab": nab, "cosT": cosT, "sinT": sinT,
            "consts": k2_consts()}


def kernel_fused(**inp):
    f = lambda a: np.ascontiguousarray(np.asarray(a, dtype=np.float32))
    inp = {k: f(v) for k, v in inp.items()}
    P = build_fused()
    P.finish()
    in_maps = [fused_inputs(inp, b) for b in range(B)]
    res = run_bass_kernel_spmd(P.nc, in_maps, core_ids=list(range(B))).results
    out = np.empty((B, L, D), np.float32)
    for b in range(B):
        for hf in range(2):
            out[b, hf * 2048:(hf + 1) * 2048] = res[b]["xoT"][hf].T[:2048]
    return out


def kernel(**inputs):
    return kernel_unfused(**inputs)
```
